# Optimizing a Trainium2 kernel written in Bass

```python
import jax, jax.numpy as jnp
from jax import lax
import numpy as np

D_MODEL = 1024
BATCH = 4
SEQ = 8192
DEPTH = 2

N_META = 16
PAD = 128
BLOCK_Q = 128
EPS = 1e-6
NEG = -1e30

FOX_HEADS = 8
FOX_DH = 64
FOX_WIDTH = FOX_HEADS * FOX_DH
GLA_HEADS = 4
GLA_DK = 64
GLA_DV = 128
GLA_KW = GLA_HEADS * GLA_DK
GLA_WIDTH = GLA_HEADS * GLA_DV
GLA_RANK = 16
GLA_GATE_NORM = 16.0
GLA_CHUNK = 16
LRU_WIDTH = 2 * GLA_WIDTH
LRU_BLOCKS = 8
LRU_BS = LRU_WIDTH // LRU_BLOCKS
CONV_W = 4
LRU_C = 8.0

D_MIX = FOX_WIDTH + GLA_WIDTH + LRU_WIDTH
COL_SIZES = (FOX_WIDTH, FOX_WIDTH, FOX_WIDTH, FOX_HEADS, FOX_WIDTH,
             GLA_KW, GLA_KW, GLA_WIDTH, GLA_RANK, GLA_WIDTH,
             LRU_WIDTH, LRU_WIDTH)
D_IN = 3 * FOX_WIDTH + FOX_HEADS + FOX_WIDTH + 2 * GLA_KW + GLA_WIDTH + GLA_RANK + GLA_WIDTH + 2 * LRU_WIDTH

kernel_name = "hybrid_fox_gla_rglru_parallel_heads"


def rms_norm(x, g):
    xf = x.astype(jnp.float32)
    y = xf * lax.rsqrt(jnp.mean(xf * xf, axis=-1, keepdims=True) + EPS)
    return (y * g.astype(jnp.float32)).astype(x.dtype)


def split_projection(proj):
    offs = []
    acc = 0
    for s in COL_SIZES[:-1]:
        acc += s
        offs.append(acc)
    return jnp.split(proj, offs, axis=-1)


def fox_attention(q, k, v, log_f, valid):
    B, L, H, dh = q.shape
    nb = L // BLOCK_Q
    scale = dh ** -0.5
    c = jnp.cumsum(log_f, axis=1).transpose(0, 2, 1)
    kf = k.astype(jnp.float32).transpose(0, 2, 1, 3)
    vf = v.astype(jnp.float32).transpose(0, 2, 1, 3)
    qb = q.astype(jnp.float32).reshape(B, nb, BLOCK_Q, H, dh).transpose(1, 0, 3, 2, 4)
    cqb = c.reshape(B, H, nb, BLOCK_Q).transpose(2, 0, 1, 3)
    kpos = jnp.arange(L)

    def one_block(args):
        qi, cqi, i = args
        qpos = i * BLOCK_Q + jnp.arange(BLOCK_Q)
        s = jnp.einsum('bhqd,bhkd->bhqk', qi, kf) * scale + cqi[..., None] - c[:, :, None, :]
        mask = (kpos[None, :] <= qpos[:, None]) & valid[None, :]
        p = jax.nn.softmax(jnp.where(mask, s, NEG), axis=-1)
        return jnp.einsum('bhqk,bhkd->bhqd', p, vf)

    o = lax.map(one_block, (qb, cqb, jnp.arange(nb)))
    return o.transpose(1, 0, 3, 2, 4).reshape(B, L, H * dh)


def gla_chunked(q, k, v, log_a):
    B, L, H, dk = q.shape
    dv = v.shape[-1]
    C = GLA_CHUNK
    n = L // C

    def chunks(t):
        return t.astype(jnp.float32).reshape(B, n, C, H, t.shape[-1]).transpose(0, 3, 1, 2, 4)

    qc = chunks(q) * (dk ** -0.5)
    kc, vc, gc = chunks(k), chunks(v), chunks(log_a)
    b = jnp.cumsum(gc, axis=3)
    b_last = b[:, :, :, -1:, :]
    causal = jnp.tril(jnp.ones((C, C), dtype=bool))[:, :, None]
    diff = b[:, :, :, :, None, :] - b[:, :, :, None, :, :]
    decay = jnp.where(causal, jnp.exp(jnp.where(causal, diff, 0.0)), 0.0)
    A = jnp.einsum('bhntd,bhnsd,bhntsd->bhnts', qc, kc, decay)
    o_intra = jnp.einsum('bhnts,bhnsv->bhntv', A, vc)
    U = jnp.einsum('bhnsd,bhnsv->bhndv', kc * jnp.exp(b_last - b), vc)
    chunk_decay = jnp.exp(b_last[:, :, :, 0, :])

    def step(S, inp):
        u, dcy = inp
        return dcy[..., None] * S + u, S

    _, S_prev = lax.scan(step, jnp.zeros((B, H, dk, dv), jnp.float32),
                         (U.transpose(2, 0, 1, 3, 4), chunk_decay.transpose(2, 0, 1, 3)))
    S_prev = S_prev.transpose(1, 2, 0, 3, 4)
    o_inter = jnp.einsum('bhntd,bhndv->bhntv', qc * jnp.exp(b), S_prev)
    o = o_intra + o_inter
    return o.transpose(0, 2, 3, 1, 4).reshape(B, L, H, dv)


def causal_depthwise_conv(x, w, b):
    W = x.shape[-1]
    y = lax.conv_general_dilated(x, w[:, None, :].astype(x.dtype), window_strides=(1,),
                                 padding=[(CONV_W - 1, 0)],
                                 dimension_numbers=('NWC', 'WIO', 'NWC'),
                                 feature_group_count=W)
    return y + b.astype(x.dtype)


def rg_lru(xc, w_r, b_r, w_i, b_i, lam):
    B, L, W = xc.shape
    xb = xc.reshape(B, L, LRU_BLOCKS, LRU_BS)
    r = jax.nn.sigmoid(jnp.einsum('blhi,hij->blhj', xb, w_r).reshape(B, L, W).astype(jnp.float32) + b_r)
    i = jax.nn.sigmoid(jnp.einsum('blhi,hij->blhj', xb, w_i).reshape(B, L, W).astype(jnp.float32) + b_i)
    log_a = -LRU_C * r * jax.nn.softplus(-lam.astype(jnp.float32))
    a = jnp.exp(log_a)
    u = jnp.sqrt(-jnp.expm1(2.0 * log_a)) * (i * xc.astype(jnp.float32))

    def combine(left, right):
        a1, b1 = left
        a2, b2 = right
        return a1 * a2, a2 * b1 + b2

    _, h = lax.associative_scan(combine, (a, u), axis=1)
    return h


def setup_inputs(seed: int = 0) -> dict:
    key = jax.random.key(seed)
    ks = jax.random.split(key, 20)
    f32 = jnp.float32
    nrm = lambda k, shape, s: jax.random.normal(k, shape, f32) * s
    a0 = jax.random.uniform(ks[14], (DEPTH, LRU_WIDTH), f32, 0.9, 0.999)
    a_root = a0 ** (1.0 / LRU_C)
    return {
        "x": nrm(ks[0], (BATCH, SEQ, D_MODEL), 1.0),
        "meta": nrm(ks[1], (N_META, D_MODEL), 1.0),
        "pre_g": 1.0 + nrm(ks[2], (DEPTH, D_MODEL), 0.02),
        "w_in": nrm(ks[3], (DEPTH, D_MODEL, D_IN), D_MODEL ** -0.5),
        "b_f": jax.random.uniform(ks[4], (DEPTH, FOX_HEADS), f32, 1.0, 4.0),
        "w_a2": nrm(ks[5], (DEPTH, GLA_RANK, GLA_KW), GLA_RANK ** -0.5),
        "b_a": nrm(ks[6], (DEPTH, GLA_KW), 0.1),
        "gla_norm_g": 1.0 + nrm(ks[7], (DEPTH, GLA_WIDTH), 0.02),
        "conv_w": nrm(ks[8], (DEPTH, CONV_W, LRU_WIDTH), CONV_W ** -0.5),
        "conv_b": nrm(ks[9], (DEPTH, LRU_WIDTH), 0.02),
        "w_r": nrm(ks[10], (DEPTH, LRU_BLOCKS, LRU_BS, LRU_BS), LRU_BS ** -0.5),
        "b_r": nrm(ks[11], (DEPTH, LRU_WIDTH), 0.02),
        "w_i": nrm(ks[12], (DEPTH, LRU_BLOCKS, LRU_BS, LRU_BS), LRU_BS ** -0.5),
        "b_i": nrm(ks[13], (DEPTH, LRU_WIDTH), 0.02),
        "lru_lambda": jnp.log(a_root) - jnp.log1p(-a_root),
        "w_out": nrm(ks[15], (DEPTH, D_MIX, D_MODEL), D_MIX ** -0.5),
        "post_g": 1.0 + nrm(ks[16], (DEPTH, D_MODEL), 0.02),
    }


def reference(x, meta, pre_g, w_in, b_f, w_a2, b_a, gla_norm_g, conv_w, conv_b,
              w_r, b_r, w_i, b_i, lru_lambda, w_out, post_g):
    B, S, D = x.shape
    dt = x.dtype
    L = S + PAD
    h = jnp.concatenate([jnp.zeros((B, PAD - N_META, D), dt),
                         jnp.broadcast_to(meta.astype(dt)[None], (B, N_META, D)), x], axis=1)
    valid = jnp.arange(L) >= (PAD - N_META)
    vmask = valid[None, :, None].astype(dt)

    for l in range(DEPTH):
        hn = rms_norm(h, pre_g[l])
        proj = hn @ w_in[l]
        (fq, fk, fv, ff, fg, gq, gk, gv, ga, gg, lx, lg) = split_projection(proj)

        log_f = jax.nn.log_sigmoid(ff.astype(jnp.float32) + b_f[l]) * valid[None, :, None]
        o_fox = fox_attention(fq.reshape(B, L, FOX_HEADS, FOX_DH), fk.reshape(B, L, FOX_HEADS, FOX_DH),
                              fv.reshape(B, L, FOX_HEADS, FOX_DH), log_f, valid)
        y_fox = o_fox.astype(dt) * jax.nn.silu(fg)

        log_a = jax.nn.log_sigmoid((ga @ w_a2[l]).astype(jnp.float32) + b_a[l]) / GLA_GATE_NORM
        o_gla = gla_chunked(gq.reshape(B, L, GLA_HEADS, GLA_DK),
                            (gk * vmask).reshape(B, L, GLA_HEADS, GLA_DK),
                            gv.reshape(B, L, GLA_HEADS, GLA_DV),
                            log_a.reshape(B, L, GLA_HEADS, GLA_DK))
        o_gla = o_gla * lax.rsqrt(jnp.mean(o_gla * o_gla, axis=-1, keepdims=True) + EPS)
        o_gla = o_gla.reshape(B, L, GLA_WIDTH) * gla_norm_g[l].astype(jnp.float32)
        y_gla = o_gla.astype(dt) * jax.nn.silu(gg)

        xc = causal_depthwise_conv(lx, conv_w[l], conv_b[l]) * vmask
        h_lru = rg_lru(xc, w_r[l], b_r[l], w_i[l], b_i[l], lru_lambda[l])
        y_lru = h_lru.astype(dt) * jax.nn.silu(lg)

        y = jnp.concatenate([y_fox, y_gla, y_lru], axis=-1) * vmask
        h = h + rms_norm(y @ w_out[l], post_g[l])

    return h[:, PAD:]
```

```python
from contextlib import ExitStack
import os
import numpy as np
import concourse.bass as bass
import concourse.mybir as mybir
from concourse.bass_utils import run_bass_kernel_spmd

F32 = mybir.dt.float32
BF16 = mybir.dt.bfloat16
AF = mybir.ActivationFunctionType
ALU = mybir.AluOpType

D = 1024
DIN = 5656
DMIX = 2048
NL = 2
EPS = 1e-6
C_FQ, C_FK, C_FV, C_FF, C_FG = 0, 512, 1024, 1536, 1544
C_GQ, C_GK, C_GV, C_GA, C_GG = 2056, 2312, 2568, 3080, 3096
C_LX, C_LG = 3608, 4632
NDS = 24
LV = int(os.environ.get('DBG_P3', '99'))

V_PREG = 0
V_NBA = V_PREG + 16
V_GNG = V_NBA + 4
V_CW = V_GNG + 8
V_CB = V_CW + 64
V_BR = V_CB + 16
V_BI = V_BR + 16
V_LAM = V_BI + 16
NV = V_LAM + 16


class _Rec:
    def __getattr__(self, name):
        def f(*a, **k):
            return (name, a, k)
        return f


_REC = _Rec()


class TR:
    def __init__(self, nc, es):
        self.nc = nc
        self.engs = ['pe', 'act', 'dve', 'pool', 'sp']
        self.q = {e: [] for e in self.engs}
        self.sem = {}
        self.cnt = {}
        for j, e in enumerate(self.engs):
            self.sem[e] = nc.monotonic_semaphore(j).sem()
            self.cnt[e] = 0
        for i in range(NDS):
            n = 'd%d' % i
            self.sem[n] = nc.monotonic_semaphore(len(self.engs) + i).sem()
            self.cnt[n] = 0
        self.dnext = 0
        self.waited = {e: {} for e in self.engs}
        self.W = {}
        self.R = {}
        self.G = {}

    def _deps(self, reads, writes):
        d = {}

        def add(m):
            for k, v in m.items():
                if d.get(k, 0) < v:
                    d[k] = v
        for k in reads:
            add(self.W.get(k, {}))
        for k in writes:
            if self.R.get(k):
                g = dict(self.R[k])
                for kk, vv in self.W.get(k, {}).items():
                    if g.get(kk, 0) < vv:
                        g[kk] = vv
                self.G[k] = g
                self.W[k] = {}
                self.R[k] = {}
            add(self.G.get(k, {}))
        return d

    def _commit(self, reads, writes, ev):
        sem, val = ev
        for k in writes:
            self.W.setdefault(k, {})[sem] = val
        for k in reads:
            self.R.setdefault(k, {})[sem] = val

    def _filter(self, eng, d):
        waits = []
        for sem, val in d.items():
            if sem == eng and eng == 'pe':
                continue
            if self.waited[eng].get(sem, 0) >= val:
                continue
            self.waited[eng][sem] = val
            waits.append((sem, val))
        return waits

    def op(self, eng, fn, reads=(), writes=()):
        fn = fn(_REC)
        d = self._deps(reads, writes)
        waits = self._filter(eng, d)
        self.cnt[eng] += 1
        ev = (eng, self.cnt[eng])
        self.q[eng].append((waits, fn, ev, 1))
        self._commit(reads, writes, ev)

    def dma(self, eng, out, in_, reads=(), writes=()):
        d = self._deps(reads, writes)
        ds = 'd%d' % self.dnext
        self.dnext = (self.dnext + 1) % NDS
        if self.cnt[ds] > 0:
            d[ds] = max(d.get(ds, 0), self.cnt[ds])
        waits = self._filter(eng, d)
        self.cnt[ds] += 16
        ev = (ds, self.cnt[ds])
        self.q[eng].append((waits, ('dma_start', (), dict(out=out, in_=in_)), ev, 16))
        self._commit(reads, writes, ev)

    def barrier(self):
        snap = dict(self.cnt)
        for e in self.engs:
            d = {k: v for k, v in snap.items() if v > 0 and k != e}
            waits = self._filter(e, d)
            self.cnt[e] += 1
            self.q[e].append((waits, ('nop', (), {}), (e, self.cnt[e]), 1))
        snap = dict(self.cnt)
        for e in self.engs:
            d = {k: snap[k] for k in self.engs if k != e}
            waits = self._filter(e, d)
            self.cnt[e] += 1
            self.q[e].append((waits, ('nop', (), {}), (e, self.cnt[e]), 1))
        self.W = {}
        self.R = {}
        self.G = {}

    def replay(self, eng, e):
        for waits, fn, (s, v), inc in self.q[eng]:
            for ws, wv in waits:
                e.wait_ge(self.sem[ws], wv)
            getattr(e, fn[0])(*fn[1], **fn[2]).then_inc(self.sem[s], inc)


class Arena:
    def __init__(self, handle, nwords):
        self.h = handle
        self.n = nwords
        self.base = 0
        self.top = 0

    def persist(self):
        self.base = self.top

    def reset(self):
        self.top = self.base

    def alloc(self, free_elems, dtype, parts=128):
        words = (free_elems * (2 if dtype == BF16 else 4) + 3) // 4
        words = (words + 7) // 8 * 8
        a = self.top
        self.top += words
        assert self.top <= self.n, ("SBUF arena overflow", self.top, self.n)
        v = self.h[:, a:a + words]
        if dtype == BF16:
            v = v.bitcast(BF16)
        return v[0:parts, 0:free_elems]


def build_nc(NT, phases=('P1', 'P2', 'P3', 'P4', 'P5'), layers=(0, 1), debug_out=False):
    L = NT * 128
    nc = bass.Bass("TRN2", target_bir_lowering=False, monotonic_sem_count=NDS + 8)
    dt = lambda n, s, d, k="Internal": nc.dram_tensor(n, s, d, kind=k).ap()
    h0 = dt("h0", [L, D], F32, "ExternalInput")
    w_in = dt("w_in", [NL, D, DIN], F32, "ExternalInput")
    w_out = dt("w_out", [NL, DMIX, D], F32, "ExternalInput")
    vec128 = dt("vec128", [128, NV], F32, "ExternalInput")
    bf_in = dt("bf_in", [8, NL], F32, "ExternalInput")
    wa2_in = dt("wa2_in", [16, NL * 256], F32, "ExternalInput")
    wr_in = dt("wr_in", [128, NL * 8 * 128], F32, "ExternalInput")
    wi_in = dt("wi_in", [128, NL * 8 * 128], F32, "ExternalInput")
    pg_in = dt("pg_in", [NL, 128, D], F32, "ExternalInput")
    cst_in = dt("cst_in", [128, 3 * 128], F32, "ExternalInput")
    out = dt("out", [L - 128, D], F32, "ExternalOutput")
    kind_s = "ExternalOutput" if debug_out else "Internal"
    qa = dt("qa", [8, 68, L], BF16, kind_s)
    ka = dt("ka", [8, 68, L], BF16, kind_s)
    va = dt("va", [8, L, 65], BF16, kind_s)
    sgf = dt("sgf", [512, L], BF16, kind_s)
    gqT = dt("gqT", [256, L], F32, kind_s)
    gkT = dt("gkT", [256, L], F32, kind_s)
    gaT = dt("gaT", [16, L], BF16, kind_s)
    gv = dt("gv", [L, 512], BF16, kind_s)
    sgg = dt("sgg", [512, L], BF16, kind_s)
    lxT = dt("lxT", [1024, L], F32, kind_s)
    slg = dt("slg", [1024, L], BF16, kind_s)
    yT = dt("yT", [DMIX, L], BF16, kind_s)
    h1 = dt("h1", [L, D], F32, kind_s)

    AW = 51 * 1024 + 512
    arena_h = nc.alloc_sbuf_tensor("arena", [128, AW], F32)
    A = Arena(arena_h, AW)
    psF = [nc.alloc_psum_tensor("psf%d" % i, [128, 512], F32)[:, :] for i in range(6)]
    psB = [nc.alloc_psum_tensor("psb%d" % i, [128, 1024], BF16)[:, :] for i in range(2)]

    assert (NT - 1) % 4 == 0
    STS = [[0]] + [list(range(1 + 4 * j, 5 + 4 * j)) for j in range((NT - 1) // 4)]

    with ExitStack() as es:
        T = TR(nc, es)
        op, dma = T.op, T.dma

        vec = A.alloc(NV, F32)
        cst = A.alloc(384, F32)
        ident = A.alloc(128, BF16)
        tri4 = A.alloc(512, BF16)
        vm0 = A.alloc(128, F32)
        ones_bf = A.alloc(512, BF16)
        ones_f = A.alloc(512, F32)
        onesN = A.alloc(128, BF16)
        nbf = A.alloc(NL, F32, 8)
        wa2 = A.alloc(NL * 256, BF16, 16)
        sp8 = A.alloc(16, F32)
        sp16 = A.alloc(16, F32)
        nba = A.alloc(4, F32)
        zeros_bf = A.alloc(8 * 65, BF16)
        tmpc = A.alloc(NL * 256, F32)
        hsp8 = A.alloc(16, F32)
        hsp16 = A.alloc(16, F32)
        hbr = A.alloc(16, F32)
        hbi = A.alloc(16, F32)
        A.persist()

        dma('sp', vec, vec128, writes=['vec'])
        dma('sp', cst, cst_in, writes=['cst'])
        dma('sp', tmpc[0:8, 0:NL], bf_in, writes=['tmpbf'])
        op('dve', lambda e: e.tensor_scalar(out=nbf, in0=tmpc[0:8, 0:NL], scalar1=-1.0, scalar2=None, op0=ALU.mult),
           reads=['tmpbf'], writes=['nbf'])
        op('dve', lambda e: e.tensor_copy(out=ident, in_=cst[:, 0:128]), reads=['cst'], writes=['ident'])
        for r in range(4):
            op('dve', lambda e, r=r: e.tensor_copy(out=tri4[:, r * 128:(r + 1) * 128], in_=cst[:, 128:256]),
               reads=['cst'], writes=['tri4'])
        op('dve', lambda e: e.tensor_copy(out=vm0, in_=cst[:, 256:384]), reads=['cst'], writes=['vm0'])
        op('dve', lambda e: e.memset(ones_bf, 1.0), writes=['ones'])
        op('dve', lambda e: e.memset(ones_f, 1.0), writes=['ones'])
        op('dve', lambda e: e.memset(onesN, 1.0 / 128), writes=['ones'])
        op('dve', lambda e: e.memset(zeros_bf, 0.0), writes=['ones'])
        op('dve', lambda e: e.tensor_scalar(out=nba, in0=vec[:, V_NBA:V_NBA + 4], scalar1=-1.0, scalar2=None, op0=ALU.mult),
           reads=['vec'], writes=['nba'])
        op('act', lambda e: e.activation(out=sp8, in_=vec[:, V_LAM:V_LAM + 16], func=AF.Exp, scale=-1.0),
           reads=['vec'], writes=['sp8'])
        op('act', lambda e: e.activation(out=sp8, in_=sp8, func=AF.Ln, bias=1.0), reads=['sp8'], writes=['sp8'])
        op('dve', lambda e: e.tensor_scalar(out=sp16, in0=sp8, scalar1=-16.0, scalar2=None, op0=ALU.mult),
           reads=['sp8'], writes=['sp16'])
        op('dve', lambda e: e.tensor_scalar(out=sp8, in0=sp8, scalar1=-8.0, scalar2=None, op0=ALU.mult),
           reads=['sp8', 'sp16'], writes=['sp8'])
        op('dve', lambda e: e.tensor_scalar(out=hsp8, in0=sp8, scalar1=0.5, scalar2=None, op0=ALU.mult), reads=['sp8'], writes=['hsp'])
        op('dve', lambda e: e.tensor_scalar(out=hsp16, in0=sp16, scalar1=0.5, scalar2=None, op0=ALU.mult), reads=['sp16'], writes=['hsp'])
        op('dve', lambda e: e.tensor_scalar(out=hbr, in0=vec[:, V_BR:V_BR + 16], scalar1=0.5, scalar2=None, op0=ALU.mult), reads=['vec'], writes=['hsp'])
        op('dve', lambda e: e.tensor_scalar(out=hbi, in0=vec[:, V_BI:V_BI + 16], scalar1=0.5, scalar2=None, op0=ALU.mult), reads=['vec'], writes=['hsp'])
        dma('sp', tmpc[0:16, :], wa2_in, reads=['nbf'], writes=['tmpwa'])
        op('dve', lambda e: e.tensor_copy(out=wa2, in_=tmpc[0:16, :]), reads=['tmpwa'], writes=['wa2'])
        for h in range(8):
            for r in (65, 66, 67):
                dma('sp', qa[h, r:r + 1, :].rearrange("o (a b) -> (o a) b", a=NT), ones_bf[0:NT, 0:128],
                    reads=['ones'], writes=['qa_ones'])
            dma('sp', ka[h, 64:65, :].rearrange("o (a b) -> (o a) b", a=NT), ones_bf[0:NT, 0:128],
                reads=['ones'], writes=['ka_ones'])
        T.barrier()

        for l in layers:
            hsrc = h0 if l == 0 else h1
            def p4_units():
                NB = 3
                gbank = [(psF[3], psF[4]), (psF[5], psB[1].bitcast(F32))]
                gkey = [('psF3', 'psF4'), ('psF5', 'psB1')]
                wrb = A.alloc(8 * 128, BF16).rearrange("p (b n) -> p b n", b=8)
                wib = A.alloc(8 * 128, BF16).rearrange("p (b n) -> p b n", b=8)
                wst = p1stage[0][:, 0:512]
                lxe = [A.alloc(3 + 512, F32) for _ in range(NB)]
                sl = [A.alloc(512, BF16) for _ in range(NB)]
                xc = [A.alloc(512, F32) for _ in range(NB)] + [p1stage[0][:, 0:512], p1stage[1][:, 0:512]]
                xcb = [A.alloc(512, BF16) for _ in range(2)]
                rr = [A.alloc(512, F32) for _ in range(2)]
                ig = [A.alloc(512, F32) for _ in range(2)]
                aa = [A.alloc(512, F32) for _ in range(2)]
                a2 = rr
                hh = [A.alloc(512, F32) for _ in range(2)]
                hprev = A.alloc(8, F32)
                yl = [A.alloc(512, BF16) for _ in range(NB)]
                for hf in range(2):
                    for (wsrc, wdst, kk) in ((wr_in, wrb, 'wrb'), (wi_in, wib, 'wib')):
                        dma('sp', wst, wsrc[:, l * 1024 + hf * 512:l * 1024 + (hf + 1) * 512], reads=['stage0'], writes=['stage0'])
                        op('dve', lambda e: e.tensor_copy(out=wdst.rearrange("p b n -> p (b n)")[:, hf * 512:(hf + 1) * 512], in_=wst),
                           reads=['stage0'], writes=[kk])
                units = [(J, bl) for J in range(len(STS)) for bl in range(8)]
                yield

                def geom(u):
                    J, bl = units[u]
                    tiles = STS[J]
                    return J, bl, len(tiles) * 128, tiles[0] * 128, u % NB, u % 2, u % 5

                def stA(u, part):
                    J, bl, ntok, q0, s, s2, s5 = geom(u)
                    rows = slice(bl * 128, (bl + 1) * 128)
                    kl = 'lxe%d' % s
                    if part == 0 and J == 0:
                        op('dve', lambda e: e.memset(lxe[s][:, 0:3], 0.0), writes=[kl])
                        dma('sp', lxe[s][:, 3:3 + ntok], lxT[rows, q0:q0 + ntok], reads=[('lx', J)], writes=[kl])
                    elif part == 0:
                        dma('sp', lxe[s][:, 0:3 + ntok], lxT[rows, q0 - 3:q0 + ntok], reads=[('lx', J), ('lx', J - 1)], writes=[kl])
                    cwi = V_CW + (l * 8 + bl) * 4
                    cbi = V_CB + l * 8 + bl
                    kx = 'xc%d' % s5
                    if part == 0:
                      op('dve', lambda e: e.tensor_scalar(out=xc[s5][:, 0:ntok], in0=lxe[s][:, 3:3 + ntok], scalar1=vec[:, cwi + 3:cwi + 4],
                                                        scalar2=vec[:, cbi:cbi + 1], op0=ALU.mult, op1=ALU.add), reads=[kl, 'vec'], writes=[kx])
                    for k in ((2,) if part == 0 else (1, 0)):
                        op('dve', lambda e: e.scalar_tensor_tensor(out=xc[s5][:, 0:ntok], in0=lxe[s][:, k:k + ntok], scalar=vec[:, cwi + k:cwi + k + 1],
                                                                   in1=xc[s5][:, 0:ntok], op0=ALU.mult, op1=ALU.add), reads=[kl, 'vec', kx], writes=[kx])
                    if part == 1 and J == 0:
                        op('dve', lambda e: e.tensor_tensor(out=xc[s5][:, 0:128], in0=xc[s5][:, 0:128], in1=vm0, op=ALU.mult),
                           reads=[kx, 'vm0'], writes=[kx])

                def stA2(u):
                    J, bl, ntok, q0, s, s2, s5 = geom(u)
                    op('act', lambda e: e.activation(out=xcb[s2][:, 0:ntok], in_=xc[s5][:, 0:ntok], func=AF.Copy), reads=['xc%d' % s5], writes=['xcb%d' % s2])

                def stB(u):
                    J, bl, ntok, q0, s, s2, s5 = geom(u)
                    rps, ips = gbank[s2]
                    kr, ki = gkey[s2]
                    rows = slice(bl * 128, (bl + 1) * 128)
                    dma('sp', sl[s][:, 0:ntok], slg[rows, q0:q0 + ntok], reads=[('lx', J)], writes=['sl%d' % s])
                    op('pe', lambda e: e.matmul(rps[:, 0:ntok], lhsT=wrb[:, bl, :], rhs=xcb[s2][:, 0:ntok], start=True, stop=True),
                       reads=['wrb', 'xcb%d' % s2], writes=[kr])
                    op('pe', lambda e: e.matmul(ips[:, 0:ntok], lhsT=wib[:, bl, :], rhs=xcb[s2][:, 0:ntok], start=True, stop=True),
                       reads=['wib', 'xcb%d' % s2], writes=[ki])

                def stB2(u, part):
                    J, bl, ntok, q0, s, s2, s5 = geom(u)
                    rps, ips = gbank[s2]
                    kr, ki = gkey[s2]
                    bri = V_BR + l * 8 + bl
                    bii = V_BI + l * 8 + bl
                    li = l * 8 + bl
                    if part == 1:
                        for (o_, sc_) in ((aa[s2], hsp8), (a2[s2], hsp16)):
                            op('act', lambda e: e.activation(out=o_[:, 0:ntok], in_=rr[s2][:, 0:ntok], func=AF.Exp,
                                                             scale=sc_[:, li:li + 1], bias=sc_[:, li:li + 1]),
                               reads=['rr%d' % s2, 'hsp'], writes=['aa%d' % s2 if o_ is aa[s2] else 'rr%d' % s2])
                        op('act', lambda e: e.activation(out=a2[s2][:, 0:ntok], in_=a2[s2][:, 0:ntok], func=AF.Ln, scale=-1.0, bias=1.0),
                           reads=['rr%d' % s2], writes=['rr%d' % s2])
                        op('act', lambda e: e.activation(out=a2[s2][:, 0:ntok], in_=a2[s2][:, 0:ntok], func=AF.Exp, scale=0.5),
                           reads=['rr%d' % s2], writes=['rr%d' % s2])
                        return
                    op('act', lambda e: e.activation(out=rr[s2][:, 0:ntok], in_=rps[:, 0:ntok], func=AF.Tanh, scale=0.5, bias=hbr[:, li:li + 1]),
                       reads=[kr, 'hsp'], writes=['rr%d' % s2])
                    op('act', lambda e: e.activation(out=ig[s2][:, 0:ntok], in_=ips[:, 0:ntok], func=AF.Tanh, scale=0.5, bias=hbi[:, li:li + 1]),
                       reads=[ki, 'hsp'], writes=['ig%d' % s2])

                def stC(u):
                    J, bl, ntok, q0, s, s2, s5 = geom(u)
                    op('pool', lambda e: e.tensor_scalar(out=ig[s2][:, 0:ntok], in0=ig[s2][:, 0:ntok], scalar1=1.0, scalar2=0.5, op0=ALU.add, op1=ALU.mult),
                       reads=['ig%d' % s2], writes=['ig%d' % s2])
                    op('pool', lambda e: e.tensor_tensor(out=ig[s2][:, 0:ntok], in0=ig[s2][:, 0:ntok], in1=xc[s5][:, 0:ntok], op=ALU.mult),
                       reads=['ig%d' % s2, 'xc%d' % s5], writes=['ig%d' % s2])
                    op('pool', lambda e: e.tensor_tensor(out=ig[s2][:, 0:ntok], in0=ig[s2][:, 0:ntok], in1=a2[s2][:, 0:ntok], op=ALU.mult),
                       reads=['ig%d' % s2, 'rr%d' % s2], writes=['ig%d' % s2])
                    init = 0.0 if J == 0 else hprev[:, bl:bl + 1]
                    op('dve', lambda e: e.tensor_tensor_scan(out=hh[s2][:, 0:ntok], data0=aa[s2][:, 0:ntok], data1=ig[s2][:, 0:ntok],
                                                             initial=init, op0=ALU.mult, op1=ALU.add),
                       reads=['aa%d' % s2, 'ig%d' % s2, 'hprev'], writes=['hh%d' % s2])
                    op('dve', lambda e: e.tensor_copy(out=hprev[:, bl:bl + 1], in_=hh[s2][:, ntok - 1:ntok]), reads=['hh%d' % s2], writes=['hprev'])
                    op('dve', lambda e: e.tensor_tensor(out=yl[s][:, 0:ntok], in0=hh[s2][:, 0:ntok], in1=sl[s][:, 0:ntok], op=ALU.mult),
                       reads=['hh%d' % s2, 'sl%d' % s], writes=['yl%d' % s])

                def stD(u):
                    J, bl, ntok, q0, s, s2, s5 = geom(u)
                    dma('sp', yT[1024 + bl * 128:1024 + (bl + 1) * 128, q0:q0 + ntok], yl[s][:, 0:ntok], reads=['yl%d' % s], writes=['scr'])

                n = len(units)
                for t in range(n + 5):
                    pieces = []
                    if 0 <= t - 5 < n:
                        pieces.append((stD, (t - 5,)))
                    if 0 <= t - 4 < n:
                        pieces.append((stC, (t - 4,)))
                    if 0 <= t - 3 < n:
                        pieces.append((stB2, (t - 3, 0)))
                        pieces.append((stB2, (t - 3, 1)))
                    if 0 <= t - 2 < n:
                        pieces.append((stB, (t - 2,)))
                    if 0 <= t - 1 < n:
                        pieces.append((stA2, (t - 1,)))
                    if t < n:
                        pieces.append((stA, (t, 0)))
                        pieces.append((stA, (t, 1)))
                    for i, (f_, a_) in enumerate(pieces):
                        f_(*a_)
                        p4n[0] = t if i + 1 < len(pieces) else t + 1
                        yield

            if 'P1' in phases:
                A.reset()
                W = A.alloc(8 * DIN, BF16).rearrange("p (c n) -> p c n", c=8)
                SC = 808
                stage = [A.alloc(SC, F32) for _ in range(2)]
                p1stage = stage
                hb = [A.alloc(D, F32) for _ in range(2)]
                junk = A.alloc(D, BF16)
                ssq = [A.alloc(1, F32) for _ in range(3)]
                hn = [A.alloc(D, BF16) for _ in range(2)]
                hnT = [A.alloc(8 * 512, BF16).rearrange("p (c n) -> p c n", c=8) for _ in range(2)]
                evf = [A.alloc(512, F32) for _ in range(4)]
                evb = [A.alloc(512, BF16) for _ in range(4)]
                vt = [A.alloc(8 * 65, BF16).rearrange("p (h c) -> p h c", h=8) for _ in range(2)]
                gvt = [A.alloc(512, BF16) for _ in range(2)]
                ffs = A.alloc(512, F32, 8)
                spf = ffs
                g4 = p4_units()
                p4n = [0]
                gcount = [0]

                def p4adv(k, lim):
                    for _ in range(k):
                        if p4n[0] < lim:
                            next(g4)
                pstores = []

                def sdma(dst, src_, reads=(), writes=()):
                    pstores.append((dst, src_, reads, writes))

                def sflush(keep):
                    while len(pstores) > keep:
                        d_, s_, r_, w_ = pstores.pop(0)
                        dma('sp', d_, s_, reads=r_, writes=w_)
                cc = A.alloc(512, F32, 8)
                cprev = A.alloc(1, F32, 8)
                cq = A.alloc(512, BF16, 8)
                kp = A.alloc(3 * 512, BF16, 8).rearrange("p (r n) -> p r n", r=3)
                r1 = A.alloc(512, F32, 8)
                gab = A.alloc(512, BF16, 16)
                si = 0
                for c in range(8):
                    for s0 in range(0, DIN, SC):
                        st = stage[si % 2]
                        k = 'stage%d' % (si % 2)
                        si += 1
                        dma('sp', st, w_in[l, c * 128:(c + 1) * 128, s0:s0 + SC], writes=[k])
                        op('dve', lambda e, st=st, c=c, s0=s0: e.tensor_scalar(
                            out=W[:, c, s0:s0 + SC], in0=st, scalar1=vec[:, V_PREG + l * 8 + c:V_PREG + l * 8 + c + 1],
                            scalar2=None, op0=ALU.mult), reads=[k, 'vec'], writes=['W'])
                for hh in range(2):
                    op('dve', lambda e, hh=hh: e.memset(vt[hh][:, :, 64:65], 1.0), writes=['vt%d' % hh])
                psi = 0
                evi = 0
                tcount = 0
                pinfo = {}

                def prep_a(Jp, ti):
                    nonlocal tcount
                    t = STS[Jp][ti]
                    b = hb[tcount % 2]
                    kb = 'hb%d' % (tcount % 2)
                    sq = ssq[tcount % 3]
                    ksq = 'ssq%d' % (tcount % 3)
                    n_ = hn[tcount % 2]
                    kn = 'hn%d' % (tcount % 2)
                    pinfo[(Jp, ti)] = (n_, kn)
                    tcount += 1
                    dma('sp', b, hsrc[t * 128:(t + 1) * 128, :], writes=[kb])
                    op('act', lambda e: e.activation(out=junk, in_=b, func=AF.Square, accum_out=sq), reads=[kb], writes=['junk', ksq])
                    op('act', lambda e: e.activation(out=sq, in_=sq, func=AF.Ln, scale=1.0 / D, bias=EPS), reads=[ksq], writes=[ksq])
                    op('act', lambda e: e.activation(out=sq, in_=sq, func=AF.Exp, scale=-0.5), reads=[ksq], writes=[ksq])
                    op('dve', lambda e: e.tensor_scalar(out=n_, in0=b, scalar1=sq, scalar2=None, op0=ALU.mult), reads=[kb, ksq], writes=[kn])

                def prep_b(Jp, ti):
                    n_, kn = pinfo.pop((Jp, ti))
                    X = hnT[Jp % 2]
                    kX = 'hnT%d' % (Jp % 2)
                    pT = psB[0]
                    kpT = 'psB0'
                    for c in range(8):
                        op('pe', lambda e: e.transpose(out=pT[:, c * 128:(c + 1) * 128], in_=n_[:, c * 128:(c + 1) * 128], identity=ident),
                           reads=[kn, 'ident'], writes=[kpT])
                    if ti % 2 == 0:
                        op('act', lambda e: e.activation(out=X[:, :, ti * 128:(ti + 1) * 128], in_=pT.rearrange("p (c n) -> p c n", c=8), func=AF.Copy),
                           reads=[kpT], writes=[kX])
                    else:
                        op('dve', lambda e: e.tensor_copy(out=X[:, :, ti * 128:(ti + 1) * 128], in_=pT.rearrange("p (c n) -> p c n", c=8)),
                           reads=[kpT], writes=[kX])

                next(g4)
                for J, tiles in enumerate(STS):
                    ntl = len(tiles)
                    ntok = ntl * 128
                    q0 = tiles[0] * 128
                    X = hnT[J % 2]
                    kX = 'hnT%d' % (J % 2)
                    if J == 0:
                        prep_a(0, 0)
                        prep_b(0, 0)
                    gl = [0]
                    def fm_group(c0, M, evac):
                        nonlocal psi
                        ps = psF[psi % 3]
                        kps = 'psF%d' % (psi % 3)
                        psi += 1
                        gcount[0] += 1
                        gl[0] += 1
                        if J + 1 < len(STS) and gl[0] in (4, 12, 20, 28):
                            prep_a(J + 1, (gl[0] - 4) // 8)
                        if J + 1 < len(STS) and gl[0] in (8, 16, 24, 32):
                            prep_b(J + 1, (gl[0] - 8) // 8)
                        sflush(3)
                        p4adv(2 if gcount[0] % 2 == 0 else 1, 8 * J)
                        for kc in range(8):
                            op('pe', lambda e, ps=ps, kc=kc: e.matmul(ps[0:M, 0:ntok], lhsT=W[:, kc, c0:c0 + M],
                                                                     rhs=X[:, kc, 0:ntok], start=(kc == 0), stop=(kc == 7)),
                               reads=['W', kX], writes=[kps])
                        evac(ps[0:M, 0:ntok], kps)

                    def ev_copy(dst_list, scale=None, dtype=BF16, eng='dve', wkey='scr'):
                        def f(ps, kps):
                            nonlocal evi
                            M = ps.shape[0]
                            buf = (evb if dtype == BF16 else evf)[evi % 4]
                            kb_ = ('evb%d' if dtype == BF16 else 'evf%d') % (evi % 4)
                            evi += 1
                            o = buf[0:M, 0:ntok]
                            if eng == 'silu':
                                op('act', lambda e: e.activation(out=o, in_=ps, func=AF.Silu), reads=[kps], writes=[kb_])
                            elif scale is not None:
                                op('dve', lambda e: e.tensor_scalar(out=o, in0=ps, scalar1=scale, scalar2=None, op0=ALU.mult),
                                   reads=[kps], writes=[kb_])
                            else:
                                op('dve', lambda e: e.tensor_copy(out=o, in_=ps), reads=[kps], writes=[kb_])
                            for (r0, nr, dst) in dst_list:
                                sdma(dst, buf[r0:r0 + nr, 0:ntok], reads=[kb_], writes=[wkey])
                        return f

                    tk = slice(q0, q0 + ntok)
                    def ev_ff(ps, kps):
                        op('dve', lambda e: e.tensor_copy(out=ffs[:, 0:ntok], in_=ps), reads=[kps], writes=['ffs'])
                    fm_group(C_FF, 8, ev_ff)
                    op('act', lambda e: e.activation(out=spf[:, 0:ntok], in_=ffs[:, 0:ntok], func=AF.Exp,
                                                     bias=nbf[:, l:l + 1], scale=-1.0), reads=['ffs', 'nbf'], writes=['ffs'])
                    op('act', lambda e: e.activation(out=spf[:, 0:ntok], in_=spf[:, 0:ntok], func=AF.Ln, bias=1.0),
                       reads=['ffs'], writes=['ffs'])
                    if J == 0:
                        op('dve', lambda e: e.tensor_tensor(out=spf[:, 0:128], in0=spf[:, 0:128], in1=vm0[0:8, :], op=ALU.mult),
                           reads=['ffs', 'vm0'], writes=['ffs'])
                    init = 0.0 if J == 0 else cprev
                    op('dve', lambda e, init=init: e.tensor_tensor_scan(out=cc[:, 0:ntok], data0=ones_f[0:8, 0:ntok], data1=spf[:, 0:ntok],
                                                                        initial=init, op0=ALU.mult, op1=ALU.subtract),
                       reads=['ffs', 'ones', 'cprev'], writes=['cc'])
                    op('dve', lambda e: e.tensor_copy(out=cprev, in_=cc[:, ntok - 1:ntok]), reads=['cc'], writes=['cprev'])
                    op('dve', lambda e: e.tensor_copy(out=cq[:, 0:ntok], in_=cc[:, 0:ntok]), reads=['cc'], writes=['cq'])
                    op('dve', lambda e: e.tensor_scalar(out=kp[:, 0, 0:ntok], in0=cc[:, 0:ntok], scalar1=-1.0, scalar2=None, op0=ALU.mult),
                       reads=['cc'], writes=['kp'])
                    op('dve', lambda e: e.scalar_tensor_tensor(out=r1[:, 0:ntok], in0=cc[:, 0:ntok], scalar=-1.0, in1=kp[:, 0, 0:ntok],
                                                               op0=ALU.mult, op1=ALU.subtract), reads=['cc', 'kp'], writes=['r1'])
                    op('dve', lambda e: e.tensor_copy(out=kp[:, 1, 0:ntok], in_=r1[:, 0:ntok]), reads=['r1'], writes=['kp'])
                    op('dve', lambda e: e.tensor_tensor(out=r1[:, 0:ntok], in0=r1[:, 0:ntok], in1=kp[:, 1, 0:ntok], op=ALU.subtract),
                       reads=['r1', 'kp'], writes=['r1'])
                    op('dve', lambda e: e.tensor_copy(out=kp[:, 2, 0:ntok], in_=r1[:, 0:ntok]), reads=['r1'], writes=['kp'])
                    sdma(qa[:, 64, tk], cq[:, 0:ntok], reads=['cq'], writes=['scr'])
                    sdma(ka[:, 65:68, tk], kp[:, :, 0:ntok], reads=['kp'], writes=['scr'])

                    def ev_ga(ps, kps):
                        op('dve', lambda e: e.tensor_copy(out=gab[:, 0:ntok], in_=ps), reads=[kps], writes=['gab'])
                        sdma(gaT[:, tk], gab[:, 0:ntok], reads=['gab'], writes=['scr'])
                    fm_group(C_GA, 16, ev_ga)
                    for g in range(4):
                        fm_group(C_FQ + g * 128, 128, ev_copy([(0, 64, qa[2 * g, 0:64, tk]), (64, 64, qa[2 * g + 1, 0:64, tk])], scale=0.125))
                    for g in range(4):
                        fm_group(C_FK + g * 128, 128, ev_copy([(0, 64, ka[2 * g, 0:64, tk]), (64, 64, ka[2 * g + 1, 0:64, tk])]))
                    for g in range(4):
                        fm_group(C_FG + g * 128, 128, ev_copy([(0, 128, sgf[g * 128:(g + 1) * 128, tk])], eng='silu'))
                    for g in range(2):
                        fm_group(C_GQ + g * 128, 128, ev_copy([(0, 128, gqT[g * 128:(g + 1) * 128, tk])], dtype=F32))
                    for g in range(2):
                        fm_group(C_GK + g * 128, 128, ev_copy([(0, 128, gkT[g * 128:(g + 1) * 128, tk])], dtype=F32))
                    for g in range(4):
                        fm_group(C_GG + g * 128, 128, ev_copy([(0, 128, sgg[g * 128:(g + 1) * 128, tk])], eng='silu'))
                    for g in range(8):
                        fm_group(C_LX + g * 128, 128, ev_copy([(0, 128, lxT[g * 128:(g + 1) * 128, tk])], dtype=F32, wkey=('lx', J)))
                    for g in range(8):
                        fm_group(C_LG + g * 128, 128, ev_copy([(0, 128, slg[g * 128:(g + 1) * 128, tk])], eng='silu', wkey=('lx', J)))
                    for ti, t in enumerate(tiles):
                        for which in (0, 1):
                            ps = psF[psi % 3]
                            kps = 'psF%d' % (psi % 3)
                            psi += 1
                            c0 = C_FV if which == 0 else C_GV
                            sflush(3)
                            p4adv(2, 8 * J)
                            for kc in range(8):
                                op('pe', lambda e, ps=ps, kc=kc, c0=c0, ti=ti: e.matmul(
                                    ps[:, :], lhsT=X[:, kc, ti * 128:(ti + 1) * 128], rhs=W[:, kc, c0:c0 + 512],
                                    start=(kc == 0), stop=(kc == 7)), reads=['W', kX], writes=[kps])
                            if which == 0:
                                v_ = vt[t % 2]
                                kv_ = 'vt%d' % (t % 2)
                                op('dve', lambda e, ps=ps, v_=v_: e.tensor_copy(out=v_[:, :, 0:64],
                                                                               in_=ps.rearrange("p (h c) -> p h c", h=8)),
                                   reads=[kps], writes=[kv_])
                                if t == 0:
                                    sdma(va[:, 112:128, :].rearrange("h t c -> t h c"), v_[112:128, :, :], reads=[kv_], writes=['scr'])
                                    sdma(va[:, 0:112, :].rearrange("h t c -> t h c"),
                                        zeros_bf[0:112, :].rearrange("p (h c) -> p h c", h=8), reads=['ones'], writes=['scr'])
                                else:
                                    sdma(va[:, t * 128:(t + 1) * 128, :].rearrange("h t c -> t h c"), v_, reads=[kv_], writes=['scr'])
                            else:
                                g_ = gvt[t % 2]
                                kg_ = 'gvt%d' % (t % 2)
                                op('act', lambda e, ps=ps, g_=g_: e.activation(out=g_, in_=ps, func=AF.Copy), reads=[kps], writes=[kg_])
                                sdma(gv[t * 128:(t + 1) * 128, :], g_, reads=[kg_], writes=['scr'])
                    while p4n[0] < 8 * J:
                        next(g4)
                sflush(0)
                for _ in g4:
                    pass
                T.barrier()

            if 'P2' in phases:
                A.reset()
                Qa = [A.alloc(L, BF16, 68) for _ in range(2)]
                Ka = [A.alloc(L, BF16, 68) for _ in range(2)]
                Va = [A.alloc(NT * 65, BF16).rearrange("p (n c) -> p n c", n=NT) for _ in range(2)]
                SG = [A.alloc(L, BF16, 64) for _ in range(2)]
                pt = [A.alloc(512, BF16) for _ in range(3)]
                dn = A.alloc(512, F32, 65)
                rdb = A.alloc(512, BF16, 65)
                t1 = [A.alloc(512, F32, 64) for _ in range(2)]
                yb = [A.alloc(512, BF16, 64) for _ in range(2)]
                pt.append(A.alloc(512, BF16))
                pt.append(A.alloc(512, BF16))
                pt.append(A.alloc(512, BF16))
                bpsF = psB[1].bitcast(F32)
                spsB = [psF[0], psF[1], psF[2], psF[3], psB[0].bitcast(F32)]
                spsK = ['psF0', 'psF1', 'psF2', 'psF3', 'psB0f']
                gk_ = [0]
                blk_ = [0]
                for h in range(8):
                    s = h % 2
                    kQ, kK, kV, kS = 'Qa%d' % s, 'Ka%d' % s, 'Va%d' % s, 'SG%d' % s
                    dma('sp', Qa[s], qa[h], writes=[kQ])
                    dma('sp', Ka[s], ka[h], writes=[kK])
                    for n0 in range(0, NT, 8):
                        n1 = min(NT, n0 + 8)
                        dma('sp', Va[s][:, n0:n1, :], va[h, n0 * 128:n1 * 128, :].rearrange("(n p) c -> p n c", p=128), writes=[kV])
                    dma('sp', SG[s], sgf[h * 64:(h + 1) * 64, :], writes=[kS])
                    tasks = []
                    for J, tiles in enumerate(STS):
                        ob = blk_[0] % 2
                        blk_[0] += 1
                        for I in range(tiles[-1] + 1):
                            tasks.append((J, I, ob, gk_[0]))
                            gk_[0] += 1
                    pend = []

                    def emit_S(task):
                        J, I, ob, g = task
                        tiles = STS[J]
                        ntok = len(tiles) * 128
                        q0 = tiles[0] * 128
                        off = max(0, I - tiles[0]) * 128
                        sps = spsB[g % 5]
                        ksps = spsK[g % 5]
                        p_ = pt[g % 6]
                        kp_ = 'pt%d' % (g % 6)
                        op('pe', lambda e: e.matmul(sps[:, off:ntok], lhsT=Ka[s][:, I * 128:(I + 1) * 128],
                                                    rhs=Qa[s][:, q0 + off:q0 + ntok], start=True, stop=True),
                           reads=[kQ, kK], writes=[ksps])
                        op('act', lambda e: e.activation(out=p_[:, off:ntok], in_=sps[:, off:ntok], func=AF.Exp),
                           reads=[ksps], writes=[kp_])
                        if I >= tiles[0]:
                            op('dve', lambda e: e.tensor_tensor(out=p_[:, off:off + 128], in0=p_[:, off:off + 128],
                                                                 in1=tri4[:, 0:128], op=ALU.mult),
                               reads=[kp_, 'tri4'], writes=[kp_])

                    def emit_PV(task):
                        J, I, ob, g = task
                        tiles = STS[J]
                        ntok = len(tiles) * 128
                        q0 = tiles[0] * 128
                        last = tiles[-1]
                        off = max(0, I - tiles[0]) * 128
                        ops_ = psF[4 + ob]
                        kops = 'psF%d' % (4 + ob)
                        p_ = pt[g % 6]
                        kp_ = 'pt%d' % (g % 6)
                        op('pe', lambda e: e.matmul(ops_[0:65, off:ntok], lhsT=Va[s][:, I, :], rhs=p_[:, off:ntok],
                                                    start=(I == 0), stop=(I == last)), reads=[kV, kp_], writes=[kops])
                        if I != last:
                            return
                        while pend:
                            pend.pop(0)[1]()
                        op('dve', lambda e: e.tensor_scalar(out=dn[64:65, 0:ntok], in0=ops_[64:65, 0:ntok], scalar1=1e-30,
                                                            scalar2=None, op0=ALU.max), reads=[kops], writes=['dn'])
                        op('dve', lambda e: e.reciprocal(out=dn[64:65, 0:ntok], in_=dn[64:65, 0:ntok]), reads=['dn'], writes=['dn'])
                        op('dve', lambda e: e.tensor_copy(out=rdb[64:65, 0:ntok], in_=dn[64:65, 0:ntok]), reads=['dn'], writes=['rdb'])

                        def partB():
                            bps = bpsF
                            op('pe', lambda e: e.matmul(bps[0:64, 0:ntok], lhsT=ones_bf[64:65, 0:64], rhs=rdb[64:65, 0:ntok],
                                                        start=True, stop=True), reads=['rdb', 'ones'], writes=['bpsF'])
                            t_ = t1[ob]
                            kt_ = 't1%d' % ob
                            y_ = yb[ob]
                            ky_ = 'yb%d' % ob
                            op('dve', lambda e: e.tensor_tensor(out=t_[:, 0:ntok], in0=ops_[0:64, 0:ntok], in1=SG[s][:, q0:q0 + ntok],
                                                                op=ALU.mult), reads=[kops, kS], writes=[kt_])
                            op('dve', lambda e: e.tensor_tensor(out=y_[:, 0:ntok], in0=t_[:, 0:ntok], in1=bps[0:64, 0:ntok],
                                                                op=ALU.mult), reads=[kt_, 'bpsF'], writes=[ky_])
                            dma('act', yT[h * 64:(h + 1) * 64, q0:q0 + ntok], y_[:, 0:ntok], reads=[ky_], writes=['scr'])
                        pend.append([3, partB])

                    DPT = 4
                    for k in range(len(tasks) + DPT):
                        if k < len(tasks):
                            emit_S(tasks[k])
                        if k - DPT >= 0:
                            for pb in pend:
                                pb[0] -= 1
                            while pend and pend[0][0] <= 0:
                                pend.pop(0)[1]()
                            emit_PV(tasks[k - DPT])
                    while pend:
                        pend.pop(0)[1]()
                T.barrier()

            def p3_units():
                r3 = lambda n, dt_: [A.alloc(n, dt_) for _ in range(3)]
                r2 = lambda n, dt_: [A.alloc(n, dt_) for _ in range(2)]
                v3 = lambda lst, c: [x.rearrange("p (c n) -> p c n", c=c) for x in lst]
                gq = v3(r3(1024, F32), 2)
                gk = v3(r3(1024, F32), 2)
                ga_ = [A.alloc(512, BF16, 16) for _ in range(3)]
                gvs = v3(r3(2048, BF16), 4)
                sg_ = v3(r3(2048, BF16), 4)
                ee = v3([A.alloc(1024, F32)] * 2, 2)
                bs = v3([A.alloc(1024, F32)] * 2, 2)
                ebh = v3([A.alloc(1024, F32)] * 2, 2)
                ebq = v3([A.alloc(1024, F32)] * 2, 2)
                ebk = v3([A.alloc(1024, F32)] * 2, 2)
                qz = [v3([A.alloc(1024, BF16) for _ in range(2)], 2) for _ in range(3)]
                kt = v3(r3(1024, BF16), 2)
                khT = v3(r3(1024, BF16), 2)
                nbl = v3(r3(8, F32), 2)
                dec = v3(r3(8, F32), 2)
                yg = v3(r2(2048, BF16), 4)
                at = r2(512, BF16)
                kh = r2(256, BF16)
                sqb = r2(512, BF16)
                rs = r2(512, F32)
                t1g = r2(512, F32)
                S = A.alloc(2 * 128, F32).rearrange("p (c n) -> p c n", c=2)
                Sbf = A.alloc(2 * 128, BF16).rearrange("p (c n) -> p c n", c=2)
                op('dve', lambda e: e.memset(S, 0.0), writes=['S'])
                op('dve', lambda e: e.memset(Sbf, 0.0), writes=['Sbf'])
                for j in range(3):
                    for p in range(2):
                        op('dve', lambda e: e.memset(qz[j][p], 0.0), writes=['qz%d' % j])
                aps, opsb, ups = psF[2], [psF[3], psF[4]], psF[5]
                xm = psB[1].bitcast(F32)
                tps = psB[0]
                flat = [(J, tt) for J, tiles in enumerate(STS) for tt in range(len(tiles))]
                v0 = {}
                for v, (J, tt) in enumerate(flat):
                    v0.setdefault(J, v)

                def geo(J):
                    tiles = STS[J]
                    return len(tiles), len(tiles) * 128, slice(tiles[0] * 128, (tiles[-1] + 1) * 128), J % 3, J % 2

                def PA(J):
                    ntl, ntok, tk, j3, j2 = geo(J)
                    dma('sp', gq[j3][:, :, 0:ntok], gqT[:, tk].rearrange("(c p) t -> p c t", p=128), writes=['gq%d' % j3])
                    dma('sp', gk[j3][:, :, 0:ntok], gkT[:, tk].rearrange("(c p) t -> p c t", p=128), writes=['gk%d' % j3])
                    dma('sp', ga_[j3][:, 0:ntok], gaT[:, tk], writes=['ga%d' % j3])
                    dma('sp', gvs[j3][:, 0:ntl, :], gv[tk, :].rearrange("(t p) n -> p t n", p=128), writes=['gvs%d' % j3])
                    dma('sp', sg_[j3][:, :, 0:ntok], sgg[:, tk].rearrange("(c p) t -> p c t", p=128), writes=['sg%d' % j3])
                    for c in range(2):
                        op('pe', lambda e: e.matmul(xm[:, 0:ntok], lhsT=wa2[:, l * 256 + c * 128:l * 256 + (c + 1) * 128],
                                                    rhs=ga_[j3][:, 0:ntok], start=True, stop=True), reads=['wa2', 'ga%d' % j3], writes=['xm'])
                        op('act', lambda e: e.activation(out=ee[j2][:, c, 0:ntok], in_=xm[:, 0:ntok], func=AF.Exp,
                                                         bias=nba[:, l * 2 + c:l * 2 + c + 1], scale=-1.0), reads=['xm', 'nba'], writes=['ee'])
                        op('act', lambda e: e.activation(out=ee[j2][:, c, 0:ntok], in_=ee[j2][:, c, 0:ntok], func=AF.Ln, bias=1.0),
                           reads=['ee'], writes=['ee'])

                def PB(J):
                    ntl, ntok, tk, j3, j2 = geo(J)
                    for c in range(2):
                        for tt in range(ntl):
                            r = slice(tt * 128, (tt + 1) * 128)
                            op('dve', lambda e: e.tensor_tensor_scan(out=bs[j2][:, c, r], data0=ones_f[:, 0:128], data1=ee[j2][:, c, r],
                                                                     initial=0.0, op0=ALU.mult, op1=ALU.add),
                               reads=['ee', 'ones'], writes=['bs'])
                            op('dve', lambda e: e.tensor_scalar(out=nbl[j3][:, c, tt:tt + 1], in0=bs[j2][:, c, tt * 128 + 127:tt * 128 + 128],
                                                                scalar1=-1.0 / 16, scalar2=None, op0=ALU.mult),
                               reads=['bs'], writes=['nbl%d' % j3])

                def PC(J):
                    ntl, ntok, tk, j3, j2 = geo(J)
                    for c in range(2):
                        for tt in range(ntl):
                            r = slice(tt * 128, (tt + 1) * 128)
                            op('act', lambda e: e.activation(out=ebh[j2][:, c, r], in_=bs[j2][:, c, r], func=AF.Exp,
                                                             bias=nbl[j3][:, c, tt:tt + 1], scale=1.0 / 16),
                               reads=['bs', 'nbl%d' % j3], writes=['ebh'])
                        op('act', lambda e: e.activation(out=dec[j3][:, c, 0:ntl], in_=nbl[j3][:, c, 0:ntl], func=AF.Exp),
                           reads=['nbl%d' % j3], writes=['dec%d' % j3])
                        op('act', lambda e: e.activation(out=ebq[j2][:, c, 0:ntok], in_=bs[j2][:, c, 0:ntok], func=AF.Exp, scale=-1.0 / 16),
                           reads=['bs'], writes=['ebq'])
                        op('act', lambda e: e.activation(out=ebk[j2][:, c, 0:ntok], in_=bs[j2][:, c, 0:ntok], func=AF.Exp, scale=1.0 / 16),
                           reads=['bs'], writes=['ebk'])

                def PD(J):
                    ntl, ntok, tk, j3, j2 = geo(J)
                    for c in range(2):
                        op('dve', lambda e: e.tensor_tensor(out=khT[j3][:, c, 0:ntok], in0=gk[j3][:, c, 0:ntok], in1=ebh[j2][:, c, 0:ntok], op=ALU.mult),
                           reads=['gk%d' % j3, 'ebh'], writes=['khT%d' % j3])
                        for p in range(2):
                            pr = slice(p * 64, (p + 1) * 64)
                            op('dve', lambda e: e.scalar_tensor_tensor(out=qz[j3][p][pr, c, 0:ntok], in0=gq[j3][pr, c, 0:ntok], scalar=0.125,
                                                                       in1=ebq[j2][pr, c, 0:ntok], op0=ALU.mult, op1=ALU.mult),
                               reads=['gq%d' % j3, 'ebq'], writes=['qz%d' % j3])
                        op('dve', lambda e: e.tensor_tensor(out=kt[j3][:, c, 0:ntok], in0=gk[j3][:, c, 0:ntok], in1=ebk[j2][:, c, 0:ntok], op=ALU.mult),
                           reads=['gk%d' % j3, 'ebk'], writes=['kt%d' % j3])

                def TA(v):
                    J, tt = flat[v]
                    j3 = J % 3
                    u = v % 2
                    r = slice(tt * 128, (tt + 1) * 128)
                    for hd in range(4):
                        c, p = hd // 2, hd % 2
                        op('pe', lambda e: e.matmul(aps[:, hd * 128:(hd + 1) * 128], lhsT=kt[j3][:, c, r], rhs=qz[j3][p][:, c, r],
                                                    start=True, stop=True), reads=['kt%d' % j3, 'qz%d' % j3], writes=['aps'])
                    op('dve', lambda e: e.tensor_tensor(out=at[u], in0=aps[:, :], in1=tri4, op=ALU.mult), reads=['aps', 'tri4'], writes=['at%d' % u])
                    for c in range(2):
                        op('pe', lambda e: e.transpose(out=tps[:, c * 128:(c + 1) * 128], in_=khT[j3][:, c, r], identity=ident),
                           reads=['khT%d' % j3, 'ident'], writes=['tps'])
                    op('act', lambda e: e.activation(out=kh[u], in_=tps[:, 0:256], func=AF.Copy), reads=['tps'], writes=['kh%d' % u])

                def TB(v):
                    J, tt = flat[v]
                    j3 = J % 3
                    u = v % 2
                    r = slice(tt * 128, (tt + 1) * 128)
                    ops_ = opsb[u]
                    ko = 'ops%d' % u
                    for hd in range(4):
                        c, p = hd // 2, hd % 2
                        op('pe', lambda e: e.matmul(ops_[:, hd * 128:(hd + 1) * 128], lhsT=gvs[j3][:, tt, hd * 128:(hd + 1) * 128],
                                                    rhs=at[u][:, hd * 128:(hd + 1) * 128], start=True, stop=False),
                           reads=['gvs%d' % j3, 'at%d' % u], writes=[ko])
                        op('pe', lambda e: e.matmul(ops_[:, hd * 128:(hd + 1) * 128], lhsT=Sbf[:, c, :], rhs=qz[j3][p][:, c, r],
                                                    start=False, stop=True), reads=['Sbf', 'qz%d' % j3], writes=[ko])
                    for c in range(2):
                        op('pe', lambda e: e.matmul(ups[:, c * 256:(c + 1) * 256], lhsT=kh[u][:, c * 128:(c + 1) * 128],
                                                    rhs=gvs[j3][:, tt, c * 256:(c + 1) * 256], start=True, stop=True),
                           reads=['kh%d' % u, 'gvs%d' % j3], writes=['ups'])
                    for hd in range(4):
                        c, p = hd // 2, hd % 2
                        pr = slice(p * 64, (p + 1) * 64)
                        op('dve', lambda e: e.scalar_tensor_tensor(out=S[pr, c, :], in0=S[pr, c, :], scalar=dec[j3][pr, c, tt:tt + 1],
                                                                   in1=ups[pr, c * 256 + p * 128:c * 256 + (p + 1) * 128],
                                                                   op0=ALU.mult, op1=ALU.add), reads=['S', 'dec%d' % j3, 'ups'], writes=['S'])
                    op('act', lambda e: e.activation(out=Sbf, in_=S, func=AF.Copy), reads=['S'], writes=['Sbf'])
                    op('act', lambda e: e.activation(out=sqb[u], in_=ops_[:, :], func=AF.Square), reads=[ko], writes=['sqb%d' % u])

                def TC(v):
                    J, tt = flat[v]
                    ntl, ntok, tk, j3, j2 = geo(J)
                    u = v % 2
                    r = slice(tt * 128, (tt + 1) * 128)
                    ops_ = opsb[u]
                    ko = 'ops%d' % u
                    op('pe', lambda e: e.matmul(xm[:, :], lhsT=onesN, rhs=sqb[u], start=True, stop=True), reads=['sqb%d' % u, 'ones'], writes=['xm'])
                    op('act', lambda e: e.activation(out=rs[u], in_=xm[:, :], func=AF.Ln, bias=EPS), reads=['xm'], writes=['rs%d' % u])
                    op('act', lambda e: e.activation(out=rs[u], in_=rs[u], func=AF.Exp, scale=-0.5), reads=['rs%d' % u], writes=['rs%d' % u])
                    op('dve', lambda e: e.tensor_tensor(out=t1g[u], in0=ops_[:, :], in1=rs[u], op=ALU.mult), reads=[ko, 'rs%d' % u], writes=['t1g%d' % u])
                    for hd in range(4):
                        op('dve', lambda e: e.scalar_tensor_tensor(out=yg[j2][:, hd, r], in0=t1g[u][:, hd * 128:(hd + 1) * 128],
                                                                   scalar=vec[:, V_GNG + l * 4 + hd:V_GNG + l * 4 + hd + 1],
                                                                   in1=sg_[j3][:, hd, r], op0=ALU.mult, op1=ALU.mult),
                           reads=['t1g%d' % u, 'vec', 'sg%d' % j3], writes=['yg%d' % j2])

                def TD(v):
                    J, tt = flat[v]
                    ntl, ntok, tk, j3, j2 = geo(J)
                    if tt == ntl - 1:
                        dma('sp', yT[512:1024, tk].rearrange("(c p) t -> p c t", p=128), yg[j2][:, :, 0:ntok], reads=['yg%d' % j2], writes=[('ygs', J)])
                        gla_done[0] = STS[J][-1] + 1

                nJ = len(STS)
                for t in range(-7, NT + 3):
                    if 0 <= t - 1 < NT:
                        TB(t - 1)
                        yield
                    if 0 <= t - 3 < NT:
                        TD(t - 3)
                    if 0 <= t < NT:
                        TA(t)
                        yield
                    if 0 <= t - 2 < NT:
                        TC(t - 2)
                        yield
                    for J in range(nJ):
                        k = t - (v0[J] - 7)
                        if k == 0:
                            PA(J)
                        elif k == 1:
                            PB(J)
                        elif k == 2:
                            PC(J)
                        elif k == 3:
                            PD(J)
                    yield

            def p5_units():
                Wo = A.alloc(16 * D, BF16).rearrange("p (c n) -> p c n", c=16)
                wst = A.alloc(D, F32)
                pg = A.alloc(D, F32)
                yt = [A.alloc(16 * 128, BF16).rearrange("p (c n) -> p c n", c=16) for _ in range(2)]
                hb = [A.alloc(D, F32) for _ in range(2)]
                zs = [A.alloc(D, F32) for _ in range(2)]
                junk = A.alloc(D, BF16)
                ssq = [A.alloc(1, F32) for _ in range(2)]
                for c in range(16):
                    dma('sp', wst, w_out[l, c * 128:(c + 1) * 128, :], writes=['wst5'])
                    op('dve', lambda e: e.tensor_copy(out=Wo[:, c, :], in_=wst), reads=['wst5'], writes=['Wo'])
                    if c % 4 == 3:
                        yield
                dma('sp', pg, pg_in[l], writes=['pg'])
                yield
                tJ = {}
                for J, tiles in enumerate(STS):
                    for t in tiles:
                        tJ[t] = J
                for t in range(NT):
                    while gla_done[0] <= t:
                        yield
                    u = t % 2
                    dma('sp', yt[u], yT[:, t * 128:(t + 1) * 128].rearrange("(c p) t -> p c t", p=128), reads=[('ygs', tJ[t])], writes=['yt%d' % u])
                    dma('sp', hb[u], hsrc[t * 128:(t + 1) * 128, :], writes=['hb%d' % u])
                    for nh in range(2):
                        zp = psF[nh]
                        for c0 in range(0, 16, 4):
                            for c in range(c0, c0 + 4):
                                op('pe', lambda e: e.matmul(zp[:, :], lhsT=yt[u][:, c, :], rhs=Wo[:, c, nh * 512:(nh + 1) * 512],
                                                            start=(c == 0), stop=(c == 15)), reads=['yt%d' % u, 'Wo'], writes=['psF%d' % nh])
                            yield
                        op('act', lambda e: e.activation(out=zs[u][:, nh * 512:(nh + 1) * 512], in_=zp[:, :], func=AF.Copy),
                           reads=['psF%d' % nh], writes=['zs%d' % u])
                    yield
                    op('act', lambda e: e.activation(out=junk, in_=zs[u], func=AF.Square, accum_out=ssq[u]),
                       reads=['zs%d' % u], writes=['junk5', 'ssq%d' % u])
                    op('act', lambda e: e.activation(out=ssq[u], in_=ssq[u], func=AF.Ln, scale=1.0 / D, bias=EPS),
                       reads=['ssq%d' % u], writes=['ssq%d' % u])
                    op('act', lambda e: e.activation(out=ssq[u], in_=ssq[u], func=AF.Exp, scale=-0.5),
                       reads=['ssq%d' % u], writes=['ssq%d' % u])
                    yield
                    op('dve', lambda e: e.scalar_tensor_tensor(out=zs[u], in0=zs[u], scalar=ssq[u], in1=pg, op0=ALU.mult, op1=ALU.mult),
                       reads=['zs%d' % u, 'ssq%d' % u, 'pg'], writes=['zs%d' % u])
                    op('dve', lambda e: e.tensor_tensor(out=zs[u], in0=zs[u], in1=hb[u], op=ALU.add),
                       reads=['zs%d' % u, 'hb%d' % u], writes=['zs%d' % u])
                    yield
                    if l == NL - 1 or l == layers[-1]:
                        if t >= 1:
                            dma('act', out[(t - 1) * 128:t * 128, :], zs[u], reads=['zs%d' % u], writes=['out'])
                    else:
                        dma('act', h1[t * 128:(t + 1) * 128, :], zs[u], reads=['zs%d' % u], writes=['scr'])
                    yield

            if 'P3' in phases:
                A.reset()
                gla_done = [0]
                g3, g5 = p3_units(), p5_units()
                a5 = True
                for _ in g3:
                    for _k in range(4):
                        if a5:
                            try:
                                next(g5)
                            except StopIteration:
                                a5 = False
                if a5:
                    for _ in g5:
                        pass
                T.barrier()

        T.barrier()
        with nc.Block() as block:
            @block.sync
            def _(e):
                T.replay('sp', e)

            @block.tensor
            def _(e):
                T.replay('pe', e)

            @block.scalar
            def _(e):
                T.replay('act', e)

            @block.vector
            def _(e):
                T.replay('dve', e)

            @block.gpsimd
            def _(e):
                T.replay('pool', e)
    return nc


def make_consts():
    ident = np.eye(128, dtype=np.float32)
    tri = (np.arange(128)[None, :] >= np.arange(128)[:, None]).astype(np.float32)
    vm = np.broadcast_to((np.arange(128) >= 112).astype(np.float32)[None, :], (128, 128))
    return np.ascontiguousarray(np.concatenate([ident, tri, vm], axis=1))


def prep_shared(pre_g, w_in, b_f, w_a2, b_a, gla_norm_g, conv_w, conv_b, w_r, b_r, w_i, b_i, lru_lambda, w_out, post_g):
    f = lambda a: np.ascontiguousarray(np.asarray(a, dtype=np.float32))
    pp = lambda a, n: f(a).reshape(NL, n, 128).transpose(2, 0, 1).reshape(128, NL * n)
    vec = np.zeros((128, NV), np.float32)
    vec[:, V_PREG:V_PREG + 16] = pp(pre_g, 8)
    vec[:, V_NBA:V_NBA + 4] = pp(b_a, 2)
    vec[:, V_GNG:V_GNG + 8] = pp(gla_norm_g, 4)
    cw = f(conv_w).reshape(NL, 4, 8, 128).transpose(3, 0, 2, 1).reshape(128, NL * 8 * 4)
    vec[:, V_CW:V_CW + 64] = cw
    vec[:, V_CB:V_CB + 16] = pp(conv_b, 8)
    vec[:, V_BR:V_BR + 16] = pp(b_r, 8)
    vec[:, V_BI:V_BI + 16] = pp(b_i, 8)
    vec[:, V_LAM:V_LAM + 16] = pp(lru_lambda, 8)
    shared = {
        "w_in": f(w_in), "w_out": f(w_out), "vec128": vec,
        "bf_in": f(np.asarray(b_f).T),
        "wa2_in": f(np.asarray(w_a2).transpose(1, 0, 2).reshape(16, NL * 256)),
        "wr_in": f(np.asarray(w_r).transpose(2, 0, 1, 3).reshape(128, NL * 8 * 128)),
        "wi_in": f(np.asarray(w_i).transpose(2, 0, 1, 3).reshape(128, NL * 8 * 128)),
        "pg_in": f(np.broadcast_to(np.asarray(post_g)[:, None, :], (NL, 128, D))),
        "cst_in": make_consts(),
    }
    return shared


def kernel(x, meta, pre_g, w_in, b_f, w_a2, b_a, gla_norm_g, conv_w, conv_b,
           w_r, b_r, w_i, b_i, lru_lambda, w_out, post_g):
    x = np.asarray(x, dtype=np.float32)
    B, S, _ = x.shape
    NT = S // 128 + 1
    shared = prep_shared(pre_g, w_in, b_f, w_a2, b_a, gla_norm_g, conv_w, conv_b, w_r, b_r, w_i, b_i, lru_lambda, w_out, post_g)
    head = np.concatenate([np.zeros((112, D), np.float32), np.asarray(meta, dtype=np.float32)], axis=0)
    in_maps = []
    for c in range(8):
        b = c % B
        m = dict(shared)
        m["h0"] = np.ascontiguousarray(np.concatenate([head, x[b]], axis=0))
        in_maps.append(m)
    nc = build_nc(NT)
    res = run_bass_kernel_spmd(nc, in_maps, core_ids=list(range(8)))
    return np.stack([res.results[b]["out"] for b in range(B)], axis=0).astype(np.float32)
```

```python
from contextlib import ExitStack
import os
import numpy as np
import concourse.bass as bass
import concourse.mybir as mybir
from concourse.bass_utils import run_bass_kernel_spmd

F32 = mybir.dt.float32
BF16 = mybir.dt.bfloat16
AF = mybir.ActivationFunctionType
ALU = mybir.AluOpType

D = 1024
DIN = 5656
DMIX = 2048
NL = 2
EPS = 1e-6
C_FQ, C_FK, C_FV, C_FF, C_FG = 0, 512, 1024, 1536, 1544
C_GQ, C_GK, C_GV, C_GA, C_GG = 2056, 2312, 2568, 3080, 3096
C_LX, C_LG = 3608, 4632
NDS = 24
LV = int(os.environ.get('DBG_P3', '99'))

V_PREG = 0
V_NBA = V_PREG + 16
V_GNG = V_NBA + 4
V_CW = V_GNG + 8
V_CB = V_CW + 64
V_BR = V_CB + 16
V_BI = V_BR + 16
V_LAM = V_BI + 16
NV = V_LAM + 16


class _Rec:
    def __getattr__(self, name):
        def f(*a, **k):
            return (name, a, k)
        return f


_REC = _Rec()


class TR:
    def __init__(self, nc, es):
        self.nc = nc
        self.engs = ['pe', 'act', 'dve', 'pool', 'sp']
        self.q = {e: [] for e in self.engs}
        self.sem = {}
        self.cnt = {}
        for j, e in enumerate(self.engs):
            self.sem[e] = nc.monotonic_semaphore(j).sem()
            self.cnt[e] = 0
        for i in range(NDS):
            n = 'd%d' % i
            self.sem[n] = nc.monotonic_semaphore(len(self.engs) + i).sem()
            self.cnt[n] = 0
        self.dnext = 0
        self.waited = {e: {} for e in self.engs}
        self.W = {}
        self.R = {}
        self.G = {}

    def _deps(self, reads, writes):
        d = {}

        def add(m):
            for k, v in m.items():
                if d.get(k, 0) < v:
                    d[k] = v
        for k in reads:
            add(self.W.get(k, {}))
        for k in writes:
            if self.R.get(k):
                g = dict(self.R[k])
                for kk, vv in self.W.get(k, {}).items():
                    if g.get(kk, 0) < vv:
                        g[kk] = vv
                self.G[k] = g
                self.W[k] = {}
                self.R[k] = {}
            add(self.G.get(k, {}))
        return d

    def _commit(self, reads, writes, ev):
        sem, val = ev
        for k in writes:
            self.W.setdefault(k, {})[sem] = val
        for k in reads:
            self.R.setdefault(k, {})[sem] = val

    def _filter(self, eng, d):
        waits = []
        for sem, val in d.items():
            if sem == eng and eng == 'pe':
                continue
            if self.waited[eng].get(sem, 0) >= val:
                continue
            self.waited[eng][sem] = val
            waits.append((sem, val))
        return waits

    def op(self, eng, fn, reads=(), writes=()):
        fn = fn(_REC)
        d = self._deps(reads, writes)
        waits = self._filter(eng, d)
        self.cnt[eng] += 1
        ev = (eng, self.cnt[eng])
        self.q[eng].append((waits, fn, ev, 1))
        self._commit(reads, writes, ev)

    def dma(self, eng, out, in_, reads=(), writes=()):
        d = self._deps(reads, writes)
        ds = 'd%d' % self.dnext
        self.dnext = (self.dnext + 1) % NDS
        if self.cnt[ds] > 0:
            d[ds] = max(d.get(ds, 0), self.cnt[ds])
        waits = self._filter(eng, d)
        self.cnt[ds] += 16
        ev = (ds, self.cnt[ds])
        self.q[eng].append((waits, ('dma_start', (), dict(out=out, in_=in_)), ev, 16))
        self._commit(reads, writes, ev)

    def barrier(self):
        snap = dict(self.cnt)
        for e in self.engs:
            d = {k: v for k, v in snap.items() if v > 0 and k != e}
            waits = self._filter(e, d)
            self.cnt[e] += 1
            self.q[e].append((waits, ('nop', (), {}), (e, self.cnt[e]), 1))
        snap = dict(self.cnt)
        for e in self.engs:
            d = {k: snap[k] for k in self.engs if k != e}
            waits = self._filter(e, d)
            self.cnt[e] += 1
            self.q[e].append((waits, ('nop', (), {}), (e, self.cnt[e]), 1))
        self.W = {}
        self.R = {}
        self.G = {}

    def replay(self, eng, e):
        for waits, fn, (s, v), inc in self.q[eng]:
            for ws, wv in waits:
                e.wait_ge(self.sem[ws], wv)
            getattr(e, fn[0])(*fn[1], **fn[2]).then_inc(self.sem[s], inc)


class Arena:
    def __init__(self, handle, nwords):
        self.h = handle
        self.n = nwords
        self.base = 0
        self.top = 0

    def persist(self):
        self.base = self.top

    def reset(self):
        self.top = self.base

    def alloc(self, free_elems, dtype, parts=128):
        words = (free_elems * (2 if dtype == BF16 else 4) + 3) // 4
        words = (words + 7) // 8 * 8
        a = self.top
        self.top += words
        assert self.top <= self.n, ("SBUF arena overflow", self.top, self.n)
        v = self.h[:, a:a + words]
        if dtype == BF16:
            v = v.bitcast(BF16)
        return v[0:parts, 0:free_elems]


def build_nc(NT, phases=('P1', 'P2', 'P3', 'P4', 'P5'), layers=(0, 1), debug_out=False):
    L = NT * 128
    nc = bass.Bass("TRN2", target_bir_lowering=False, monotonic_sem_count=NDS + 8)
    dt = lambda n, s, d, k="Internal": nc.dram_tensor(n, s, d, kind=k).ap()
    h0 = dt("h0", [L, D], F32, "ExternalInput")
    w_in = dt("w_in", [NL, D, DIN], F32, "ExternalInput")
    w_out = dt("w_out", [NL, DMIX, D], F32, "ExternalInput")
    vec128 = dt("vec128", [128, NV], F32, "ExternalInput")
    bf_in = dt("bf_in", [8, NL], F32, "ExternalInput")
    wa2_in = dt("wa2_in", [16, NL * 256], F32, "ExternalInput")
    wr_in = dt("wr_in", [128, NL * 8 * 128], F32, "ExternalInput")
    wi_in = dt("wi_in", [128, NL * 8 * 128], F32, "ExternalInput")
    pg_in = dt("pg_in", [NL, 128, D], F32, "ExternalInput")
    cst_in = dt("cst_in", [128, 3 * 128], F32, "ExternalInput")
    out = dt("out", [L - 128, D], F32, "ExternalOutput")
    kind_s = "ExternalOutput" if debug_out else "Internal"
    qa = dt("qa", [8, 68, L], BF16, kind_s)
    ka = dt("ka", [8, 68, L], BF16, kind_s)
    va = dt("va", [8, L, 65], BF16, kind_s)
    sgf = dt("sgf", [512, L], BF16, kind_s)
    gqT = dt("gqT", [256, L], F32, kind_s)
    gkT = dt("gkT", [256, L], F32, kind_s)
    gaT = dt("gaT", [16, L], BF16, kind_s)
    gv = dt("gv", [L, 512], BF16, kind_s)
    sgg = dt("sgg", [512, L], BF16, kind_s)
    lxT = dt("lxT", [1024, L], F32, kind_s)
    slg = dt("slg", [1024, L], BF16, kind_s)
    yT = dt("yT", [DMIX, L], BF16, kind_s)
    h1 = dt("h1", [L, D], F32, kind_s)

    AW = 51 * 1024 + 512
    arena_h = nc.alloc_sbuf_tensor("arena", [128, AW], F32)
    A = Arena(arena_h, AW)
    psF = [nc.alloc_psum_tensor("psf%d" % i, [128, 512], F32)[:, :] for i in range(6)]
    psB = [nc.alloc_psum_tensor("psb%d" % i, [128, 1024], BF16)[:, :] for i in range(2)]

    assert (NT - 1) % 4 == 0
    STS = [[0]] + [list(range(1 + 4 * j, 5 + 4 * j)) for j in range((NT - 1) // 4)]

    with ExitStack() as es:
        T = TR(nc, es)
        op, dma = T.op, T.dma

        vec = A.alloc(NV, F32)
        cst = A.alloc(384, F32)
        ident = A.alloc(128, BF16)
        tri4 = A.alloc(512, BF16)
        vm0 = A.alloc(128, F32)
        ones_bf = A.alloc(512, BF16)
        ones_f = A.alloc(512, F32)
        onesN = A.alloc(128, BF16)
        nbf = A.alloc(NL, F32, 8)
        wa2 = A.alloc(NL * 256, BF16, 16)
        sp8 = A.alloc(16, F32)
        sp16 = A.alloc(16, F32)
        nba = A.alloc(4, F32)
        zeros_bf = A.alloc(8 * 65, BF16)
        tmpc = A.alloc(NL * 256, F32)
        hsp8 = A.alloc(16, F32)
        hsp16 = A.alloc(16, F32)
        hbr = A.alloc(16, F32)
        hbi = A.alloc(16, F32)
        A.persist()

        dma('sp', vec, vec128, writes=['vec'])
        dma('sp', cst, cst_in, writes=['cst'])
        dma('sp', tmpc[0:8, 0:NL], bf_in, writes=['tmpbf'])
        op('dve', lambda e: e.tensor_scalar(out=nbf, in0=tmpc[0:8, 0:NL], scalar1=-1.0, scalar2=None, op0=ALU.mult),
           reads=['tmpbf'], writes=['nbf'])
        op('dve', lambda e: e.tensor_copy(out=ident, in_=cst[:, 0:128]), reads=['cst'], writes=['ident'])
        for r in range(4):
            op('dve', lambda e, r=r: e.tensor_copy(out=tri4[:, r * 128:(r + 1) * 128], in_=cst[:, 128:256]),
               reads=['cst'], writes=['tri4'])
        op('dve', lambda e: e.tensor_copy(out=vm0, in_=cst[:, 256:384]), reads=['cst'], writes=['vm0'])
        op('dve', lambda e: e.memset(ones_bf, 1.0), writes=['ones'])
        op('dve', lambda e: e.memset(ones_f, 1.0), writes=['ones'])
        op('dve', lambda e: e.memset(onesN, 1.0 / 128), writes=['ones'])
        op('dve', lambda e: e.memset(zeros_bf, 0.0), writes=['ones'])
        op('dve', lambda e: e.tensor_scalar(out=nba, in0=vec[:, V_NBA:V_NBA + 4], scalar1=-1.0, scalar2=None, op0=ALU.mult),
           reads=['vec'], writes=['nba'])
        op('act', lambda e: e.activation(out=sp8, in_=vec[:, V_LAM:V_LAM + 16], func=AF.Exp, scale=-1.0),
           reads=['vec'], writes=['sp8'])
        op('act', lambda e: e.activation(out=sp8, in_=sp8, func=AF.Ln, bias=1.0), reads=['sp8'], writes=['sp8'])
        op('dve', lambda e: e.tensor_scalar(out=sp16, in0=sp8, scalar1=-16.0, scalar2=None, op0=ALU.mult),
           reads=['sp8'], writes=['sp16'])
        op('dve', lambda e: e.tensor_scalar(out=sp8, in0=sp8, scalar1=-8.0, scalar2=None, op0=ALU.mult),
           reads=['sp8', 'sp16'], writes=['sp8'])
        op('dve', lambda e: e.tensor_scalar(out=hsp8, in0=sp8, scalar1=0.5, scalar2=None, op0=ALU.mult), reads=['sp8'], writes=['hsp'])
        op('dve', lambda e: e.tensor_scalar(out=hsp16, in0=sp16, scalar1=0.5, scalar2=None, op0=ALU.mult), reads=['sp16'], writes=['hsp'])
        op('dve', lambda e: e.tensor_scalar(out=hbr, in0=vec[:, V_BR:V_BR + 16], scalar1=0.5, scalar2=None, op0=ALU.mult), reads=['vec'], writes=['hsp'])
        op('dve', lambda e: e.tensor_scalar(out=hbi, in0=vec[:, V_BI:V_BI + 16], scalar1=0.5, scalar2=None, op0=ALU.mult), reads=['vec'], writes=['hsp'])
        dma('sp', tmpc[0:16, :], wa2_in, reads=['nbf'], writes=['tmpwa'])
        op('dve', lambda e: e.tensor_copy(out=wa2, in_=tmpc[0:16, :]), reads=['tmpwa'], writes=['wa2'])
        for h in range(8):
            for r in (65, 66, 67):
                dma('sp', qa[h, r:r + 1, :].rearrange("o (a b) -> (o a) b", a=NT), ones_bf[0:NT, 0:128],
                    reads=['ones'], writes=['qa_ones'])
            dma('sp', ka[h, 64:65, :].rearrange("o (a b) -> (o a) b", a=NT), ones_bf[0:NT, 0:128],
                reads=['ones'], writes=['ka_ones'])
        T.barrier()

        for l in layers:
            hsrc = h0 if l == 0 else h1
            def p4_units():
                NB = 3
                gbank = [(psF[3], psF[4]), (psF[5], psB[1].bitcast(F32))]
                gkey = [('psF3', 'psF4'), ('psF5', 'psB1')]
                wrb = A.alloc(8 * 128, BF16).rearrange("p (b n) -> p b n", b=8)
                wib = A.alloc(8 * 128, BF16).rearrange("p (b n) -> p b n", b=8)
                wst = p1stage[0][:, 0:512]
                lxe = [A.alloc(3 + 512, F32) for _ in range(NB)]
                sl = [A.alloc(512, BF16) for _ in range(NB)]
                xc = [A.alloc(512, F32) for _ in range(NB)] + [p1stage[0][:, 0:512], p1stage[1][:, 0:512]]
                xcb = [A.alloc(512, BF16) for _ in range(2)]
                rr = [A.alloc(512, F32) for _ in range(2)]
                ig = [A.alloc(512, F32) for _ in range(2)]
                aa = [A.alloc(512, F32) for _ in range(2)]
                a2 = rr
                hh = [A.alloc(512, F32) for _ in range(2)]
                hprev = A.alloc(8, F32)
                yl = [A.alloc(512, BF16) for _ in range(NB)]
                for hf in range(2):
                    for (wsrc, wdst, kk) in ((wr_in, wrb, 'wrb'), (wi_in, wib, 'wib')):
                        dma('sp', wst, wsrc[:, l * 1024 + hf * 512:l * 1024 + (hf + 1) * 512], reads=['stage0'], writes=['stage0'])
                        op('dve', lambda e: e.tensor_copy(out=wdst.rearrange("p b n -> p (b n)")[:, hf * 512:(hf + 1) * 512], in_=wst),
                           reads=['stage0'], writes=[kk])
                units = [(J, bl) for J in range(len(STS)) for bl in range(8)]
                yield

                def geom(u):
                    J, bl = units[u]
                    tiles = STS[J]
                    return J, bl, len(tiles) * 128, tiles[0] * 128, u % NB, u % 2, u % 5

                def stA(u, part):
                    J, bl, ntok, q0, s, s2, s5 = geom(u)
                    rows = slice(bl * 128, (bl + 1) * 128)
                    kl = 'lxe%d' % s
                    if part == 0 and J == 0:
                        op('dve', lambda e: e.memset(lxe[s][:, 0:3], 0.0), writes=[kl])
                        dma('sp', lxe[s][:, 3:3 + ntok], lxT[rows, q0:q0 + ntok], reads=[('lx', J)], writes=[kl])
                    elif part == 0:
                        dma('sp', lxe[s][:, 0:3 + ntok], lxT[rows, q0 - 3:q0 + ntok], reads=[('lx', J), ('lx', J - 1)], writes=[kl])
                    cwi = V_CW + (l * 8 + bl) * 4
                    cbi = V_CB + l * 8 + bl
                    kx = 'xc%d' % s5
                    if part == 0:
                      op('dve', lambda e: e.tensor_scalar(out=xc[s5][:, 0:ntok], in0=lxe[s][:, 3:3 + ntok], scalar1=vec[:, cwi + 3:cwi + 4],
                                                        scalar2=vec[:, cbi:cbi + 1], op0=ALU.mult, op1=ALU.add), reads=[kl, 'vec'], writes=[kx])
                    for k in ((2,) if part == 0 else (1, 0)):
                        op('dve', lambda e: e.scalar_tensor_tensor(out=xc[s5][:, 0:ntok], in0=lxe[s][:, k:k + ntok], scalar=vec[:, cwi + k:cwi + k + 1],
                                                                   in1=xc[s5][:, 0:ntok], op0=ALU.mult, op1=ALU.add), reads=[kl, 'vec', kx], writes=[kx])
                    if part == 1 and J == 0:
                        op('dve', lambda e: e.tensor_tensor(out=xc[s5][:, 0:128], in0=xc[s5][:, 0:128], in1=vm0, op=ALU.mult),
                           reads=[kx, 'vm0'], writes=[kx])

                def stA2(u):
                    J, bl, ntok, q0, s, s2, s5 = geom(u)
                    op('act', lambda e: e.activation(out=xcb[s2][:, 0:ntok], in_=xc[s5][:, 0:ntok], func=AF.Copy), reads=['xc%d' % s5], writes=['xcb%d' % s2])

                def stB(u):
                    J, bl, ntok, q0, s, s2, s5 = geom(u)
                    rps, ips = gbank[s2]
                    kr, ki = gkey[s2]
                    rows = slice(bl * 128, (bl + 1) * 128)
                    dma('sp', sl[s][:, 0:ntok], slg[rows, q0:q0 + ntok], reads=[('lx', J)], writes=['sl%d' % s])
                    op('pe', lambda e: e.matmul(rps[:, 0:ntok], lhsT=wrb[:, bl, :], rhs=xcb[s2][:, 0:ntok], start=True, stop=True),
                       reads=['wrb', 'xcb%d' % s2], writes=[kr])
                    op('pe', lambda e: e.matmul(ips[:, 0:ntok], lhsT=wib[:, bl, :], rhs=xcb[s2][:, 0:ntok], start=True, stop=True),
                       reads=['wib', 'xcb%d' % s2], writes=[ki])

                def stB2(u, part):
                    J, bl, ntok, q0, s, s2, s5 = geom(u)
                    rps, ips = gbank[s2]
                    kr, ki = gkey[s2]
                    bri = V_BR + l * 8 + bl
                    bii = V_BI + l * 8 + bl
                    li = l * 8 + bl
                    if part == 1:
                        for (o_, sc_) in ((aa[s2], hsp8), (a2[s2], hsp16)):
                            op('act', lambda e: e.activation(out=o_[:, 0:ntok], in_=rr[s2][:, 0:ntok], func=AF.Exp,
                                                             scale=sc_[:, li:li + 1], bias=sc_[:, li:li + 1]),
                               reads=['rr%d' % s2, 'hsp'], writes=['aa%d' % s2 if o_ is aa[s2] else 'rr%d' % s2])
                        op('act', lambda e: e.activation(out=a2[s2][:, 0:ntok], in_=a2[s2][:, 0:ntok], func=AF.Ln, scale=-1.0, bias=1.0),
                           reads=['rr%d' % s2], writes=['rr%d' % s2])
                        op('act', lambda e: e.activation(out=a2[s2][:, 0:ntok], in_=a2[s2][:, 0:ntok], func=AF.Exp, scale=0.5),
                           reads=['rr%d' % s2], writes=['rr%d' % s2])
                        return
                    op('act', lambda e: e.activation(out=rr[s2][:, 0:ntok], in_=rps[:, 0:ntok], func=AF.Tanh, scale=0.5, bias=hbr[:, li:li + 1]),
                       reads=[kr, 'hsp'], writes=['rr%d' % s2])
                    op('act', lambda e: e.activation(out=ig[s2][:, 0:ntok], in_=ips[:, 0:ntok], func=AF.Tanh, scale=0.5, bias=hbi[:, li:li + 1]),
                       reads=[ki, 'hsp'], writes=['ig%d' % s2])

                def stC(u):
                    J, bl, ntok, q0, s, s2, s5 = geom(u)
                    op('pool', lambda e: e.tensor_scalar(out=ig[s2][:, 0:ntok], in0=ig[s2][:, 0:ntok], scalar1=1.0, scalar2=0.5, op0=ALU.add, op1=ALU.mult),
                       reads=['ig%d' % s2], writes=['ig%d' % s2])
                    op('pool', lambda e: e.tensor_tensor(out=ig[s2][:, 0:ntok], in0=ig[s2][:, 0:ntok], in1=xc[s5][:, 0:ntok], op=ALU.mult),
                       reads=['ig%d' % s2, 'xc%d' % s5], writes=['ig%d' % s2])
                    op('pool', lambda e: e.tensor_tensor(out=ig[s2][:, 0:ntok], in0=ig[s2][:, 0:ntok], in1=a2[s2][:, 0:ntok], op=ALU.mult),
                       reads=['ig%d' % s2, 'rr%d' % s2], writes=['ig%d' % s2])
                    init = 0.0 if J == 0 else hprev[:, bl:bl + 1]
                    op('dve', lambda e: e.tensor_tensor_scan(out=hh[s2][:, 0:ntok], data0=aa[s2][:, 0:ntok], data1=ig[s2][:, 0:ntok],
                                                             initial=init, op0=ALU.mult, op1=ALU.add),
                       reads=['aa%d' % s2, 'ig%d' % s2, 'hprev'], writes=['hh%d' % s2])
                    op('dve', lambda e: e.tensor_copy(out=hprev[:, bl:bl + 1], in_=hh[s2][:, ntok - 1:ntok]), reads=['hh%d' % s2], writes=['hprev'])
                    op('dve', lambda e: e.tensor_tensor(out=yl[s][:, 0:ntok], in0=hh[s2][:, 0:ntok], in1=sl[s][:, 0:ntok], op=ALU.mult),
                       reads=['hh%d' % s2, 'sl%d' % s], writes=['yl%d' % s])

                def stD(u):
                    J, bl, ntok, q0, s, s2, s5 = geom(u)
                    dma('sp', yT[1024 + bl * 128:1024 + (bl + 1) * 128, q0:q0 + ntok], yl[s][:, 0:ntok], reads=['yl%d' % s], writes=['scr'])

                n = len(units)
                for t in range(n + 5):
                    pieces = []
                    if 0 <= t - 5 < n:
                        pieces.append((stD, (t - 5,)))
                    if 0 <= t - 4 < n:
                        pieces.append((stC, (t - 4,)))
                    if 0 <= t - 3 < n:
                        pieces.append((stB2, (t - 3, 0)))
                        pieces.append((stB2, (t - 3, 1)))
                    if 0 <= t - 2 < n:
                        pieces.append((stB, (t - 2,)))
                    if 0 <= t - 1 < n:
                        pieces.append((stA2, (t - 1,)))
                    if t < n:
                        pieces.append((stA, (t, 0)))
                        pieces.append((stA, (t, 1)))
                    for i, (f_, a_) in enumerate(pieces):
                        f_(*a_)
                        p4n[0] = t if i + 1 < len(pieces) else t + 1
                        yield

            if 'P1' in phases:
                A.reset()
                W = A.alloc(8 * DIN, BF16).rearrange("p (c n) -> p c n", c=8)
                SC = 808
                stage = [A.alloc(SC, F32) for _ in range(2)]
                p1stage = stage
                hb = [A.alloc(D, F32) for _ in range(2)]
                junk = A.alloc(D, BF16)
                ssq = [A.alloc(1, F32) for _ in range(3)]
                hn = [A.alloc(D, BF16) for _ in range(2)]
                hnT = [A.alloc(8 * 512, BF16).rearrange("p (c n) -> p c n", c=8) for _ in range(2)]
                evf = [A.alloc(512, F32) for _ in range(4)]
                evb = [A.alloc(512, BF16) for _ in range(4)]
                vt = [A.alloc(8 * 65, BF16).rearrange("p (h c) -> p h c", h=8) for _ in range(2)]
                gvt = [A.alloc(512, BF16) for _ in range(2)]
                ffs = A.alloc(512, F32, 8)
                spf = ffs
                g4 = p4_units()
                p4n = [0]
                gcount = [0]

                def p4adv(k, lim):
                    for _ in range(k):
                        if p4n[0] < lim:
                            next(g4)
                pstores = []

                def sdma(dst, src_, reads=(), writes=()):
                    pstores.append((dst, src_, reads, writes))

                def sflush(keep):
                    while len(pstores) > keep:
                        d_, s_, r_, w_ = pstores.pop(0)
                        dma('sp', d_, s_, reads=r_, writes=w_)
                cc = A.alloc(512, F32, 8)
                cprev = A.alloc(1, F32, 8)
                cq = A.alloc(512, BF16, 8)
                kp = A.alloc(3 * 512, BF16, 8).rearrange("p (r n) -> p r n", r=3)
                r1 = A.alloc(512, F32, 8)
                gab = A.alloc(512, BF16, 16)
                si = 0
                for c in range(8):
                    for s0 in range(0, DIN, SC):
                        st = stage[si % 2]
                        k = 'stage%d' % (si % 2)
                        si += 1
                        dma('sp', st, w_in[l, c * 128:(c + 1) * 128, s0:s0 + SC], writes=[k])
                        op('dve', lambda e, st=st, c=c, s0=s0: e.tensor_scalar(
                            out=W[:, c, s0:s0 + SC], in0=st, scalar1=vec[:, V_PREG + l * 8 + c:V_PREG + l * 8 + c + 1],
                            scalar2=None, op0=ALU.mult), reads=[k, 'vec'], writes=['W'])
                for hh in range(2):
                    op('dve', lambda e, hh=hh: e.memset(vt[hh][:, :, 64:65], 1.0), writes=['vt%d' % hh])
                psi = 0
                evi = 0
                tcount = 0
                pinfo = {}

                def prep_a(Jp, ti):
                    nonlocal tcount
                    t = STS[Jp][ti]
                    b = hb[tcount % 2]
                    kb = 'hb%d' % (tcount % 2)
                    sq = ssq[tcount % 3]
                    ksq = 'ssq%d' % (tcount % 3)
                    n_ = hn[tcount % 2]
                    kn = 'hn%d' % (tcount % 2)
                    pinfo[(Jp, ti)] = (n_, kn)
                    tcount += 1
                    dma('sp', b, hsrc[t * 128:(t + 1) * 128, :], writes=[kb])
                    op('act', lambda e: e.activation(out=junk, in_=b, func=AF.Square, accum_out=sq), reads=[kb], writes=['junk', ksq])
                    op('act', lambda e: e.activation(out=sq, in_=sq, func=AF.Ln, scale=1.0 / D, bias=EPS), reads=[ksq], writes=[ksq])
                    op('act', lambda e: e.activation(out=sq, in_=sq, func=AF.Exp, scale=-0.5), reads=[ksq], writes=[ksq])
                    op('dve', lambda e: e.tensor_scalar(out=n_, in0=b, scalar1=sq, scalar2=None, op0=ALU.mult), reads=[kb, ksq], writes=[kn])

                def prep_b(Jp, ti):
                    n_, kn = pinfo.pop((Jp, ti))
                    X = hnT[Jp % 2]
                    kX = 'hnT%d' % (Jp % 2)
                    pT = psB[0]
                    kpT = 'psB0'
                    for c in range(8):
                        op('pe', lambda e: e.transpose(out=pT[:, c * 128:(c + 1) * 128], in_=n_[:, c * 128:(c + 1) * 128], identity=ident),
                           reads=[kn, 'ident'], writes=[kpT])
                    if ti % 2 == 0:
                        op('act', lambda e: e.activation(out=X[:, :, ti * 128:(ti + 1) * 128], in_=pT.rearrange("p (c n) -> p c n", c=8), func=AF.Copy),
                           reads=[kpT], writes=[kX])
                    else:
                        op('dve', lambda e: e.tensor_copy(out=X[:, :, ti * 128:(ti + 1) * 128], in_=pT.rearrange("p (c n) -> p c n", c=8)),
                           reads=[kpT], writes=[kX])

                next(g4)
                for J, tiles in enumerate(STS):
                    ntl = len(tiles)
                    ntok = ntl * 128
                    q0 = tiles[0] * 128
                    X = hnT[J % 2]
                    kX = 'hnT%d' % (J % 2)
                    if J == 0:
                        prep_a(0, 0)
                        prep_b(0, 0)
                    gl = [0]
                    def fm_group(c0, M, evac):
                        nonlocal psi
                        ps = psF[psi % 3]
                        kps = 'psF%d' % (psi % 3)
                        psi += 1
                        gcount[0] += 1
                        gl[0] += 1
                        if J + 1 < len(STS) and gl[0] in (4, 12, 20, 28):
                            prep_a(J + 1, (gl[0] - 4) // 8)
                        if J + 1 < len(STS) and gl[0] in (8, 16, 24, 32):
                            prep_b(J + 1, (gl[0] - 8) // 8)
                        sflush(3)
                        p4adv(2 if gcount[0] % 2 == 0 else 1, 8 * J)
                        for kc in range(8):
                            op('pe', lambda e, ps=ps, kc=kc: e.matmul(ps[0:M, 0:ntok], lhsT=W[:, kc, c0:c0 + M],
                                                                     rhs=X[:, kc, 0:ntok], start=(kc == 0), stop=(kc == 7)),
                               reads=['W', kX], writes=[kps])
                        evac(ps[0:M, 0:ntok], kps)

                    def ev_copy(dst_list, scale=None, dtype=BF16, eng='dve', wkey='scr'):
                        def f(ps, kps):
                            nonlocal evi
                            M = ps.shape[0]
                            buf = (evb if dtype == BF16 else evf)[evi % 4]
                            kb_ = ('evb%d' if dtype == BF16 else 'evf%d') % (evi % 4)
                            evi += 1
                            o = buf[0:M, 0:ntok]
                            if eng == 'silu':
                                op('act', lambda e: e.activation(out=o, in_=ps, func=AF.Silu), reads=[kps], writes=[kb_])
                            elif scale is not None:
                                op('dve', lambda e: e.tensor_scalar(out=o, in0=ps, scalar1=scale, scalar2=None, op0=ALU.mult),
                                   reads=[kps], writes=[kb_])
                            else:
                                op('dve', lambda e: e.tensor_copy(out=o, in_=ps), reads=[kps], writes=[kb_])
                            for (r0, nr, dst) in dst_list:
                                sdma(dst, buf[r0:r0 + nr, 0:ntok], reads=[kb_], writes=[wkey])
                        return f

                    tk = slice(q0, q0 + ntok)
                    def ev_ff(ps, kps):
                        op('dve', lambda e: e.tensor_copy(out=ffs[:, 0:ntok], in_=ps), reads=[kps], writes=['ffs'])
                    fm_group(C_FF, 8, ev_ff)
                    op('act', lambda e: e.activation(out=spf[:, 0:ntok], in_=ffs[:, 0:ntok], func=AF.Exp,
                                                     bias=nbf[:, l:l + 1], scale=-1.0), reads=['ffs', 'nbf'], writes=['ffs'])
                    op('act', lambda e: e.activation(out=spf[:, 0:ntok], in_=spf[:, 0:ntok], func=AF.Ln, bias=1.0),
                       reads=['ffs'], writes=['ffs'])
                    if J == 0:
                        op('dve', lambda e: e.tensor_tensor(out=spf[:, 0:128], in0=spf[:, 0:128], in1=vm0[0:8, :], op=ALU.mult),
                           reads=['ffs', 'vm0'], writes=['ffs'])
                    init = 0.0 if J == 0 else cprev
                    op('dve', lambda e, init=init: e.tensor_tensor_scan(out=cc[:, 0:ntok], data0=ones_f[0:8, 0:ntok], data1=spf[:, 0:ntok],
                                                                        initial=init, op0=ALU.mult, op1=ALU.subtract),
                       reads=['ffs', 'ones', 'cprev'], writes=['cc'])
                    op('dve', lambda e: e.tensor_copy(out=cprev, in_=cc[:, ntok - 1:ntok]), reads=['cc'], writes=['cprev'])
                    op('dve', lambda e: e.tensor_copy(out=cq[:, 0:ntok], in_=cc[:, 0:ntok]), reads=['cc'], writes=['cq'])
                    op('dve', lambda e: e.tensor_scalar(out=kp[:, 0, 0:ntok], in0=cc[:, 0:ntok], scalar1=-1.0, scalar2=None, op0=ALU.mult),
                       reads=['cc'], writes=['kp'])
                    op('dve', lambda e: e.scalar_tensor_tensor(out=r1[:, 0:ntok], in0=cc[:, 0:ntok], scalar=-1.0, in1=kp[:, 0, 0:ntok],
                                                               op0=ALU.mult, op1=ALU.subtract), reads=['cc', 'kp'], writes=['r1'])
                    op('dve', lambda e: e.tensor_copy(out=kp[:, 1, 0:ntok], in_=r1[:, 0:ntok]), reads=['r1'], writes=['kp'])
                    op('dve', lambda e: e.tensor_tensor(out=r1[:, 0:ntok], in0=r1[:, 0:ntok], in1=kp[:, 1, 0:ntok], op=ALU.subtract),
                       reads=['r1', 'kp'], writes=['r1'])
                    op('dve', lambda e: e.tensor_copy(out=kp[:, 2, 0:ntok], in_=r1[:, 0:ntok]), reads=['r1'], writes=['kp'])
                    sdma(qa[:, 64, tk], cq[:, 0:ntok], reads=['cq'], writes=['scr'])
                    sdma(ka[:, 65:68, tk], kp[:, :, 0:ntok], reads=['kp'], writes=['scr'])

                    def ev_ga(ps, kps):
                        op('dve', lambda e: e.tensor_copy(out=gab[:, 0:ntok], in_=ps), reads=[kps], writes=['gab'])
                        sdma(gaT[:, tk], gab[:, 0:ntok], reads=['gab'], writes=['scr'])
                    fm_group(C_GA, 16, ev_ga)
                    for g in range(4):
                        fm_group(C_FQ + g * 128, 128, ev_copy([(0, 64, qa[2 * g, 0:64, tk]), (64, 64, qa[2 * g + 1, 0:64, tk])], scale=0.125))
                    for g in range(4):
                        fm_group(C_FK + g * 128, 128, ev_copy([(0, 64, ka[2 * g, 0:64, tk]), (64, 64, ka[2 * g + 1, 0:64, tk])]))
                    for g in range(4):
                        fm_group(C_FG + g * 128, 128, ev_copy([(0, 128, sgf[g * 128:(g + 1) * 128, tk])], eng='silu'))
                    for g in range(2):
                        fm_group(C_GQ + g * 128, 128, ev_copy([(0, 128, gqT[g * 128:(g + 1) * 128, tk])], dtype=F32))
                    for g in range(2):
                        fm_group(C_GK + g * 128, 128, ev_copy([(0, 128, gkT[g * 128:(g + 1) * 128, tk])], dtype=F32))
                    for g in range(4):
                        fm_group(C_GG + g * 128, 128, ev_copy([(0, 128, sgg[g * 128:(g + 1) * 128, tk])], eng='silu'))
                    for g in range(8):
                        fm_group(C_LX + g * 128, 128, ev_copy([(0, 128, lxT[g * 128:(g + 1) * 128, tk])], dtype=F32, wkey=('lx', J)))
                    for g in range(8):
                        fm_group(C_LG + g * 128, 128, ev_copy([(0, 128, slg[g * 128:(g + 1) * 128, tk])], eng='silu', wkey=('lx', J)))
                    for ti, t in enumerate(tiles):
                        for which in (0, 1):
                            ps = psF[psi % 3]
                            kps = 'psF%d' % (psi % 3)
                            psi += 1
                            c0 = C_FV if which == 0 else C_GV
                            sflush(3)
                            p4adv(2, 8 * J)
                            for kc in range(8):
                                op('pe', lambda e, ps=ps, kc=kc, c0=c0, ti=ti: e.matmul(
                                    ps[:, :], lhsT=X[:, kc, ti * 128:(ti + 1) * 128], rhs=W[:, kc, c0:c0 + 512],
                                    start=(kc == 0), stop=(kc == 7)), reads=['W', kX], writes=[kps])
                            if which == 0:
                                v_ = vt[t % 2]
                                kv_ = 'vt%d' % (t % 2)
                                op('dve', lambda e, ps=ps, v_=v_: e.tensor_copy(out=v_[:, :, 0:64],
                                                                               in_=ps.rearrange("p (h c) -> p h c", h=8)),
                                   reads=[kps], writes=[kv_])
                                if t == 0:
                                    sdma(va[:, 112:128, :].rearrange("h t c -> t h c"), v_[112:128, :, :], reads=[kv_], writes=['scr'])
                                    sdma(va[:, 0:112, :].rearrange("h t c -> t h c"),
                                        zeros_bf[0:112, :].rearrange("p (h c) -> p h c", h=8), reads=['ones'], writes=['scr'])
                                else:
                                    sdma(va[:, t * 128:(t + 1) * 128, :].rearrange("h t c -> t h c"), v_, reads=[kv_], writes=['scr'])
                            else:
                                g_ = gvt[t % 2]
                                kg_ = 'gvt%d' % (t % 2)
                                op('act', lambda e, ps=ps, g_=g_: e.activation(out=g_, in_=ps, func=AF.Copy), reads=[kps], writes=[kg_])
                                sdma(gv[t * 128:(t + 1) * 128, :], g_, reads=[kg_], writes=['scr'])
                    while p4n[0] < 8 * J:
                        next(g4)
                sflush(0)
                for _ in g4:
                    pass
                T.barrier()

            if 'P2' in phases:
                A.reset()
                Qa = [A.alloc(L, BF16, 68) for _ in range(2)]
                Ka = [A.alloc(L, BF16, 68) for _ in range(2)]
                Va = [A.alloc(NT * 65, BF16).rearrange("p (n c) -> p n c", n=NT) for _ in range(2)]
                SG = [A.alloc(L, BF16, 64) for _ in range(2)]
                pt = [A.alloc(512, BF16) for _ in range(3)]
                dn = A.alloc(512, F32, 65)
                rdb = A.alloc(512, BF16, 65)
                t1 = [A.alloc(512, F32, 64) for _ in range(2)]
                yb = [A.alloc(512, BF16, 64) for _ in range(2)]
                pt.append(A.alloc(512, BF16))
                pt.append(A.alloc(512, BF16))
                bpsF = psB[1].bitcast(F32)
                gk_ = [0]
                blk_ = [0]
                for h in range(8):
                    s = h % 2
                    kQ, kK, kV, kS = 'Qa%d' % s, 'Ka%d' % s, 'Va%d' % s, 'SG%d' % s
                    dma('sp', Qa[s], qa[h], writes=[kQ])
                    dma('sp', Ka[s], ka[h], writes=[kK])
                    for n0 in range(0, NT, 8):
                        n1 = min(NT, n0 + 8)
                        dma('sp', Va[s][:, n0:n1, :], va[h, n0 * 128:n1 * 128, :].rearrange("(n p) c -> p n c", p=128), writes=[kV])
                    dma('sp', SG[s], sgf[h * 64:(h + 1) * 64, :], writes=[kS])
                    tasks = []
                    for J, tiles in enumerate(STS):
                        ob = blk_[0] % 2
                        blk_[0] += 1
                        for I in range(tiles[-1] + 1):
                            tasks.append((J, I, ob, gk_[0]))
                            gk_[0] += 1
                    pend = []

                    def emit_S(task):
                        J, I, ob, g = task
                        tiles = STS[J]
                        ntok = len(tiles) * 128
                        q0 = tiles[0] * 128
                        off = max(0, I - tiles[0]) * 128
                        sps = psF[g % 4]
                        ksps = 'psF%d' % (g % 4)
                        p_ = pt[g % 5]
                        kp_ = 'pt%d' % (g % 5)
                        op('pe', lambda e: e.matmul(sps[:, off:ntok], lhsT=Ka[s][:, I * 128:(I + 1) * 128],
                                                    rhs=Qa[s][:, q0 + off:q0 + ntok], start=True, stop=True),
                           reads=[kQ, kK], writes=[ksps])
                        op('act', lambda e: e.activation(out=p_[:, off:ntok], in_=sps[:, off:ntok], func=AF.Exp),
                           reads=[ksps], writes=[kp_])
                        if I >= tiles[0]:
                            op('dve', lambda e: e.tensor_tensor(out=p_[:, off:off + 128], in0=p_[:, off:off + 128],
                                                                 in1=tri4[:, 0:128], op=ALU.mult),
                               reads=[kp_, 'tri4'], writes=[kp_])

                    def emit_PV(task):
                        J, I, ob, g = task
                        tiles = STS[J]
                        ntok = len(tiles) * 128
                        q0 = tiles[0] * 128
                        last = tiles[-1]
                        off = max(0, I - tiles[0]) * 128
                        ops_ = psF[4 + ob]
                        kops = 'psF%d' % (4 + ob)
                        p_ = pt[g % 5]
                        kp_ = 'pt%d' % (g % 5)
                        op('pe', lambda e: e.matmul(ops_[0:65, off:ntok], lhsT=Va[s][:, I, :], rhs=p_[:, off:ntok],
                                                    start=(I == 0), stop=(I == last)), reads=[kV, kp_], writes=[kops])
                        if I != last:
                            return
                        while pend:
                            pend.pop(0)[1]()
                        op('dve', lambda e: e.tensor_scalar(out=dn[64:65, 0:ntok], in0=ops_[64:65, 0:ntok], scalar1=1e-30,
                                                            scalar2=None, op0=ALU.max), reads=[kops], writes=['dn'])
                        op('dve', lambda e: e.reciprocal(out=dn[64:65, 0:ntok], in_=dn[64:65, 0:ntok]), reads=['dn'], writes=['dn'])
                        op('dve', lambda e: e.tensor_copy(out=rdb[64:65, 0:ntok], in_=dn[64:65, 0:ntok]), reads=['dn'], writes=['rdb'])

                        def partB():
                            bps = bpsF
                            op('pe', lambda e: e.matmul(bps[0:64, 0:ntok], lhsT=ones_bf[64:65, 0:64], rhs=rdb[64:65, 0:ntok],
                                                        start=True, stop=True), reads=['rdb', 'ones'], writes=['bpsF'])
                            t_ = t1[ob]
                            kt_ = 't1%d' % ob
                            y_ = yb[ob]
                            ky_ = 'yb%d' % ob
                            op('dve', lambda e: e.tensor_tensor(out=t_[:, 0:ntok], in0=ops_[0:64, 0:ntok], in1=SG[s][:, q0:q0 + ntok],
                                                                op=ALU.mult), reads=[kops, kS], writes=[kt_])
                            op('dve', lambda e: e.tensor_tensor(out=y_[:, 0:ntok], in0=t_[:, 0:ntok], in1=bps[0:64, 0:ntok],
                                                                op=ALU.mult), reads=[kt_, 'bpsF'], writes=[ky_])
                            dma('act', yT[h * 64:(h + 1) * 64, q0:q0 + ntok], y_[:, 0:ntok], reads=[ky_], writes=['scr'])
                        pend.append([3, partB])

                    DPT = 3
                    for k in range(len(tasks) + DPT):
                        if k < len(tasks):
                            emit_S(tasks[k])
                        if k - DPT >= 0:
                            for pb in pend:
                                pb[0] -= 1
                            while pend and pend[0][0] <= 0:
                                pend.pop(0)[1]()
                            emit_PV(tasks[k - DPT])
                    while pend:
                        pend.pop(0)[1]()
                T.barrier()

            def p3_units():
                r3 = lambda n, dt_: [A.alloc(n, dt_) for _ in range(3)]
                r2 = lambda n, dt_: [A.alloc(n, dt_) for _ in range(2)]
                v3 = lambda lst, c: [x.rearrange("p (c n) -> p c n", c=c) for x in lst]
                gq = v3(r3(1024, F32), 2)
                gk = v3(r3(1024, F32), 2)
                ga_ = [A.alloc(512, BF16, 16) for _ in range(3)]
                gvs = v3(r3(2048, BF16), 4)
                sg_ = v3(r3(2048, BF16), 4)
                ee = v3([A.alloc(1024, F32)] * 2, 2)
                bs = v3([A.alloc(1024, F32)] * 2, 2)
                ebh = v3([A.alloc(1024, F32)] * 2, 2)
                ebq = v3([A.alloc(1024, F32)] * 2, 2)
                ebk = v3([A.alloc(1024, F32)] * 2, 2)
                qz = [v3([A.alloc(1024, BF16) for _ in range(2)], 2) for _ in range(3)]
                kt = v3(r3(1024, BF16), 2)
                khT = v3(r3(1024, BF16), 2)
                nbl = v3(r3(8, F32), 2)
                dec = v3(r3(8, F32), 2)
                yg = v3(r2(2048, BF16), 4)
                at = r2(512, BF16)
                kh = r2(256, BF16)
                sqb = r2(512, BF16)
                rs = r2(512, F32)
                t1g = r2(512, F32)
                S = A.alloc(2 * 128, F32).rearrange("p (c n) -> p c n", c=2)
                Sbf = A.alloc(2 * 128, BF16).rearrange("p (c n) -> p c n", c=2)
                op('dve', lambda e: e.memset(S, 0.0), writes=['S'])
                op('dve', lambda e: e.memset(Sbf, 0.0), writes=['Sbf'])
                for j in range(3):
                    for p in range(2):
                        op('dve', lambda e: e.memset(qz[j][p], 0.0), writes=['qz%d' % j])
                aps, opsb, ups = psF[2], [psF[3], psF[4]], psF[5]
                xm = psB[1].bitcast(F32)
                tps = psB[0]
                flat = [(J, tt) for J, tiles in enumerate(STS) for tt in range(len(tiles))]
                v0 = {}
                for v, (J, tt) in enumerate(flat):
                    v0.setdefault(J, v)

                def geo(J):
                    tiles = STS[J]
                    return len(tiles), len(tiles) * 128, slice(tiles[0] * 128, (tiles[-1] + 1) * 128), J % 3, J % 2

                def PA(J):
                    ntl, ntok, tk, j3, j2 = geo(J)
                    dma('sp', gq[j3][:, :, 0:ntok], gqT[:, tk].rearrange("(c p) t -> p c t", p=128), writes=['gq%d' % j3])
                    dma('sp', gk[j3][:, :, 0:ntok], gkT[:, tk].rearrange("(c p) t -> p c t", p=128), writes=['gk%d' % j3])
                    dma('sp', ga_[j3][:, 0:ntok], gaT[:, tk], writes=['ga%d' % j3])
                    dma('sp', gvs[j3][:, 0:ntl, :], gv[tk, :].rearrange("(t p) n -> p t n", p=128), writes=['gvs%d' % j3])
                    dma('sp', sg_[j3][:, :, 0:ntok], sgg[:, tk].rearrange("(c p) t -> p c t", p=128), writes=['sg%d' % j3])
                    for c in range(2):
                        op('pe', lambda e: e.matmul(xm[:, 0:ntok], lhsT=wa2[:, l * 256 + c * 128:l * 256 + (c + 1) * 128],
                                                    rhs=ga_[j3][:, 0:ntok], start=True, stop=True), reads=['wa2', 'ga%d' % j3], writes=['xm'])
                        op('act', lambda e: e.activation(out=ee[j2][:, c, 0:ntok], in_=xm[:, 0:ntok], func=AF.Exp,
                                                         bias=nba[:, l * 2 + c:l * 2 + c + 1], scale=-1.0), reads=['xm', 'nba'], writes=['ee'])
                        op('act', lambda e: e.activation(out=ee[j2][:, c, 0:ntok], in_=ee[j2][:, c, 0:ntok], func=AF.Ln, bias=1.0),
                           reads=['ee'], writes=['ee'])

                def PB(J):
                    ntl, ntok, tk, j3, j2 = geo(J)
                    for c in range(2):
                        for tt in range(ntl):
                            r = slice(tt * 128, (tt + 1) * 128)
                            op('dve', lambda e: e.tensor_tensor_scan(out=bs[j2][:, c, r], data0=ones_f[:, 0:128], data1=ee[j2][:, c, r],
                                                                     initial=0.0, op0=ALU.mult, op1=ALU.add),
                               reads=['ee', 'ones'], writes=['bs'])
                            op('dve', lambda e: e.tensor_scalar(out=nbl[j3][:, c, tt:tt + 1], in0=bs[j2][:, c, tt * 128 + 127:tt * 128 + 128],
                                                                scalar1=-1.0 / 16, scalar2=None, op0=ALU.mult),
                               reads=['bs'], writes=['nbl%d' % j3])

                def PC(J):
                    ntl, ntok, tk, j3, j2 = geo(J)
                    for c in range(2):
                        for tt in range(ntl):
                            r = slice(tt * 128, (tt + 1) * 128)
                            op('act', lambda e: e.activation(out=ebh[j2][:, c, r], in_=bs[j2][:, c, r], func=AF.Exp,
                                                             bias=nbl[j3][:, c, tt:tt + 1], scale=1.0 / 16),
                               reads=['bs', 'nbl%d' % j3], writes=['ebh'])
                        op('act', lambda e: e.activation(out=dec[j3][:, c, 0:ntl], in_=nbl[j3][:, c, 0:ntl], func=AF.Exp),
                           reads=['nbl%d' % j3], writes=['dec%d' % j3])
                        op('act', lambda e: e.activation(out=ebq[j2][:, c, 0:ntok], in_=bs[j2][:, c, 0:ntok], func=AF.Exp, scale=-1.0 / 16),
                           reads=['bs'], writes=['ebq'])
                        op('act', lambda e: e.activation(out=ebk[j2][:, c, 0:ntok], in_=bs[j2][:, c, 0:ntok], func=AF.Exp, scale=1.0 / 16),
                           reads=['bs'], writes=['ebk'])

                def PD(J):
                    ntl, ntok, tk, j3, j2 = geo(J)
                    for c in range(2):
                        op('dve', lambda e: e.tensor_tensor(out=khT[j3][:, c, 0:ntok], in0=gk[j3][:, c, 0:ntok], in1=ebh[j2][:, c, 0:ntok], op=ALU.mult),
                           reads=['gk%d' % j3, 'ebh'], writes=['khT%d' % j3])
                        for p in range(2):
                            pr = slice(p * 64, (p + 1) * 64)
                            op('dve', lambda e: e.scalar_tensor_tensor(out=qz[j3][p][pr, c, 0:ntok], in0=gq[j3][pr, c, 0:ntok], scalar=0.125,
                                                                       in1=ebq[j2][pr, c, 0:ntok], op0=ALU.mult, op1=ALU.mult),
                               reads=['gq%d' % j3, 'ebq'], writes=['qz%d' % j3])
                        op('dve', lambda e: e.tensor_tensor(out=kt[j3][:, c, 0:ntok], in0=gk[j3][:, c, 0:ntok], in1=ebk[j2][:, c, 0:ntok], op=ALU.mult),
                           reads=['gk%d' % j3, 'ebk'], writes=['kt%d' % j3])

                def TA(v):
                    J, tt = flat[v]
                    j3 = J % 3
                    u = v % 2
                    r = slice(tt * 128, (tt + 1) * 128)
                    for hd in range(4):
                        c, p = hd // 2, hd % 2
                        op('pe', lambda e: e.matmul(aps[:, hd * 128:(hd + 1) * 128], lhsT=kt[j3][:, c, r], rhs=qz[j3][p][:, c, r],
                                                    start=True, stop=True), reads=['kt%d' % j3, 'qz%d' % j3], writes=['aps'])
                    op('dve', lambda e: e.tensor_tensor(out=at[u], in0=aps[:, :], in1=tri4, op=ALU.mult), reads=['aps', 'tri4'], writes=['at%d' % u])
                    for c in range(2):
                        op('pe', lambda e: e.transpose(out=tps[:, c * 128:(c + 1) * 128], in_=khT[j3][:, c, r], identity=ident),
                           reads=['khT%d' % j3, 'ident'], writes=['tps'])
                    op('act', lambda e: e.activation(out=kh[u], in_=tps[:, 0:256], func=AF.Copy), reads=['tps'], writes=['kh%d' % u])

                def TB(v):
                    J, tt = flat[v]
                    j3 = J % 3
                    u = v % 2
                    r = slice(tt * 128, (tt + 1) * 128)
                    ops_ = opsb[u]
                    ko = 'ops%d' % u
                    for hd in range(4):
                        c, p = hd // 2, hd % 2
                        op('pe', lambda e: e.matmul(ops_[:, hd * 128:(hd + 1) * 128], lhsT=gvs[j3][:, tt, hd * 128:(hd + 1) * 128],
                                                    rhs=at[u][:, hd * 128:(hd + 1) * 128], start=True, stop=False),
                           reads=['gvs%d' % j3, 'at%d' % u], writes=[ko])
                        op('pe', lambda e: e.matmul(ops_[:, hd * 128:(hd + 1) * 128], lhsT=Sbf[:, c, :], rhs=qz[j3][p][:, c, r],
                                                    start=False, stop=True), reads=['Sbf', 'qz%d' % j3], writes=[ko])
                    for c in range(2):
                        op('pe', lambda e: e.matmul(ups[:, c * 256:(c + 1) * 256], lhsT=kh[u][:, c * 128:(c + 1) * 128],
                                                    rhs=gvs[j3][:, tt, c * 256:(c + 1) * 256], start=True, stop=True),
                           reads=['kh%d' % u, 'gvs%d' % j3], writes=['ups'])
                    for hd in range(4):
                        c, p = hd // 2, hd % 2
                        pr = slice(p * 64, (p + 1) * 64)
                        op('dve', lambda e: e.scalar_tensor_tensor(out=S[pr, c, :], in0=S[pr, c, :], scalar=dec[j3][pr, c, tt:tt + 1],
                                                                   in1=ups[pr, c * 256 + p * 128:c * 256 + (p + 1) * 128],
                                                                   op0=ALU.mult, op1=ALU.add), reads=['S', 'dec%d' % j3, 'ups'], writes=['S'])
                    op('act', lambda e: e.activation(out=Sbf, in_=S, func=AF.Copy), reads=['S'], writes=['Sbf'])
                    op('act', lambda e: e.activation(out=sqb[u], in_=ops_[:, :], func=AF.Square), reads=[ko], writes=['sqb%d' % u])

                def TC(v):
                    J, tt = flat[v]
                    ntl, ntok, tk, j3, j2 = geo(J)
                    u = v % 2
                    r = slice(tt * 128, (tt + 1) * 128)
                    ops_ = opsb[u]
                    ko = 'ops%d' % u
                    op('pe', lambda e: e.matmul(xm[:, :], lhsT=onesN, rhs=sqb[u], start=True, stop=True), reads=['sqb%d' % u, 'ones'], writes=['xm'])
                    op('act', lambda e: e.activation(out=rs[u], in_=xm[:, :], func=AF.Ln, bias=EPS), reads=['xm'], writes=['rs%d' % u])
                    op('act', lambda e: e.activation(out=rs[u], in_=rs[u], func=AF.Exp, scale=-0.5), reads=['rs%d' % u], writes=['rs%d' % u])
                    op('dve', lambda e: e.tensor_tensor(out=t1g[u], in0=ops_[:, :], in1=rs[u], op=ALU.mult), reads=[ko, 'rs%d' % u], writes=['t1g%d' % u])
                    for hd in range(4):
                        op('dve', lambda e: e.scalar_tensor_tensor(out=yg[j2][:, hd, r], in0=t1g[u][:, hd * 128:(hd + 1) * 128],
                                                                   scalar=vec[:, V_GNG + l * 4 + hd:V_GNG + l * 4 + hd + 1],
                                                                   in1=sg_[j3][:, hd, r], op0=ALU.mult, op1=ALU.mult),
                           reads=['t1g%d' % u, 'vec', 'sg%d' % j3], writes=['yg%d' % j2])

                def TD(v):
                    J, tt = flat[v]
                    ntl, ntok, tk, j3, j2 = geo(J)
                    if tt == ntl - 1:
                        dma('sp', yT[512:1024, tk].rearrange("(c p) t -> p c t", p=128), yg[j2][:, :, 0:ntok], reads=['yg%d' % j2], writes=[('ygs', J)])
                        gla_done[0] = STS[J][-1] + 1

                nJ = len(STS)
                for t in range(-7, NT + 3):
                    if 0 <= t - 1 < NT:
                        TB(t - 1)
                        yield
                    if 0 <= t - 3 < NT:
                        TD(t - 3)
                    if 0 <= t < NT:
                        TA(t)
                        yield
                    if 0 <= t - 2 < NT:
                        TC(t - 2)
                        yield
                    for J in range(nJ):
                        k = t - (v0[J] - 7)
                        if k == 0:
                            PA(J)
                        elif k == 1:
                            PB(J)
                        elif k == 2:
                            PC(J)
                        elif k == 3:
                            PD(J)
                    yield

            def p5_units():
                Wo = A.alloc(16 * D, BF16).rearrange("p (c n) -> p c n", c=16)
                wst = A.alloc(D, F32)
                pg = A.alloc(D, F32)
                yt = [A.alloc(16 * 128, BF16).rearrange("p (c n) -> p c n", c=16) for _ in range(2)]
                hb = [A.alloc(D, F32) for _ in range(2)]
                zs = [A.alloc(D, F32) for _ in range(2)]
                junk = A.alloc(D, BF16)
                ssq = [A.alloc(1, F32) for _ in range(2)]
                for c in range(16):
                    dma('sp', wst, w_out[l, c * 128:(c + 1) * 128, :], writes=['wst5'])
                    op('dve', lambda e: e.tensor_copy(out=Wo[:, c, :], in_=wst), reads=['wst5'], writes=['Wo'])
                    if c % 4 == 3:
                        yield
                dma('sp', pg, pg_in[l], writes=['pg'])
                yield
                tJ = {}
                for J, tiles in enumerate(STS):
                    for t in tiles:
                        tJ[t] = J
                for t in range(NT):
                    while gla_done[0] <= t:
                        yield
                    u = t % 2
                    dma('sp', yt[u], yT[:, t * 128:(t + 1) * 128].rearrange("(c p) t -> p c t", p=128), reads=[('ygs', tJ[t])], writes=['yt%d' % u])
                    dma('sp', hb[u], hsrc[t * 128:(t + 1) * 128, :], writes=['hb%d' % u])
                    for nh in range(2):
                        zp = psF[nh]
                        for c0 in range(0, 16, 4):
                            for c in range(c0, c0 + 4):
                                op('pe', lambda e: e.matmul(zp[:, :], lhsT=yt[u][:, c, :], rhs=Wo[:, c, nh * 512:(nh + 1) * 512],
                                                            start=(c == 0), stop=(c == 15)), reads=['yt%d' % u, 'Wo'], writes=['psF%d' % nh])
                            yield
                        op('act', lambda e: e.activation(out=zs[u][:, nh * 512:(nh + 1) * 512], in_=zp[:, :], func=AF.Copy),
                           reads=['psF%d' % nh], writes=['zs%d' % u])
                    yield
                    op('act', lambda e: e.activation(out=junk, in_=zs[u], func=AF.Square, accum_out=ssq[u]),
                       reads=['zs%d' % u], writes=['junk5', 'ssq%d' % u])
                    op('act', lambda e: e.activation(out=ssq[u], in_=ssq[u], func=AF.Ln, scale=1.0 / D, bias=EPS),
                       reads=['ssq%d' % u], writes=['ssq%d' % u])
                    op('act', lambda e: e.activation(out=ssq[u], in_=ssq[u], func=AF.Exp, scale=-0.5),
                       reads=['ssq%d' % u], writes=['ssq%d' % u])
                    yield
                    op('dve', lambda e: e.scalar_tensor_tensor(out=zs[u], in0=zs[u], scalar=ssq[u], in1=pg, op0=ALU.mult, op1=ALU.mult),
                       reads=['zs%d' % u, 'ssq%d' % u, 'pg'], writes=['zs%d' % u])
                    op('dve', lambda e: e.tensor_tensor(out=zs[u], in0=zs[u], in1=hb[u], op=ALU.add),
                       reads=['zs%d' % u, 'hb%d' % u], writes=['zs%d' % u])
                    yield
                    if l == NL - 1 or l == layers[-1]:
                        if t >= 1:
                            dma('act', out[(t - 1) * 128:t * 128, :], zs[u], reads=['zs%d' % u], writes=['out'])
                    else:
                        dma('act', h1[t * 128:(t + 1) * 128, :], zs[u], reads=['zs%d' % u], writes=['scr'])
                    yield

            if 'P3' in phases:
                A.reset()
                gla_done = [0]
                g3, g5 = p3_units(), p5_units()
                a5 = True
                for _ in g3:
                    for _k in range(3):
                        if a5:
                            try:
                                next(g5)
                            except StopIteration:
                                a5 = False
                if a5:
                    for _ in g5:
                        pass
                T.barrier()

        T.barrier()
        with nc.Block() as block:
            @block.sync
            def _(e):
                T.replay('sp', e)

            @block.tensor
            def _(e):
                T.replay('pe', e)

            @block.scalar
            def _(e):
                T.replay('act', e)

            @block.vector
            def _(e):
                T.replay('dve', e)

            @block.gpsimd
            def _(e):
                T.replay('pool', e)
    return nc


def make_consts():
    ident = np.eye(128, dtype=np.float32)
    tri = (np.arange(128)[None, :] >= np.arange(128)[:, None]).astype(np.float32)
    vm = np.broadcast_to((np.arange(128) >= 112).astype(np.float32)[None, :], (128, 128))
    return np.ascontiguousarray(np.concatenate([ident, tri, vm], axis=1))


def prep_shared(pre_g, w_in, b_f, w_a2, b_a, gla_norm_g, conv_w, conv_b, w_r, b_r, w_i, b_i, lru_lambda, w_out, post_g):
    f = lambda a: np.ascontiguousarray(np.asarray(a, dtype=np.float32))
    pp = lambda a, n: f(a).reshape(NL, n, 128).transpose(2, 0, 1).reshape(128, NL * n)
    vec = np.zeros((128, NV), np.float32)
    vec[:, V_PREG:V_PREG + 16] = pp(pre_g, 8)
    vec[:, V_NBA:V_NBA + 4] = pp(b_a, 2)
    vec[:, V_GNG:V_GNG + 8] = pp(gla_norm_g, 4)
    cw = f(conv_w).reshape(NL, 4, 8, 128).transpose(3, 0, 2, 1).reshape(128, NL * 8 * 4)
    vec[:, V_CW:V_CW + 64] = cw
    vec[:, V_CB:V_CB + 16] = pp(conv_b, 8)
    vec[:, V_BR:V_BR + 16] = pp(b_r, 8)
    vec[:, V_BI:V_BI + 16] = pp(b_i, 8)
    vec[:, V_LAM:V_LAM + 16] = pp(lru_lambda, 8)
    shared = {
        "w_in": f(w_in), "w_out": f(w_out), "vec128": vec,
        "bf_in": f(np.asarray(b_f).T),
        "wa2_in": f(np.asarray(w_a2).transpose(1, 0, 2).reshape(16, NL * 256)),
        "wr_in": f(np.asarray(w_r).transpose(2, 0, 1, 3).reshape(128, NL * 8 * 128)),
        "wi_in": f(np.asarray(w_i).transpose(2, 0, 1, 3).reshape(128, NL * 8 * 128)),
        "pg_in": f(np.broadcast_to(np.asarray(post_g)[:, None, :], (NL, 128, D))),
        "cst_in": make_consts(),
    }
    return shared


def kernel(x, meta, pre_g, w_in, b_f, w_a2, b_a, gla_norm_g, conv_w, conv_b,
           w_r, b_r, w_i, b_i, lru_lambda, w_out, post_g):
    x = np.asarray(x, dtype=np.float32)
    B, S, _ = x.shape
    NT = S // 128 + 1
    shared = prep_shared(pre_g, w_in, b_f, w_a2, b_a, gla_norm_g, conv_w, conv_b, w_r, b_r, w_i, b_i, lru_lambda, w_out, post_g)
    head = np.concatenate([np.zeros((112, D), np.float32), np.asarray(meta, dtype=np.float32)], axis=0)
    real = [0, 1, 4, 5][:B]
    zero_map = {k: np.zeros_like(v) for k, v in shared.items()}
    zero_map["h0"] = np.zeros((NT * 128, D), np.float32)
    in_maps = []
    for c in range(8):
        if c in real:
            m = dict(shared)
            m["h0"] = np.ascontiguousarray(np.concatenate([head, x[real.index(c)]], axis=0))
        else:
            m = zero_map
        in_maps.append(m)
    nc = build_nc(NT)
    res = run_bass_kernel_spmd(nc, in_maps, core_ids=list(range(8)))
    return np.stack([res.results[real[b]]["out"] for b in range(B)], axis=0).astype(np.float32)
```

```python
from contextlib import ExitStack
import os
import numpy as np
import concourse.bass as bass
import concourse.mybir as mybir
from concourse.bass_utils import run_bass_kernel_spmd

F32 = mybir.dt.float32
BF16 = mybir.dt.bfloat16
AF = mybir.ActivationFunctionType
ALU = mybir.AluOpType

D = 1024
DIN = 5656
DMIX = 2048
NL = 2
EPS = 1e-6
C_FQ, C_FK, C_FV, C_FF, C_FG = 0, 512, 1024, 1536, 1544
C_GQ, C_GK, C_GV, C_GA, C_GG = 2056, 2312, 2568, 3080, 3096
C_LX, C_LG = 3608, 4632
NDS = 24
LV = int(os.environ.get('DBG_P3', '99'))

V_PREG = 0
V_NBA = V_PREG + 16
V_GNG = V_NBA + 4
V_CW = V_GNG + 8
V_CB = V_CW + 64
V_BR = V_CB + 16
V_BI = V_BR + 16
V_LAM = V_BI + 16
NV = V_LAM + 16


class _Rec:
    def __getattr__(self, name):
        def f(*a, **k):
            return (name, a, k)
        return f


_REC = _Rec()


class TR:
    def __init__(self, nc, es):
        self.nc = nc
        self.engs = ['pe', 'act', 'dve', 'pool', 'sp']
        self.q = {e: [] for e in self.engs}
        self.sem = {}
        self.cnt = {}
        for j, e in enumerate(self.engs):
            self.sem[e] = nc.monotonic_semaphore(j).sem()
            self.cnt[e] = 0
        for i in range(NDS):
            n = 'd%d' % i
            self.sem[n] = nc.monotonic_semaphore(len(self.engs) + i).sem()
            self.cnt[n] = 0
        self.dnext = 0
        self.waited = {e: {} for e in self.engs}
        self.W = {}
        self.R = {}
        self.G = {}

    def _deps(self, reads, writes):
        d = {}

        def add(m):
            for k, v in m.items():
                if d.get(k, 0) < v:
                    d[k] = v
        for k in reads:
            add(self.W.get(k, {}))
        for k in writes:
            if self.R.get(k):
                g = dict(self.R[k])
                for kk, vv in self.W.get(k, {}).items():
                    if g.get(kk, 0) < vv:
                        g[kk] = vv
                self.G[k] = g
                self.W[k] = {}
                self.R[k] = {}
            add(self.G.get(k, {}))
        return d

    def _commit(self, reads, writes, ev):
        sem, val = ev
        for k in writes:
            self.W.setdefault(k, {})[sem] = val
        for k in reads:
            self.R.setdefault(k, {})[sem] = val

    def _filter(self, eng, d):
        waits = []
        for sem, val in d.items():
            if sem == eng and eng == 'pe':
                continue
            if self.waited[eng].get(sem, 0) >= val:
                continue
            self.waited[eng][sem] = val
            waits.append((sem, val))
        return waits

    def op(self, eng, fn, reads=(), writes=()):
        fn = fn(_REC)
        d = self._deps(reads, writes)
        waits = self._filter(eng, d)
        self.cnt[eng] += 1
        ev = (eng, self.cnt[eng])
        self.q[eng].append((waits, fn, ev, 1))
        self._commit(reads, writes, ev)

    def dma(self, eng, out, in_, reads=(), writes=()):
        d = self._deps(reads, writes)
        ds = 'd%d' % self.dnext
        self.dnext = (self.dnext + 1) % NDS
        if self.cnt[ds] > 0:
            d[ds] = max(d.get(ds, 0), self.cnt[ds])
        waits = self._filter(eng, d)
        self.cnt[ds] += 16
        ev = (ds, self.cnt[ds])
        self.q[eng].append((waits, ('dma_start', (), dict(out=out, in_=in_)), ev, 16))
        self._commit(reads, writes, ev)

    def barrier(self):
        snap = dict(self.cnt)
        for e in self.engs:
            d = {k: v for k, v in snap.items() if v > 0 and k != e}
            waits = self._filter(e, d)
            self.cnt[e] += 1
            self.q[e].append((waits, ('nop', (), {}), (e, self.cnt[e]), 1))
        snap = dict(self.cnt)
        for e in self.engs:
            d = {k: snap[k] for k in self.engs if k != e}
            waits = self._filter(e, d)
            self.cnt[e] += 1
            self.q[e].append((waits, ('nop', (), {}), (e, self.cnt[e]), 1))
        self.W = {}
        self.R = {}
        self.G = {}

    def replay(self, eng, e):
        for waits, fn, (s, v), inc in self.q[eng]:
            for ws, wv in waits:
                e.wait_ge(self.sem[ws], wv)
            getattr(e, fn[0])(*fn[1], **fn[2]).then_inc(self.sem[s], inc)


class Arena:
    def __init__(self, handle, nwords):
        self.h = handle
        self.n = nwords
        self.base = 0
        self.top = 0

    def persist(self):
        self.base = self.top

    def reset(self):
        self.top = self.base

    def alloc(self, free_elems, dtype, parts=128):
        words = (free_elems * (2 if dtype == BF16 else 4) + 3) // 4
        words = (words + 7) // 8 * 8
        a = self.top
        self.top += words
        assert self.top <= self.n, ("SBUF arena overflow", self.top, self.n)
        v = self.h[:, a:a + words]
        if dtype == BF16:
            v = v.bitcast(BF16)
        return v[0:parts, 0:free_elems]


def build_nc(NT, phases=('P1', 'P2', 'P3', 'P4', 'P5'), layers=(0, 1), debug_out=False):
    L = NT * 128
    nc = bass.Bass("TRN2", target_bir_lowering=False, monotonic_sem_count=NDS + 8)
    dt = lambda n, s, d, k="Internal": nc.dram_tensor(n, s, d, kind=k).ap()
    h0 = dt("h0", [L, D], F32, "ExternalInput")
    w_in = dt("w_in", [NL, D, DIN], F32, "ExternalInput")
    w_out = dt("w_out", [NL, DMIX, D], F32, "ExternalInput")
    vec128 = dt("vec128", [128, NV], F32, "ExternalInput")
    bf_in = dt("bf_in", [8, NL], F32, "ExternalInput")
    wa2_in = dt("wa2_in", [16, NL * 256], F32, "ExternalInput")
    wr_in = dt("wr_in", [128, NL * 8 * 128], F32, "ExternalInput")
    wi_in = dt("wi_in", [128, NL * 8 * 128], F32, "ExternalInput")
    pg_in = dt("pg_in", [NL, 128, D], F32, "ExternalInput")
    cst_in = dt("cst_in", [128, 3 * 128], F32, "ExternalInput")
    out = dt("out", [L - 128, D], F32, "ExternalOutput")
    kind_s = "ExternalOutput" if debug_out else "Internal"
    qa = dt("qa", [8, 68, L], BF16, kind_s)
    ka = dt("ka", [8, 68, L], BF16, kind_s)
    va = dt("va", [8, L, 65], BF16, kind_s)
    sgf = dt("sgf", [512, L], BF16, kind_s)
    gqT = dt("gqT", [256, L], F32, kind_s)
    gkT = dt("gkT", [256, L], F32, kind_s)
    gaT = dt("gaT", [16, L], BF16, kind_s)
    gv = dt("gv", [L, 512], BF16, kind_s)
    sgg = dt("sgg", [512, L], BF16, kind_s)
    lxT = dt("lxT", [1024, L], F32, kind_s)
    slg = dt("slg", [1024, L], BF16, kind_s)
    yT = dt("yT", [DMIX, L], BF16, kind_s)
    h1 = dt("h1", [L, D], F32, kind_s)

    AW = 51 * 1024 + 512
    arena_h = nc.alloc_sbuf_tensor("arena", [128, AW], F32)
    A = Arena(arena_h, AW)
    psF = [nc.alloc_psum_tensor("psf%d" % i, [128, 512], F32)[:, :] for i in range(6)]
    psB = [nc.alloc_psum_tensor("psb%d" % i, [128, 1024], BF16)[:, :] for i in range(2)]

    assert (NT - 1) % 4 == 0
    STS = [[0]] + [list(range(1 + 4 * j, 5 + 4 * j)) for j in range((NT - 1) // 4)]

    with ExitStack() as es:
        T = TR(nc, es)
        op, dma = T.op, T.dma

        vec = A.alloc(NV, F32)
        cst = A.alloc(384, F32)
        ident = A.alloc(128, BF16)
        tri4 = A.alloc(512, BF16)
        vm0 = A.alloc(128, F32)
        ones_bf = A.alloc(512, BF16)
        ones_f = A.alloc(512, F32)
        onesN = A.alloc(128, BF16)
        nbf = A.alloc(NL, F32, 8)
        wa2 = A.alloc(NL * 256, BF16, 16)
        sp8 = A.alloc(16, F32)
        sp16 = A.alloc(16, F32)
        nba = A.alloc(4, F32)
        zeros_bf = A.alloc(8 * 65, BF16)
        tmpc = A.alloc(NL * 256, F32)
        hsp8 = A.alloc(16, F32)
        hsp16 = A.alloc(16, F32)
        hbr = A.alloc(16, F32)
        hbi = A.alloc(16, F32)
        A.persist()

        dma('sp', vec, vec128, writes=['vec'])
        dma('sp', cst, cst_in, writes=['cst'])
        dma('sp', tmpc[0:8, 0:NL], bf_in, writes=['tmpbf'])
        op('dve', lambda e: e.tensor_scalar(out=nbf, in0=tmpc[0:8, 0:NL], scalar1=-1.0, scalar2=None, op0=ALU.mult),
           reads=['tmpbf'], writes=['nbf'])
        op('dve', lambda e: e.tensor_copy(out=ident, in_=cst[:, 0:128]), reads=['cst'], writes=['ident'])
        for r in range(4):
            op('dve', lambda e, r=r: e.tensor_copy(out=tri4[:, r * 128:(r + 1) * 128], in_=cst[:, 128:256]),
               reads=['cst'], writes=['tri4'])
        op('dve', lambda e: e.tensor_copy(out=vm0, in_=cst[:, 256:384]), reads=['cst'], writes=['vm0'])
        op('dve', lambda e: e.memset(ones_bf, 1.0), writes=['ones'])
        op('dve', lambda e: e.memset(ones_f, 1.0), writes=['ones'])
        op('dve', lambda e: e.memset(onesN, 1.0 / 128), writes=['ones'])
        op('dve', lambda e: e.memset(zeros_bf, 0.0), writes=['ones'])
        op('dve', lambda e: e.tensor_scalar(out=nba, in0=vec[:, V_NBA:V_NBA + 4], scalar1=-1.0, scalar2=None, op0=ALU.mult),
           reads=['vec'], writes=['nba'])
        op('act', lambda e: e.activation(out=sp8, in_=vec[:, V_LAM:V_LAM + 16], func=AF.Exp, scale=-1.0),
           reads=['vec'], writes=['sp8'])
        op('act', lambda e: e.activation(out=sp8, in_=sp8, func=AF.Ln, bias=1.0), reads=['sp8'], writes=['sp8'])
        op('dve', lambda e: e.tensor_scalar(out=sp16, in0=sp8, scalar1=-16.0, scalar2=None, op0=ALU.mult),
           reads=['sp8'], writes=['sp16'])
        op('dve', lambda e: e.tensor_scalar(out=sp8, in0=sp8, scalar1=-8.0, scalar2=None, op0=ALU.mult),
           reads=['sp8', 'sp16'], writes=['sp8'])
        op('dve', lambda e: e.tensor_scalar(out=hsp8, in0=sp8, scalar1=0.5, scalar2=None, op0=ALU.mult), reads=['sp8'], writes=['hsp'])
        op('dve', lambda e: e.tensor_scalar(out=hsp16, in0=sp16, scalar1=0.5, scalar2=None, op0=ALU.mult), reads=['sp16'], writes=['hsp'])
        op('dve', lambda e: e.tensor_scalar(out=hbr, in0=vec[:, V_BR:V_BR + 16], scalar1=0.5, scalar2=None, op0=ALU.mult), reads=['vec'], writes=['hsp'])
        op('dve', lambda e: e.tensor_scalar(out=hbi, in0=vec[:, V_BI:V_BI + 16], scalar1=0.5, scalar2=None, op0=ALU.mult), reads=['vec'], writes=['hsp'])
        dma('sp', tmpc[0:16, :], wa2_in, reads=['nbf'], writes=['tmpwa'])
        op('dve', lambda e: e.tensor_copy(out=wa2, in_=tmpc[0:16, :]), reads=['tmpwa'], writes=['wa2'])
        for h in range(8):
            for r in (65, 66, 67):
                dma('sp', qa[h, r:r + 1, :].rearrange("o (a b) -> (o a) b", a=NT), ones_bf[0:NT, 0:128],
                    reads=['ones'], writes=['qa_ones'])
            dma('sp', ka[h, 64:65, :].rearrange("o (a b) -> (o a) b", a=NT), ones_bf[0:NT, 0:128],
                reads=['ones'], writes=['ka_ones'])
        T.barrier()

        for l in layers:
            hsrc = h0 if l == 0 else h1
            def p4_units():
                NB = 3
                gbank = [(psF[3], psF[4]), (psF[5], psB[1].bitcast(F32))]
                gkey = [('psF3', 'psF4'), ('psF5', 'psB1')]
                wrb = A.alloc(8 * 128, BF16).rearrange("p (b n) -> p b n", b=8)
                wib = A.alloc(8 * 128, BF16).rearrange("p (b n) -> p b n", b=8)
                wst = p1stage[0][:, 0:512]
                lxe = [A.alloc(3 + 512, F32) for _ in range(NB)]
                sl = [A.alloc(512, BF16) for _ in range(NB)]
                xc = [A.alloc(512, F32) for _ in range(NB)] + [p1stage[0][:, 0:512], p1stage[1][:, 0:512]]
                xcb = [A.alloc(512, BF16) for _ in range(2)]
                rr = [A.alloc(512, F32) for _ in range(2)]
                ig = [A.alloc(512, F32) for _ in range(2)]
                aa = [A.alloc(512, F32) for _ in range(2)]
                a2 = rr
                hh = [A.alloc(512, F32) for _ in range(2)]
                hprev = A.alloc(8, F32)
                yl = [A.alloc(512, BF16) for _ in range(NB)]
                for hf in range(2):
                    for (wsrc, wdst, kk) in ((wr_in, wrb, 'wrb'), (wi_in, wib, 'wib')):
                        dma('sp', wst, wsrc[:, l * 1024 + hf * 512:l * 1024 + (hf + 1) * 512], reads=['stage0'], writes=['stage0'])
                        op('dve', lambda e: e.tensor_copy(out=wdst.rearrange("p b n -> p (b n)")[:, hf * 512:(hf + 1) * 512], in_=wst),
                           reads=['stage0'], writes=[kk])
                units = [(J, bl) for J in range(len(STS)) for bl in range(8)]
                yield

                def geom(u):
                    J, bl = units[u]
                    tiles = STS[J]
                    return J, bl, len(tiles) * 128, tiles[0] * 128, u % NB, u % 2, u % 5

                def stA(u, part):
                    J, bl, ntok, q0, s, s2, s5 = geom(u)
                    rows = slice(bl * 128, (bl + 1) * 128)
                    kl = 'lxe%d' % s
                    if part == 0 and J == 0:
                        op('dve', lambda e: e.memset(lxe[s][:, 0:3], 0.0), writes=[kl])
                        dma('sp', lxe[s][:, 3:3 + ntok], lxT[rows, q0:q0 + ntok], reads=[('lx', J)], writes=[kl])
                    elif part == 0:
                        dma('sp', lxe[s][:, 0:3 + ntok], lxT[rows, q0 - 3:q0 + ntok], reads=[('lx', J), ('lx', J - 1)], writes=[kl])
                    cwi = V_CW + (l * 8 + bl) * 4
                    cbi = V_CB + l * 8 + bl
                    kx = 'xc%d' % s5
                    if part == 0:
                      op('dve', lambda e: e.tensor_scalar(out=xc[s5][:, 0:ntok], in0=lxe[s][:, 3:3 + ntok], scalar1=vec[:, cwi + 3:cwi + 4],
                                                        scalar2=vec[:, cbi:cbi + 1], op0=ALU.mult, op1=ALU.add), reads=[kl, 'vec'], writes=[kx])
                    for k in ((2,) if part == 0 else (1, 0)):
                        op('dve', lambda e: e.scalar_tensor_tensor(out=xc[s5][:, 0:ntok], in0=lxe[s][:, k:k + ntok], scalar=vec[:, cwi + k:cwi + k + 1],
                                                                   in1=xc[s5][:, 0:ntok], op0=ALU.mult, op1=ALU.add), reads=[kl, 'vec', kx], writes=[kx])
                    if part == 1 and J == 0:
                        op('dve', lambda e: e.tensor_tensor(out=xc[s5][:, 0:128], in0=xc[s5][:, 0:128], in1=vm0, op=ALU.mult),
                           reads=[kx, 'vm0'], writes=[kx])

                def stA2(u):
                    J, bl, ntok, q0, s, s2, s5 = geom(u)
                    op('act', lambda e: e.activation(out=xcb[s2][:, 0:ntok], in_=xc[s5][:, 0:ntok], func=AF.Copy), reads=['xc%d' % s5], writes=['xcb%d' % s2])

                def stB(u):
                    J, bl, ntok, q0, s, s2, s5 = geom(u)
                    rps, ips = gbank[s2]
                    kr, ki = gkey[s2]
                    rows = slice(bl * 128, (bl + 1) * 128)
                    dma('sp', sl[s][:, 0:ntok], slg[rows, q0:q0 + ntok], reads=[('lx', J)], writes=['sl%d' % s])
                    op('pe', lambda e: e.matmul(rps[:, 0:ntok], lhsT=wrb[:, bl, :], rhs=xcb[s2][:, 0:ntok], start=True, stop=True),
                       reads=['wrb', 'xcb%d' % s2], writes=[kr])
                    op('pe', lambda e: e.matmul(ips[:, 0:ntok], lhsT=wib[:, bl, :], rhs=xcb[s2][:, 0:ntok], start=True, stop=True),
                       reads=['wib', 'xcb%d' % s2], writes=[ki])

                def stB2(u, part):
                    J, bl, ntok, q0, s, s2, s5 = geom(u)
                    rps, ips = gbank[s2]
                    kr, ki = gkey[s2]
                    bri = V_BR + l * 8 + bl
                    bii = V_BI + l * 8 + bl
                    li = l * 8 + bl
                    if part == 1:
                        for (o_, sc_) in ((aa[s2], hsp8), (a2[s2], hsp16)):
                            op('act', lambda e: e.activation(out=o_[:, 0:ntok], in_=rr[s2][:, 0:ntok], func=AF.Exp,
                                                             scale=sc_[:, li:li + 1], bias=sc_[:, li:li + 1]),
                               reads=['rr%d' % s2, 'hsp'], writes=['aa%d' % s2 if o_ is aa[s2] else 'rr%d' % s2])
                        op('act', lambda e: e.activation(out=a2[s2][:, 0:ntok], in_=a2[s2][:, 0:ntok], func=AF.Ln, scale=-1.0, bias=1.0),
                           reads=['rr%d' % s2], writes=['rr%d' % s2])
                        op('act', lambda e: e.activation(out=a2[s2][:, 0:ntok], in_=a2[s2][:, 0:ntok], func=AF.Exp, scale=0.5),
                           reads=['rr%d' % s2], writes=['rr%d' % s2])
                        return
                    op('act', lambda e: e.activation(out=rr[s2][:, 0:ntok], in_=rps[:, 0:ntok], func=AF.Tanh, scale=0.5, bias=hbr[:, li:li + 1]),
                       reads=[kr, 'hsp'], writes=['rr%d' % s2])
                    op('act', lambda e: e.activation(out=ig[s2][:, 0:ntok], in_=ips[:, 0:ntok], func=AF.Tanh, scale=0.5, bias=hbi[:, li:li + 1]),
                       reads=[ki, 'hsp'], writes=['ig%d' % s2])

                def stC(u):
                    J, bl, ntok, q0, s, s2, s5 = geom(u)
                    op('pool', lambda e: e.tensor_scalar(out=ig[s2][:, 0:ntok], in0=ig[s2][:, 0:ntok], scalar1=1.0, scalar2=0.5, op0=ALU.add, op1=ALU.mult),
                       reads=['ig%d' % s2], writes=['ig%d' % s2])
                    op('pool', lambda e: e.tensor_tensor(out=ig[s2][:, 0:ntok], in0=ig[s2][:, 0:ntok], in1=xc[s5][:, 0:ntok], op=ALU.mult),
                       reads=['ig%d' % s2, 'xc%d' % s5], writes=['ig%d' % s2])
                    op('pool', lambda e: e.tensor_tensor(out=ig[s2][:, 0:ntok], in0=ig[s2][:, 0:ntok], in1=a2[s2][:, 0:ntok], op=ALU.mult),
                       reads=['ig%d' % s2, 'rr%d' % s2], writes=['ig%d' % s2])
                    init = 0.0 if J == 0 else hprev[:, bl:bl + 1]
                    op('dve', lambda e: e.tensor_tensor_scan(out=hh[s2][:, 0:ntok], data0=aa[s2][:, 0:ntok], data1=ig[s2][:, 0:ntok],
                                                             initial=init, op0=ALU.mult, op1=ALU.add),
                       reads=['aa%d' % s2, 'ig%d' % s2, 'hprev'], writes=['hh%d' % s2])
                    op('dve', lambda e: e.tensor_copy(out=hprev[:, bl:bl + 1], in_=hh[s2][:, ntok - 1:ntok]), reads=['hh%d' % s2], writes=['hprev'])
                    op('dve', lambda e: e.tensor_tensor(out=yl[s][:, 0:ntok], in0=hh[s2][:, 0:ntok], in1=sl[s][:, 0:ntok], op=ALU.mult),
                       reads=['hh%d' % s2, 'sl%d' % s], writes=['yl%d' % s])

                def stD(u):
                    J, bl, ntok, q0, s, s2, s5 = geom(u)
                    dma('sp', yT[1024 + bl * 128:1024 + (bl + 1) * 128, q0:q0 + ntok], yl[s][:, 0:ntok], reads=['yl%d' % s], writes=['scr'])

                n = len(units)
                for t in range(n + 5):
                    pieces = []
                    if 0 <= t - 5 < n:
                        pieces.append((stD, (t - 5,)))
                    if 0 <= t - 4 < n:
                        pieces.append((stC, (t - 4,)))
                    if 0 <= t - 3 < n:
                        pieces.append((stB2, (t - 3, 0)))
                        pieces.append((stB2, (t - 3, 1)))
                    if 0 <= t - 2 < n:
                        pieces.append((stB, (t - 2,)))
                    if 0 <= t - 1 < n:
                        pieces.append((stA2, (t - 1,)))
                    if t < n:
                        pieces.append((stA, (t, 0)))
                        pieces.append((stA, (t, 1)))
                    for i, (f_, a_) in enumerate(pieces):
                        f_(*a_)
                        p4n[0] = t if i + 1 < len(pieces) else t + 1
                        yield

            if 'P1' in phases:
                A.reset()
                W = A.alloc(8 * DIN, BF16).rearrange("p (c n) -> p c n", c=8)
                SC = 808
                stage = [A.alloc(SC, F32) for _ in range(2)]
                p1stage = stage
                hb = [A.alloc(D, F32) for _ in range(2)]
                junk = A.alloc(D, BF16)
                ssq = [A.alloc(1, F32) for _ in range(3)]
                hn = [A.alloc(D, BF16) for _ in range(2)]
                hnT = [A.alloc(8 * 512, BF16).rearrange("p (c n) -> p c n", c=8) for _ in range(2)]
                evf = [A.alloc(512, F32) for _ in range(4)]
                evb = [A.alloc(512, BF16) for _ in range(4)]
                vt = [A.alloc(8 * 65, BF16).rearrange("p (h c) -> p h c", h=8) for _ in range(2)]
                gvt = [A.alloc(512, BF16) for _ in range(2)]
                ffs = A.alloc(512, F32, 8)
                spf = ffs
                g4 = p4_units()
                p4n = [0]
                gcount = [0]

                def p4adv(k, lim):
                    for _ in range(k):
                        if p4n[0] < lim:
                            next(g4)
                pstores = []

                def sdma(dst, src_, reads=(), writes=()):
                    pstores.append((dst, src_, reads, writes))

                def sflush(keep):
                    while len(pstores) > keep:
                        d_, s_, r_, w_ = pstores.pop(0)
                        dma('sp', d_, s_, reads=r_, writes=w_)
                cc = A.alloc(512, F32, 8)
                cprev = A.alloc(1, F32, 8)
                cq = A.alloc(512, BF16, 8)
                kp = A.alloc(3 * 512, BF16, 8).rearrange("p (r n) -> p r n", r=3)
                r1 = A.alloc(512, F32, 8)
                gab = A.alloc(512, BF16, 16)
                si = 0
                for c in range(8):
                    for s0 in range(0, DIN, SC):
                        st = stage[si % 2]
                        k = 'stage%d' % (si % 2)
                        si += 1
                        dma('sp', st, w_in[l, c * 128:(c + 1) * 128, s0:s0 + SC], writes=[k])
                        op('dve', lambda e, st=st, c=c, s0=s0: e.tensor_scalar(
                            out=W[:, c, s0:s0 + SC], in0=st, scalar1=vec[:, V_PREG + l * 8 + c:V_PREG + l * 8 + c + 1],
                            scalar2=None, op0=ALU.mult), reads=[k, 'vec'], writes=['W'])
                for hh in range(2):
                    op('dve', lambda e, hh=hh: e.memset(vt[hh][:, :, 64:65], 1.0), writes=['vt%d' % hh])
                psi = 0
                evi = 0
                tcount = 0
                pinfo = {}

                def prep_a(Jp, ti):
                    nonlocal tcount
                    t = STS[Jp][ti]
                    b = hb[tcount % 2]
                    kb = 'hb%d' % (tcount % 2)
                    sq = ssq[tcount % 3]
                    ksq = 'ssq%d' % (tcount % 3)
                    n_ = hn[tcount % 2]
                    kn = 'hn%d' % (tcount % 2)
                    pinfo[(Jp, ti)] = (n_, kn)
                    tcount += 1
                    dma('sp', b, hsrc[t * 128:(t + 1) * 128, :], writes=[kb])
                    op('act', lambda e: e.activation(out=junk, in_=b, func=AF.Square, accum_out=sq), reads=[kb], writes=['junk', ksq])
                    op('act', lambda e: e.activation(out=sq, in_=sq, func=AF.Ln, scale=1.0 / D, bias=EPS), reads=[ksq], writes=[ksq])
                    op('act', lambda e: e.activation(out=sq, in_=sq, func=AF.Exp, scale=-0.5), reads=[ksq], writes=[ksq])
                    op('dve', lambda e: e.tensor_scalar(out=n_, in0=b, scalar1=sq, scalar2=None, op0=ALU.mult), reads=[kb, ksq], writes=[kn])

                def prep_b(Jp, ti):
                    n_, kn = pinfo.pop((Jp, ti))
                    X = hnT[Jp % 2]
                    kX = 'hnT%d' % (Jp % 2)
                    pT = psB[0]
                    kpT = 'psB0'
                    for c in range(8):
                        op('pe', lambda e: e.transpose(out=pT[:, c * 128:(c + 1) * 128], in_=n_[:, c * 128:(c + 1) * 128], identity=ident),
                           reads=[kn, 'ident'], writes=[kpT])
                    if ti % 2 == 0:
                        op('act', lambda e: e.activation(out=X[:, :, ti * 128:(ti + 1) * 128], in_=pT.rearrange("p (c n) -> p c n", c=8), func=AF.Copy),
                           reads=[kpT], writes=[kX])
                    else:
                        op('dve', lambda e: e.tensor_copy(out=X[:, :, ti * 128:(ti + 1) * 128], in_=pT.rearrange("p (c n) -> p c n", c=8)),
                           reads=[kpT], writes=[kX])

                next(g4)
                for J, tiles in enumerate(STS):
                    ntl = len(tiles)
                    ntok = ntl * 128
                    q0 = tiles[0] * 128
                    X = hnT[J % 2]
                    kX = 'hnT%d' % (J % 2)
                    if J == 0:
                        prep_a(0, 0)
                        prep_b(0, 0)
                    gl = [0]
                    def fm_group(c0, M, evac):
                        nonlocal psi
                        ps = psF[psi % 3]
                        kps = 'psF%d' % (psi % 3)
                        psi += 1
                        gcount[0] += 1
                        gl[0] += 1
                        if J + 1 < len(STS) and gl[0] in (4, 12, 20, 28):
                            prep_a(J + 1, (gl[0] - 4) // 8)
                        if J + 1 < len(STS) and gl[0] in (8, 16, 24, 32):
                            prep_b(J + 1, (gl[0] - 8) // 8)
                        sflush(3)
                        p4adv(2 if gcount[0] % 2 == 0 else 1, 8 * J)
                        for kc in range(8):
                            op('pe', lambda e, ps=ps, kc=kc: e.matmul(ps[0:M, 0:ntok], lhsT=W[:, kc, c0:c0 + M],
                                                                     rhs=X[:, kc, 0:ntok], start=(kc == 0), stop=(kc == 7)),
                               reads=['W', kX], writes=[kps])
                        evac(ps[0:M, 0:ntok], kps)

                    def ev_copy(dst_list, scale=None, dtype=BF16, eng='dve', wkey='scr'):
                        def f(ps, kps):
                            nonlocal evi
                            M = ps.shape[0]
                            buf = (evb if dtype == BF16 else evf)[evi % 4]
                            kb_ = ('evb%d' if dtype == BF16 else 'evf%d') % (evi % 4)
                            evi += 1
                            o = buf[0:M, 0:ntok]
                            if eng == 'silu':
                                op('act', lambda e: e.activation(out=o, in_=ps, func=AF.Silu), reads=[kps], writes=[kb_])
                            elif scale is not None:
                                op('dve', lambda e: e.tensor_scalar(out=o, in0=ps, scalar1=scale, scalar2=None, op0=ALU.mult),
                                   reads=[kps], writes=[kb_])
                            else:
                                op('dve', lambda e: e.tensor_copy(out=o, in_=ps), reads=[kps], writes=[kb_])
                            for (r0, nr, dst) in dst_list:
                                sdma(dst, buf[r0:r0 + nr, 0:ntok], reads=[kb_], writes=[wkey])
                        return f

                    tk = slice(q0, q0 + ntok)
                    def ev_ff(ps, kps):
                        op('dve', lambda e: e.tensor_copy(out=ffs[:, 0:ntok], in_=ps), reads=[kps], writes=['ffs'])
                    fm_group(C_FF, 8, ev_ff)
                    op('act', lambda e: e.activation(out=spf[:, 0:ntok], in_=ffs[:, 0:ntok], func=AF.Exp,
                                                     bias=nbf[:, l:l + 1], scale=-1.0), reads=['ffs', 'nbf'], writes=['ffs'])
                    op('act', lambda e: e.activation(out=spf[:, 0:ntok], in_=spf[:, 0:ntok], func=AF.Ln, bias=1.0),
                       reads=['ffs'], writes=['ffs'])
                    if J == 0:
                        op('dve', lambda e: e.tensor_tensor(out=spf[:, 0:128], in0=spf[:, 0:128], in1=vm0[0:8, :], op=ALU.mult),
                           reads=['ffs', 'vm0'], writes=['ffs'])
                    init = 0.0 if J == 0 else cprev
                    op('dve', lambda e, init=init: e.tensor_tensor_scan(out=cc[:, 0:ntok], data0=ones_f[0:8, 0:ntok], data1=spf[:, 0:ntok],
                                                                        initial=init, op0=ALU.mult, op1=ALU.subtract),
                       reads=['ffs', 'ones', 'cprev'], writes=['cc'])
                    op('dve', lambda e: e.tensor_copy(out=cprev, in_=cc[:, ntok - 1:ntok]), reads=['cc'], writes=['cprev'])
                    op('dve', lambda e: e.tensor_copy(out=cq[:, 0:ntok], in_=cc[:, 0:ntok]), reads=['cc'], writes=['cq'])
                    op('dve', lambda e: e.tensor_scalar(out=kp[:, 0, 0:ntok], in0=cc[:, 0:ntok], scalar1=-1.0, scalar2=None, op0=ALU.mult),
                       reads=['cc'], writes=['kp'])
                    op('dve', lambda e: e.scalar_tensor_tensor(out=r1[:, 0:ntok], in0=cc[:, 0:ntok], scalar=-1.0, in1=kp[:, 0, 0:ntok],
                                                               op0=ALU.mult, op1=ALU.subtract), reads=['cc', 'kp'], writes=['r1'])
                    op('dve', lambda e: e.tensor_copy(out=kp[:, 1, 0:ntok], in_=r1[:, 0:ntok]), reads=['r1'], writes=['kp'])
                    op('dve', lambda e: e.tensor_tensor(out=r1[:, 0:ntok], in0=r1[:, 0:ntok], in1=kp[:, 1, 0:ntok], op=ALU.subtract),
                       reads=['r1', 'kp'], writes=['r1'])
                    op('dve', lambda e: e.tensor_copy(out=kp[:, 2, 0:ntok], in_=r1[:, 0:ntok]), reads=['r1'], writes=['kp'])
                    sdma(qa[:, 64, tk], cq[:, 0:ntok], reads=['cq'], writes=['scr'])
                    sdma(ka[:, 65:68, tk], kp[:, :, 0:ntok], reads=['kp'], writes=['scr'])

                    def ev_ga(ps, kps):
                        op('dve', lambda e: e.tensor_copy(out=gab[:, 0:ntok], in_=ps), reads=[kps], writes=['gab'])
                        sdma(gaT[:, tk], gab[:, 0:ntok], reads=['gab'], writes=['scr'])
                    fm_group(C_GA, 16, ev_ga)
                    for g in range(4):
                        fm_group(C_FQ + g * 128, 128, ev_copy([(0, 64, qa[2 * g, 0:64, tk]), (64, 64, qa[2 * g + 1, 0:64, tk])], scale=0.125))
                    for g in range(4):
                        fm_group(C_FK + g * 128, 128, ev_copy([(0, 64, ka[2 * g, 0:64, tk]), (64, 64, ka[2 * g + 1, 0:64, tk])]))
                    for g in range(4):
                        fm_group(C_FG + g * 128, 128, ev_copy([(0, 128, sgf[g * 128:(g + 1) * 128, tk])], eng='silu'))
                    for g in range(2):
                        fm_group(C_GQ + g * 128, 128, ev_copy([(0, 128, gqT[g * 128:(g + 1) * 128, tk])], dtype=F32))
                    for g in range(2):
                        fm_group(C_GK + g * 128, 128, ev_copy([(0, 128, gkT[g * 128:(g + 1) * 128, tk])], dtype=F32))
                    for g in range(4):
                        fm_group(C_GG + g * 128, 128, ev_copy([(0, 128, sgg[g * 128:(g + 1) * 128, tk])], eng='silu'))
                    for g in range(8):
                        fm_group(C_LX + g * 128, 128, ev_copy([(0, 128, lxT[g * 128:(g + 1) * 128, tk])], dtype=F32, wkey=('lx', J)))
                    for g in range(8):
                        fm_group(C_LG + g * 128, 128, ev_copy([(0, 128, slg[g * 128:(g + 1) * 128, tk])], eng='silu', wkey=('lx', J)))
                    for ti, t in enumerate(tiles):
                        for which in (0, 1):
                            ps = psF[psi % 3]
                            kps = 'psF%d' % (psi % 3)
                            psi += 1
                            c0 = C_FV if which == 0 else C_GV
                            sflush(3)
                            p4adv(2, 8 * J)
                            for kc in range(8):
                                op('pe', lambda e, ps=ps, kc=kc, c0=c0, ti=ti: e.matmul(
                                    ps[:, :], lhsT=X[:, kc, ti * 128:(ti + 1) * 128], rhs=W[:, kc, c0:c0 + 512],
                                    start=(kc == 0), stop=(kc == 7)), reads=['W', kX], writes=[kps])
                            if which == 0:
                                v_ = vt[t % 2]
                                kv_ = 'vt%d' % (t % 2)
                                op('dve', lambda e, ps=ps, v_=v_: e.tensor_copy(out=v_[:, :, 0:64],
                                                                               in_=ps.rearrange("p (h c) -> p h c", h=8)),
                                   reads=[kps], writes=[kv_])
                                if t == 0:
                                    sdma(va[:, 112:128, :].rearrange("h t c -> t h c"), v_[112:128, :, :], reads=[kv_], writes=['scr'])
                                    sdma(va[:, 0:112, :].rearrange("h t c -> t h c"),
                                        zeros_bf[0:112, :].rearrange("p (h c) -> p h c", h=8), reads=['ones'], writes=['scr'])
                                else:
                                    sdma(va[:, t * 128:(t + 1) * 128, :].rearrange("h t c -> t h c"), v_, reads=[kv_], writes=['scr'])
                            else:
                                g_ = gvt[t % 2]
                                kg_ = 'gvt%d' % (t % 2)
                                op('act', lambda e, ps=ps, g_=g_: e.activation(out=g_, in_=ps, func=AF.Copy), reads=[kps], writes=[kg_])
                                sdma(gv[t * 128:(t + 1) * 128, :], g_, reads=[kg_], writes=['scr'])
                    while p4n[0] < 8 * J:
                        next(g4)
                sflush(0)
                for _ in g4:
                    pass
                T.barrier()

            if 'P2' in phases:
                A.reset()
                Qa = [A.alloc(L, BF16, 68) for _ in range(2)]
                Ka = [A.alloc(L, BF16, 68) for _ in range(2)]
                Va = [A.alloc(NT * 65, BF16).rearrange("p (n c) -> p n c", n=NT) for _ in range(2)]
                SG = [A.alloc(L, BF16, 64) for _ in range(2)]
                pt = [A.alloc(512, BF16) for _ in range(3)]
                dn = A.alloc(512, F32, 65)
                rdb = A.alloc(512, BF16, 65)
                t1 = [A.alloc(512, F32, 64) for _ in range(2)]
                yb = [A.alloc(512, BF16, 64) for _ in range(2)]
                pt.append(A.alloc(512, BF16))
                pt.append(A.alloc(512, BF16))
                bpsF = psB[1].bitcast(F32)
                gk_ = [0]
                blk_ = [0]
                for h in range(8):
                    s = h % 2
                    kQ, kK, kV, kS = 'Qa%d' % s, 'Ka%d' % s, 'Va%d' % s, 'SG%d' % s
                    dma('sp', Qa[s], qa[h], writes=[kQ])
                    dma('sp', Ka[s], ka[h], writes=[kK])
                    for n0 in range(0, NT, 8):
                        n1 = min(NT, n0 + 8)
                        dma('sp', Va[s][:, n0:n1, :], va[h, n0 * 128:n1 * 128, :].rearrange("(n p) c -> p n c", p=128), writes=[kV])
                    dma('sp', SG[s], sgf[h * 64:(h + 1) * 64, :], writes=[kS])
                    tasks = []
                    for J, tiles in enumerate(STS):
                        ob = blk_[0] % 2
                        blk_[0] += 1
                        for I in range(tiles[-1] + 1):
                            tasks.append((J, I, ob, gk_[0]))
                            gk_[0] += 1
                    pend = []

                    def emit_S(task):
                        J, I, ob, g = task
                        tiles = STS[J]
                        ntok = len(tiles) * 128
                        q0 = tiles[0] * 128
                        off = max(0, I - tiles[0]) * 128
                        sps = psF[g % 4]
                        ksps = 'psF%d' % (g % 4)
                        p_ = pt[g % 5]
                        kp_ = 'pt%d' % (g % 5)
                        op('pe', lambda e: e.matmul(sps[:, off:ntok], lhsT=Ka[s][:, I * 128:(I + 1) * 128],
                                                    rhs=Qa[s][:, q0 + off:q0 + ntok], start=True, stop=True),
                           reads=[kQ, kK], writes=[ksps])
                        op('act', lambda e: e.activation(out=p_[:, off:ntok], in_=sps[:, off:ntok], func=AF.Exp),
                           reads=[ksps], writes=[kp_])
                        if I >= tiles[0]:
                            op('dve', lambda e: e.tensor_tensor(out=p_[:, off:off + 128], in0=p_[:, off:off + 128],
                                                                 in1=tri4[:, 0:128], op=ALU.mult),
                               reads=[kp_, 'tri4'], writes=[kp_])

                    def emit_PV(task):
                        J, I, ob, g = task
                        tiles = STS[J]
                        ntok = len(tiles) * 128
                        q0 = tiles[0] * 128
                        last = tiles[-1]
                        off = max(0, I - tiles[0]) * 128
                        ops_ = psF[4 + ob]
                        kops = 'psF%d' % (4 + ob)
                        p_ = pt[g % 5]
                        kp_ = 'pt%d' % (g % 5)
                        op('pe', lambda e: e.matmul(ops_[0:65, off:ntok], lhsT=Va[s][:, I, :], rhs=p_[:, off:ntok],
                                                    start=(I == 0), stop=(I == last)), reads=[kV, kp_], writes=[kops])
                        if I != last:
                            return
                        while pend:
                            pend.pop(0)[1]()
                        op('dve', lambda e: e.tensor_scalar(out=dn[64:65, 0:ntok], in0=ops_[64:65, 0:ntok], scalar1=1e-30,
                                                            scalar2=None, op0=ALU.max), reads=[kops], writes=['dn'])
                        op('dve', lambda e: e.reciprocal(out=dn[64:65, 0:ntok], in_=dn[64:65, 0:ntok]), reads=['dn'], writes=['dn'])
                        op('dve', lambda e: e.tensor_copy(out=rdb[64:65, 0:ntok], in_=dn[64:65, 0:ntok]), reads=['dn'], writes=['rdb'])

                        def partB():
                            bps = bpsF
                            op('pe', lambda e: e.matmul(bps[0:64, 0:ntok], lhsT=ones_bf[64:65, 0:64], rhs=rdb[64:65, 0:ntok],
                                                        start=True, stop=True), reads=['rdb', 'ones'], writes=['bpsF'])
                            t_ = t1[ob]
                            kt_ = 't1%d' % ob
                            y_ = yb[ob]
                            ky_ = 'yb%d' % ob
                            op('dve', lambda e: e.tensor_tensor(out=t_[:, 0:ntok], in0=ops_[0:64, 0:ntok], in1=SG[s][:, q0:q0 + ntok],
                                                                op=ALU.mult), reads=[kops, kS], writes=[kt_])
                            op('dve', lambda e: e.tensor_tensor(out=y_[:, 0:ntok], in0=t_[:, 0:ntok], in1=bps[0:64, 0:ntok],
                                                                op=ALU.mult), reads=[kt_, 'bpsF'], writes=[ky_])
                            dma('act', yT[h * 64:(h + 1) * 64, q0:q0 + ntok], y_[:, 0:ntok], reads=[ky_], writes=['scr'])
                        pend.append([8, partB])

                    DPT = 3
                    for k in range(len(tasks) + DPT):
                        if k < len(tasks):
                            emit_S(tasks[k])
                        if k - DPT >= 0:
                            for pb in pend:
                                pb[0] -= 1
                            while pend and pend[0][0] <= 0:
                                pend.pop(0)[1]()
                            emit_PV(tasks[k - DPT])
                    while pend:
                        pend.pop(0)[1]()
                T.barrier()

            def p3_units():
                r3 = lambda n, dt_: [A.alloc(n, dt_) for _ in range(3)]
                r2 = lambda n, dt_: [A.alloc(n, dt_) for _ in range(2)]
                v3 = lambda lst, c: [x.rearrange("p (c n) -> p c n", c=c) for x in lst]
                gq = v3(r3(1024, F32), 2)
                gk = v3(r3(1024, F32), 2)
                ga_ = [A.alloc(512, BF16, 16) for _ in range(3)]
                gvs = v3(r3(2048, BF16), 4)
                sg_ = v3(r3(2048, BF16), 4)
                ee = v3([A.alloc(1024, F32)] * 2, 2)
                bs = v3([A.alloc(1024, F32)] * 2, 2)
                ebh = v3([A.alloc(1024, F32)] * 2, 2)
                ebq = v3([A.alloc(1024, F32)] * 2, 2)
                ebk = v3([A.alloc(1024, F32)] * 2, 2)
                qz = [v3([A.alloc(1024, BF16) for _ in range(2)], 2) for _ in range(3)]
                kt = v3(r3(1024, BF16), 2)
                khT = v3(r3(1024, BF16), 2)
                nbl = v3(r3(8, F32), 2)
                dec = v3(r3(8, F32), 2)
                yg = v3(r2(2048, BF16), 4)
                at = r2(512, BF16)
                kh = r2(256, BF16)
                sqb = r2(512, BF16)
                rs = r2(512, F32)
                t1g = r2(512, F32)
                S = A.alloc(2 * 128, F32).rearrange("p (c n) -> p c n", c=2)
                Sbf = A.alloc(2 * 128, BF16).rearrange("p (c n) -> p c n", c=2)
                op('dve', lambda e: e.memset(S, 0.0), writes=['S'])
                op('dve', lambda e: e.memset(Sbf, 0.0), writes=['Sbf'])
                for j in range(3):
                    for p in range(2):
                        op('dve', lambda e: e.memset(qz[j][p], 0.0), writes=['qz%d' % j])
                aps, opsb, ups = psF[2], [psF[3], psF[4]], psF[5]
                xm = psB[1].bitcast(F32)
                tps = psB[0]
                flat = [(J, tt) for J, tiles in enumerate(STS) for tt in range(len(tiles))]
                v0 = {}
                for v, (J, tt) in enumerate(flat):
                    v0.setdefault(J, v)

                def geo(J):
                    tiles = STS[J]
                    return len(tiles), len(tiles) * 128, slice(tiles[0] * 128, (tiles[-1] + 1) * 128), J % 3, J % 2

                def PA(J):
                    ntl, ntok, tk, j3, j2 = geo(J)
                    dma('sp', gq[j3][:, :, 0:ntok], gqT[:, tk].rearrange("(c p) t -> p c t", p=128), writes=['gq%d' % j3])
                    dma('sp', gk[j3][:, :, 0:ntok], gkT[:, tk].rearrange("(c p) t -> p c t", p=128), writes=['gk%d' % j3])
                    dma('sp', ga_[j3][:, 0:ntok], gaT[:, tk], writes=['ga%d' % j3])
                    dma('sp', gvs[j3][:, 0:ntl, :], gv[tk, :].rearrange("(t p) n -> p t n", p=128), writes=['gvs%d' % j3])
                    dma('sp', sg_[j3][:, :, 0:ntok], sgg[:, tk].rearrange("(c p) t -> p c t", p=128), writes=['sg%d' % j3])
                    for c in range(2):
                        op('pe', lambda e: e.matmul(xm[:, 0:ntok], lhsT=wa2[:, l * 256 + c * 128:l * 256 + (c + 1) * 128],
                                                    rhs=ga_[j3][:, 0:ntok], start=True, stop=True), reads=['wa2', 'ga%d' % j3], writes=['xm'])
                        op('act', lambda e: e.activation(out=ee[j2][:, c, 0:ntok], in_=xm[:, 0:ntok], func=AF.Exp,
                                                         bias=nba[:, l * 2 + c:l * 2 + c + 1], scale=-1.0), reads=['xm', 'nba'], writes=['ee'])
                        op('act', lambda e: e.activation(out=ee[j2][:, c, 0:ntok], in_=ee[j2][:, c, 0:ntok], func=AF.Ln, bias=1.0),
                           reads=['ee'], writes=['ee'])

                def PB(J):
                    ntl, ntok, tk, j3, j2 = geo(J)
                    for c in range(2):
                        for tt in range(ntl):
                            r = slice(tt * 128, (tt + 1) * 128)
                            op('dve', lambda e: e.tensor_tensor_scan(out=bs[j2][:, c, r], data0=ones_f[:, 0:128], data1=ee[j2][:, c, r],
                                                                     initial=0.0, op0=ALU.mult, op1=ALU.add),
                               reads=['ee', 'ones'], writes=['bs'])
                            op('dve', lambda e: e.tensor_scalar(out=nbl[j3][:, c, tt:tt + 1], in0=bs[j2][:, c, tt * 128 + 127:tt * 128 + 128],
                                                                scalar1=-1.0 / 16, scalar2=None, op0=ALU.mult),
                               reads=['bs'], writes=['nbl%d' % j3])

                def PC(J):
                    ntl, ntok, tk, j3, j2 = geo(J)
                    for c in range(2):
                        for tt in range(ntl):
                            r = slice(tt * 128, (tt + 1) * 128)
                            op('act', lambda e: e.activation(out=ebh[j2][:, c, r], in_=bs[j2][:, c, r], func=AF.Exp,
                                                             bias=nbl[j3][:, c, tt:tt + 1], scale=1.0 / 16),
                               reads=['bs', 'nbl%d' % j3], writes=['ebh'])
                        op('act', lambda e: e.activation(out=dec[j3][:, c, 0:ntl], in_=nbl[j3][:, c, 0:ntl], func=AF.Exp),
                           reads=['nbl%d' % j3], writes=['dec%d' % j3])
                        op('act', lambda e: e.activation(out=ebq[j2][:, c, 0:ntok], in_=bs[j2][:, c, 0:ntok], func=AF.Exp, scale=-1.0 / 16),
                           reads=['bs'], writes=['ebq'])
                        op('act', lambda e: e.activation(out=ebk[j2][:, c, 0:ntok], in_=bs[j2][:, c, 0:ntok], func=AF.Exp, scale=1.0 / 16),
                           reads=['bs'], writes=['ebk'])

                def PD(J):
                    ntl, ntok, tk, j3, j2 = geo(J)
                    for c in range(2):
                        op('dve', lambda e: e.tensor_tensor(out=khT[j3][:, c, 0:ntok], in0=gk[j3][:, c, 0:ntok], in1=ebh[j2][:, c, 0:ntok], op=ALU.mult),
                           reads=['gk%d' % j3, 'ebh'], writes=['khT%d' % j3])
                        for p in range(2):
                            pr = slice(p * 64, (p + 1) * 64)
                            op('dve', lambda e: e.scalar_tensor_tensor(out=qz[j3][p][pr, c, 0:ntok], in0=gq[j3][pr, c, 0:ntok], scalar=0.125,
                                                                       in1=ebq[j2][pr, c, 0:ntok], op0=ALU.mult, op1=ALU.mult),
                               reads=['gq%d' % j3, 'ebq'], writes=['qz%d' % j3])
                        op('dve', lambda e: e.tensor_tensor(out=kt[j3][:, c, 0:ntok], in0=gk[j3][:, c, 0:ntok], in1=ebk[j2][:, c, 0:ntok], op=ALU.mult),
                           reads=['gk%d' % j3, 'ebk'], writes=['kt%d' % j3])

                def TA(v):
                    J, tt = flat[v]
                    j3 = J % 3
                    u = v % 2
                    r = slice(tt * 128, (tt + 1) * 128)
                    for hd in range(4):
                        c, p = hd // 2, hd % 2
                        op('pe', lambda e: e.matmul(aps[:, hd * 128:(hd + 1) * 128], lhsT=kt[j3][:, c, r], rhs=qz[j3][p][:, c, r],
                                                    start=True, stop=True), reads=['kt%d' % j3, 'qz%d' % j3], writes=['aps'])
                    op('dve', lambda e: e.tensor_tensor(out=at[u], in0=aps[:, :], in1=tri4, op=ALU.mult), reads=['aps', 'tri4'], writes=['at%d' % u])
                    for c in range(2):
                        op('pe', lambda e: e.transpose(out=tps[:, c * 128:(c + 1) * 128], in_=khT[j3][:, c, r], identity=ident),
                           reads=['khT%d' % j3, 'ident'], writes=['tps'])
                    op('act', lambda e: e.activation(out=kh[u], in_=tps[:, 0:256], func=AF.Copy), reads=['tps'], writes=['kh%d' % u])

                def TB(v):
                    J, tt = flat[v]
                    j3 = J % 3
                    u = v % 2
                    r = slice(tt * 128, (tt + 1) * 128)
                    ops_ = opsb[u]
                    ko = 'ops%d' % u
                    for hd in range(4):
                        c, p = hd // 2, hd % 2
                        op('pe', lambda e: e.matmul(ops_[:, hd * 128:(hd + 1) * 128], lhsT=gvs[j3][:, tt, hd * 128:(hd + 1) * 128],
                                                    rhs=at[u][:, hd * 128:(hd + 1) * 128], start=True, stop=False),
                           reads=['gvs%d' % j3, 'at%d' % u], writes=[ko])
                        op('pe', lambda e: e.matmul(ops_[:, hd * 128:(hd + 1) * 128], lhsT=Sbf[:, c, :], rhs=qz[j3][p][:, c, r],
                                                    start=False, stop=True), reads=['Sbf', 'qz%d' % j3], writes=[ko])
                    for c in range(2):
                        op('pe', lambda e: e.matmul(ups[:, c * 256:(c + 1) * 256], lhsT=kh[u][:, c * 128:(c + 1) * 128],
                                                    rhs=gvs[j3][:, tt, c * 256:(c + 1) * 256], start=True, stop=True),
                           reads=['kh%d' % u, 'gvs%d' % j3], writes=['ups'])
                    for hd in range(4):
                        c, p = hd // 2, hd % 2
                        pr = slice(p * 64, (p + 1) * 64)
                        op('dve', lambda e: e.scalar_tensor_tensor(out=S[pr, c, :], in0=S[pr, c, :], scalar=dec[j3][pr, c, tt:tt + 1],
                                                                   in1=ups[pr, c * 256 + p * 128:c * 256 + (p + 1) * 128],
                                                                   op0=ALU.mult, op1=ALU.add), reads=['S', 'dec%d' % j3, 'ups'], writes=['S'])
                    op('act', lambda e: e.activation(out=Sbf, in_=S, func=AF.Copy), reads=['S'], writes=['Sbf'])
                    op('act', lambda e: e.activation(out=sqb[u], in_=ops_[:, :], func=AF.Square), reads=[ko], writes=['sqb%d' % u])

                def TC(v):
                    J, tt = flat[v]
                    ntl, ntok, tk, j3, j2 = geo(J)
                    u = v % 2
                    r = slice(tt * 128, (tt + 1) * 128)
                    ops_ = opsb[u]
                    ko = 'ops%d' % u
                    op('pe', lambda e: e.matmul(xm[:, :], lhsT=onesN, rhs=sqb[u], start=True, stop=True), reads=['sqb%d' % u, 'ones'], writes=['xm'])
                    op('act', lambda e: e.activation(out=rs[u], in_=xm[:, :], func=AF.Ln, bias=EPS), reads=['xm'], writes=['rs%d' % u])
                    op('act', lambda e: e.activation(out=rs[u], in_=rs[u], func=AF.Exp, scale=-0.5), reads=['rs%d' % u], writes=['rs%d' % u])
                    op('dve', lambda e: e.tensor_tensor(out=t1g[u], in0=ops_[:, :], in1=rs[u], op=ALU.mult), reads=[ko, 'rs%d' % u], writes=['t1g%d' % u])
                    for hd in range(4):
                        op('dve', lambda e: e.scalar_tensor_tensor(out=yg[j2][:, hd, r], in0=t1g[u][:, hd * 128:(hd + 1) * 128],
                                                                   scalar=vec[:, V_GNG + l * 4 + hd:V_GNG + l * 4 + hd + 1],
                                                                   in1=sg_[j3][:, hd, r], op0=ALU.mult, op1=ALU.mult),
                           reads=['t1g%d' % u, 'vec', 'sg%d' % j3], writes=['yg%d' % j2])

                def TD(v):
                    J, tt = flat[v]
                    ntl, ntok, tk, j3, j2 = geo(J)
                    if tt == ntl - 1:
                        dma('sp', yT[512:1024, tk].rearrange("(c p) t -> p c t", p=128), yg[j2][:, :, 0:ntok], reads=['yg%d' % j2], writes=[('ygs', J)])
                        gla_done[0] = STS[J][-1] + 1

                nJ = len(STS)
                for t in range(-7, NT + 3):
                    if 0 <= t - 1 < NT:
                        TB(t - 1)
                        yield
                    if 0 <= t - 3 < NT:
                        TD(t - 3)
                    if 0 <= t < NT:
                        TA(t)
                        yield
                    if 0 <= t - 2 < NT:
                        TC(t - 2)
                        yield
                    for J in range(nJ):
                        k = t - (v0[J] - 7)
                        if k == 0:
                            PA(J)
                        elif k == 1:
                            PB(J)
                        elif k == 2:
                            PC(J)
                        elif k == 3:
                            PD(J)
                    yield

            def p5_units():
                Wo = A.alloc(16 * D, BF16).rearrange("p (c n) -> p c n", c=16)
                wst = A.alloc(D, F32)
                pg = A.alloc(D, F32)
                yt = [A.alloc(16 * 128, BF16).rearrange("p (c n) -> p c n", c=16) for _ in range(2)]
                hb = [A.alloc(D, F32) for _ in range(2)]
                zs = [A.alloc(D, F32) for _ in range(2)]
                junk = A.alloc(D, BF16)
                ssq = [A.alloc(1, F32) for _ in range(2)]
                for c in range(16):
                    dma('sp', wst, w_out[l, c * 128:(c + 1) * 128, :], writes=['wst5'])
                    op('dve', lambda e: e.tensor_copy(out=Wo[:, c, :], in_=wst), reads=['wst5'], writes=['Wo'])
                    if c % 4 == 3:
                        yield
                dma('sp', pg, pg_in[l], writes=['pg'])
                yield
                tJ = {}
                for J, tiles in enumerate(STS):
                    for t in tiles:
                        tJ[t] = J
                for t in range(NT):
                    while gla_done[0] <= t:
                        yield
                    u = t % 2
                    dma('sp', yt[u], yT[:, t * 128:(t + 1) * 128].rearrange("(c p) t -> p c t", p=128), reads=[('ygs', tJ[t])], writes=['yt%d' % u])
                    dma('sp', hb[u], hsrc[t * 128:(t + 1) * 128, :], writes=['hb%d' % u])
                    for nh in range(2):
                        zp = psF[nh]
                        for c0 in range(0, 16, 4):
                            for c in range(c0, c0 + 4):
                                op('pe', lambda e: e.matmul(zp[:, :], lhsT=yt[u][:, c, :], rhs=Wo[:, c, nh * 512:(nh + 1) * 512],
                                                            start=(c == 0), stop=(c == 15)), reads=['yt%d' % u, 'Wo'], writes=['psF%d' % nh])
                            yield
                        op('act', lambda e: e.activation(out=zs[u][:, nh * 512:(nh + 1) * 512], in_=zp[:, :], func=AF.Copy),
                           reads=['psF%d' % nh], writes=['zs%d' % u])
                    yield
                    op('act', lambda e: e.activation(out=junk, in_=zs[u], func=AF.Square, accum_out=ssq[u]),
                       reads=['zs%d' % u], writes=['junk5', 'ssq%d' % u])
                    op('act', lambda e: e.activation(out=ssq[u], in_=ssq[u], func=AF.Ln, scale=1.0 / D, bias=EPS),
                       reads=['ssq%d' % u], writes=['ssq%d' % u])
                    op('act', lambda e: e.activation(out=ssq[u], in_=ssq[u], func=AF.Exp, scale=-0.5),
                       reads=['ssq%d' % u], writes=['ssq%d' % u])
                    yield
                    op('dve', lambda e: e.scalar_tensor_tensor(out=zs[u], in0=zs[u], scalar=ssq[u], in1=pg, op0=ALU.mult, op1=ALU.mult),
                       reads=['zs%d' % u, 'ssq%d' % u, 'pg'], writes=['zs%d' % u])
                    op('dve', lambda e: e.tensor_tensor(out=zs[u], in0=zs[u], in1=hb[u], op=ALU.add),
                       reads=['zs%d' % u, 'hb%d' % u], writes=['zs%d' % u])
                    yield
                    if l == NL - 1 or l == layers[-1]:
                        if t >= 1:
                            dma('act', out[(t - 1) * 128:t * 128, :], zs[u], reads=['zs%d' % u], writes=['out'])
                    else:
                        dma('act', h1[t * 128:(t + 1) * 128, :], zs[u], reads=['zs%d' % u], writes=['scr'])
                    yield

            if 'P3' in phases:
                A.reset()
                gla_done = [0]
                g3, g5 = p3_units(), p5_units()
                a5 = True
                for _ in g3:
                    for _k in range(3):
                        if a5:
                            try:
                                next(g5)
                            except StopIteration:
                                a5 = False
                if a5:
                    for _ in g5:
                        pass
                T.barrier()

        T.barrier()
        with nc.Block() as block:
            @block.sync
            def _(e):
                T.replay('sp', e)

            @block.tensor
            def _(e):
                T.replay('pe', e)

            @block.scalar
            def _(e):
                T.replay('act', e)

            @block.vector
            def _(e):
                T.replay('dve', e)

            @block.gpsimd
            def _(e):
                T.replay('pool', e)
    return nc


def make_consts():
    ident = np.eye(128, dtype=np.float32)
    tri = (np.arange(128)[None, :] >= np.arange(128)[:, None]).astype(np.float32)
    vm = np.broadcast_to((np.arange(128) >= 112).astype(np.float32)[None, :], (128, 128))
    return np.ascontiguousarray(np.concatenate([ident, tri, vm], axis=1))


def prep_shared(pre_g, w_in, b_f, w_a2, b_a, gla_norm_g, conv_w, conv_b, w_r, b_r, w_i, b_i, lru_lambda, w_out, post_g):
    f = lambda a: np.ascontiguousarray(np.asarray(a, dtype=np.float32))
    pp = lambda a, n: f(a).reshape(NL, n, 128).transpose(2, 0, 1).reshape(128, NL * n)
    vec = np.zeros((128, NV), np.float32)
    vec[:, V_PREG:V_PREG + 16] = pp(pre_g, 8)
    vec[:, V_NBA:V_NBA + 4] = pp(b_a, 2)
    vec[:, V_GNG:V_GNG + 8] = pp(gla_norm_g, 4)
    cw = f(conv_w).reshape(NL, 4, 8, 128).transpose(3, 0, 2, 1).reshape(128, NL * 8 * 4)
    vec[:, V_CW:V_CW + 64] = cw
    vec[:, V_CB:V_CB + 16] = pp(conv_b, 8)
    vec[:, V_BR:V_BR + 16] = pp(b_r, 8)
    vec[:, V_BI:V_BI + 16] = pp(b_i, 8)
    vec[:, V_LAM:V_LAM + 16] = pp(lru_lambda, 8)
    shared = {
        "w_in": f(w_in), "w_out": f(w_out), "vec128": vec,
        "bf_in": f(np.asarray(b_f).T),
        "wa2_in": f(np.asarray(w_a2).transpose(1, 0, 2).reshape(16, NL * 256)),
        "wr_in": f(np.asarray(w_r).transpose(2, 0, 1, 3).reshape(128, NL * 8 * 128)),
        "wi_in": f(np.asarray(w_i).transpose(2, 0, 1, 3).reshape(128, NL * 8 * 128)),
        "pg_in": f(np.broadcast_to(np.asarray(post_g)[:, None, :], (NL, 128, D))),
        "cst_in": make_consts(),
    }
    return shared


def kernel(x, meta, pre_g, w_in, b_f, w_a2, b_a, gla_norm_g, conv_w, conv_b,
           w_r, b_r, w_i, b_i, lru_lambda, w_out, post_g):
    x = np.asarray(x, dtype=np.float32)
    B, S, _ = x.shape
    NT = S // 128 + 1
    shared = prep_shared(pre_g, w_in, b_f, w_a2, b_a, gla_norm_g, conv_w, conv_b, w_r, b_r, w_i, b_i, lru_lambda, w_out, post_g)
    head = np.concatenate([np.zeros((112, D), np.float32), np.asarray(meta, dtype=np.float32)], axis=0)
    real = [0, 1, 4, 5][:B]
    zero_map = {k: np.zeros_like(v) for k, v in shared.items()}
    zero_map["h0"] = np.zeros((NT * 128, D), np.float32)
    in_maps = []
    for c in range(8):
        if c in real:
            m = dict(shared)
            m["h0"] = np.ascontiguousarray(np.concatenate([head, x[real.index(c)]], axis=0))
        else:
            m = zero_map
        in_maps.append(m)
    nc = build_nc(NT)
    res = run_bass_kernel_spmd(nc, in_maps, core_ids=list(range(8)))
    return np.stack([res.results[real[b]]["out"] for b in range(B)], axis=0).astype(np.float32)
```

```python
from contextlib import ExitStack
import os
import numpy as np
import concourse.bass as bass
import concourse.mybir as mybir
from concourse.bass_utils import run_bass_kernel_spmd

F32 = mybir.dt.float32
BF16 = mybir.dt.bfloat16
AF = mybir.ActivationFunctionType
ALU = mybir.AluOpType

D = 1024
DIN = 5656
DMIX = 2048
NL = 2
EPS = 1e-6
C_FQ, C_FK, C_FV, C_FF, C_FG = 0, 512, 1024, 1536, 1544
C_GQ, C_GK, C_GV, C_GA, C_GG = 2056, 2312, 2568, 3080, 3096
C_LX, C_LG = 3608, 4632
NDS = 24
LV = int(os.environ.get('DBG_P3', '99'))

V_PREG = 0
V_NBA = V_PREG + 16
V_GNG = V_NBA + 4
V_CW = V_GNG + 8
V_CB = V_CW + 64
V_BR = V_CB + 16
V_BI = V_BR + 16
V_LAM = V_BI + 16
NV = V_LAM + 16


class _Rec:
    def __getattr__(self, name):
        def f(*a, **k):
            return (name, a, k)
        return f


_REC = _Rec()


class TR:
    def __init__(self, nc, es):
        self.nc = nc
        self.engs = ['pe', 'act', 'dve', 'pool', 'sp']
        self.q = {e: [] for e in self.engs}
        self.sem = {}
        self.cnt = {}
        for j, e in enumerate(self.engs):
            self.sem[e] = nc.monotonic_semaphore(j).sem()
            self.cnt[e] = 0
        for i in range(NDS):
            n = 'd%d' % i
            self.sem[n] = nc.monotonic_semaphore(len(self.engs) + i).sem()
            self.cnt[n] = 0
        self.dnext = 0
        self.waited = {e: {} for e in self.engs}
        self.W = {}
        self.R = {}
        self.G = {}

    def _deps(self, reads, writes):
        d = {}

        def add(m):
            for k, v in m.items():
                if d.get(k, 0) < v:
                    d[k] = v
        for k in reads:
            add(self.W.get(k, {}))
        for k in writes:
            if self.R.get(k):
                g = dict(self.R[k])
                for kk, vv in self.W.get(k, {}).items():
                    if g.get(kk, 0) < vv:
                        g[kk] = vv
                self.G[k] = g
                self.W[k] = {}
                self.R[k] = {}
            add(self.G.get(k, {}))
        return d

    def _commit(self, reads, writes, ev):
        sem, val = ev
        for k in writes:
            self.W.setdefault(k, {})[sem] = val
        for k in reads:
            self.R.setdefault(k, {})[sem] = val

    def _filter(self, eng, d):
        waits = []
        for sem, val in d.items():
            if sem == eng and eng == 'pe':
                continue
            if self.waited[eng].get(sem, 0) >= val:
                continue
            self.waited[eng][sem] = val
            waits.append((sem, val))
        return waits

    def op(self, eng, fn, reads=(), writes=()):
        fn = fn(_REC)
        d = self._deps(reads, writes)
        waits = self._filter(eng, d)
        self.cnt[eng] += 1
        ev = (eng, self.cnt[eng])
        self.q[eng].append((waits, fn, ev, 1))
        self._commit(reads, writes, ev)

    def dma(self, eng, out, in_, reads=(), writes=()):
        d = self._deps(reads, writes)
        ds = 'd%d' % self.dnext
        self.dnext = (self.dnext + 1) % NDS
        if self.cnt[ds] > 0:
            d[ds] = max(d.get(ds, 0), self.cnt[ds])
        waits = self._filter(eng, d)
        self.cnt[ds] += 16
        ev = (ds, self.cnt[ds])
        self.q[eng].append((waits, ('dma_start', (), dict(out=out, in_=in_)), ev, 16))
        self._commit(reads, writes, ev)

    def barrier(self):
        snap = dict(self.cnt)
        for e in self.engs:
            d = {k: v for k, v in snap.items() if v > 0 and k != e}
            waits = self._filter(e, d)
            self.cnt[e] += 1
            self.q[e].append((waits, ('nop', (), {}), (e, self.cnt[e]), 1))
        snap = dict(self.cnt)
        for e in self.engs:
            d = {k: snap[k] for k in self.engs if k != e}
            waits = self._filter(e, d)
            self.cnt[e] += 1
            self.q[e].append((waits, ('nop', (), {}), (e, self.cnt[e]), 1))
        self.W = {}
        self.R = {}
        self.G = {}

    def replay(self, eng, e):
        for waits, fn, (s, v), inc in self.q[eng]:
            for ws, wv in waits:
                e.wait_ge(self.sem[ws], wv)
            getattr(e, fn[0])(*fn[1], **fn[2]).then_inc(self.sem[s], inc)


class Arena:
    def __init__(self, handle, nwords):
        self.h = handle
        self.n = nwords
        self.base = 0
        self.top = 0

    def persist(self):
        self.base = self.top

    def reset(self):
        self.top = self.base

    def alloc(self, free_elems, dtype, parts=128):
        words = (free_elems * (2 if dtype == BF16 else 4) + 3) // 4
        words = (words + 7) // 8 * 8
        a = self.top
        self.top += words
        assert self.top <= self.n, ("SBUF arena overflow", self.top, self.n)
        v = self.h[:, a:a + words]
        if dtype == BF16:
            v = v.bitcast(BF16)
        return v[0:parts, 0:free_elems]


def build_nc(NT, phases=('P1', 'P2', 'P3', 'P4', 'P5'), layers=(0, 1), debug_out=False):
    L = NT * 128
    nc = bass.Bass("TRN2", target_bir_lowering=False, monotonic_sem_count=NDS + 8)
    dt = lambda n, s, d, k="Internal": nc.dram_tensor(n, s, d, kind=k).ap()
    h0 = dt("h0", [L, D], F32, "ExternalInput")
    w_in = dt("w_in", [NL, D, DIN], F32, "ExternalInput")
    w_out = dt("w_out", [NL, DMIX, D], F32, "ExternalInput")
    vec128 = dt("vec128", [128, NV], F32, "ExternalInput")
    bf_in = dt("bf_in", [8, NL], F32, "ExternalInput")
    wa2_in = dt("wa2_in", [16, NL * 256], F32, "ExternalInput")
    wr_in = dt("wr_in", [128, NL * 8 * 128], F32, "ExternalInput")
    wi_in = dt("wi_in", [128, NL * 8 * 128], F32, "ExternalInput")
    pg_in = dt("pg_in", [NL, 128, D], F32, "ExternalInput")
    cst_in = dt("cst_in", [128, 3 * 128], F32, "ExternalInput")
    out = dt("out", [L - 128, D], F32, "ExternalOutput")
    kind_s = "ExternalOutput" if debug_out else "Internal"
    qa = dt("qa", [8, 68, L], BF16, kind_s)
    ka = dt("ka", [8, 68, L], BF16, kind_s)
    va = dt("va", [8, L, 65], BF16, kind_s)
    sgf = dt("sgf", [512, L], BF16, kind_s)
    gqT = dt("gqT", [256, L], F32, kind_s)
    gkT = dt("gkT", [256, L], F32, kind_s)
    gaT = dt("gaT", [16, L], BF16, kind_s)
    gv = dt("gv", [L, 512], BF16, kind_s)
    sgg = dt("sgg", [512, L], BF16, kind_s)
    lxT = dt("lxT", [1024, L], F32, kind_s)
    slg = dt("slg", [1024, L], BF16, kind_s)
    yT = dt("yT", [DMIX, L], BF16, kind_s)
    h1 = dt("h1", [L, D], F32, kind_s)

    AW = 51 * 1024 + 512
    arena_h = nc.alloc_sbuf_tensor("arena", [128, AW], F32)
    A = Arena(arena_h, AW)
    psF = [nc.alloc_psum_tensor("psf%d" % i, [128, 512], F32)[:, :] for i in range(6)]
    psB = [nc.alloc_psum_tensor("psb%d" % i, [128, 1024], BF16)[:, :] for i in range(2)]

    assert (NT - 1) % 4 == 0
    STS = [[0]] + [list(range(1 + 4 * j, 5 + 4 * j)) for j in range((NT - 1) // 4)]

    with ExitStack() as es:
        T = TR(nc, es)
        op, dma = T.op, T.dma

        vec = A.alloc(NV, F32)
        cst = A.alloc(384, F32)
        ident = A.alloc(128, BF16)
        tri4 = A.alloc(512, BF16)
        vm0 = A.alloc(128, F32)
        ones_bf = A.alloc(512, BF16)
        ones_f = A.alloc(512, F32)
        onesN = A.alloc(128, BF16)
        nbf = A.alloc(NL, F32, 8)
        wa2 = A.alloc(NL * 256, BF16, 16)
        sp8 = A.alloc(16, F32)
        sp16 = A.alloc(16, F32)
        nba = A.alloc(4, F32)
        zeros_bf = A.alloc(8 * 65, BF16)
        tmpc = A.alloc(NL * 256, F32)
        hsp8 = A.alloc(16, F32)
        hsp16 = A.alloc(16, F32)
        hbr = A.alloc(16, F32)
        hbi = A.alloc(16, F32)
        A.persist()

        dma('sp', vec, vec128, writes=['vec'])
        dma('sp', cst, cst_in, writes=['cst'])
        dma('sp', tmpc[0:8, 0:NL], bf_in, writes=['tmpbf'])
        op('dve', lambda e: e.tensor_scalar(out=nbf, in0=tmpc[0:8, 0:NL], scalar1=-1.0, scalar2=None, op0=ALU.mult),
           reads=['tmpbf'], writes=['nbf'])
        op('dve', lambda e: e.tensor_copy(out=ident, in_=cst[:, 0:128]), reads=['cst'], writes=['ident'])
        for r in range(4):
            op('dve', lambda e, r=r: e.tensor_copy(out=tri4[:, r * 128:(r + 1) * 128], in_=cst[:, 128:256]),
               reads=['cst'], writes=['tri4'])
        op('dve', lambda e: e.tensor_copy(out=vm0, in_=cst[:, 256:384]), reads=['cst'], writes=['vm0'])
        op('dve', lambda e: e.memset(ones_bf, 1.0), writes=['ones'])
        op('dve', lambda e: e.memset(ones_f, 1.0), writes=['ones'])
        op('dve', lambda e: e.memset(onesN, 1.0 / 128), writes=['ones'])
        op('dve', lambda e: e.memset(zeros_bf, 0.0), writes=['ones'])
        op('dve', lambda e: e.tensor_scalar(out=nba, in0=vec[:, V_NBA:V_NBA + 4], scalar1=-1.0, scalar2=None, op0=ALU.mult),
           reads=['vec'], writes=['nba'])
        op('act', lambda e: e.activation(out=sp8, in_=vec[:, V_LAM:V_LAM + 16], func=AF.Exp, scale=-1.0),
           reads=['vec'], writes=['sp8'])
        op('act', lambda e: e.activation(out=sp8, in_=sp8, func=AF.Ln, bias=1.0), reads=['sp8'], writes=['sp8'])
        op('dve', lambda e: e.tensor_scalar(out=sp16, in0=sp8, scalar1=-16.0, scalar2=None, op0=ALU.mult),
           reads=['sp8'], writes=['sp16'])
        op('dve', lambda e: e.tensor_scalar(out=sp8, in0=sp8, scalar1=-8.0, scalar2=None, op0=ALU.mult),
           reads=['sp8', 'sp16'], writes=['sp8'])
        op('dve', lambda e: e.tensor_scalar(out=hsp8, in0=sp8, scalar1=0.5, scalar2=None, op0=ALU.mult), reads=['sp8'], writes=['hsp'])
        op('dve', lambda e: e.tensor_scalar(out=hsp16, in0=sp16, scalar1=0.5, scalar2=None, op0=ALU.mult), reads=['sp16'], writes=['hsp'])
        op('dve', lambda e: e.tensor_scalar(out=hbr, in0=vec[:, V_BR:V_BR + 16], scalar1=0.5, scalar2=None, op0=ALU.mult), reads=['vec'], writes=['hsp'])
        op('dve', lambda e: e.tensor_scalar(out=hbi, in0=vec[:, V_BI:V_BI + 16], scalar1=0.5, scalar2=None, op0=ALU.mult), reads=['vec'], writes=['hsp'])
        dma('sp', tmpc[0:16, :], wa2_in, reads=['nbf'], writes=['tmpwa'])
        op('dve', lambda e: e.tensor_copy(out=wa2, in_=tmpc[0:16, :]), reads=['tmpwa'], writes=['wa2'])
        for h in range(8):
            for r in (65, 66, 67):
                dma('sp', qa[h, r:r + 1, :].rearrange("o (a b) -> (o a) b", a=NT), ones_bf[0:NT, 0:128],
                    reads=['ones'], writes=['qa_ones'])
            dma('sp', ka[h, 64:65, :].rearrange("o (a b) -> (o a) b", a=NT), ones_bf[0:NT, 0:128],
                reads=['ones'], writes=['ka_ones'])
        T.barrier()

        for l in layers:
            hsrc = h0 if l == 0 else h1
            def p4_units():
                NB = 3
                gbank = [(psF[3], psF[4]), (psF[5], psB[1].bitcast(F32))]
                gkey = [('psF3', 'psF4'), ('psF5', 'psB1')]
                wrb = A.alloc(8 * 128, BF16).rearrange("p (b n) -> p b n", b=8)
                wib = A.alloc(8 * 128, BF16).rearrange("p (b n) -> p b n", b=8)
                wst = p1stage[0][:, 0:512]
                lxe = [A.alloc(3 + 512, F32) for _ in range(NB)]
                sl = [A.alloc(512, BF16) for _ in range(NB)]
                xc = [A.alloc(512, F32) for _ in range(NB)] + [p1stage[0][:, 0:512], p1stage[1][:, 0:512]]
                xcb = [A.alloc(512, BF16) for _ in range(2)]
                rr = [A.alloc(512, F32) for _ in range(2)]
                ig = [A.alloc(512, F32) for _ in range(2)]
                aa = [A.alloc(512, F32) for _ in range(2)]
                a2 = rr
                hh = [A.alloc(512, F32) for _ in range(2)]
                hprev = A.alloc(8, F32)
                yl = [A.alloc(512, BF16) for _ in range(NB)]
                for hf in range(2):
                    for (wsrc, wdst, kk) in ((wr_in, wrb, 'wrb'), (wi_in, wib, 'wib')):
                        dma('sp', wst, wsrc[:, l * 1024 + hf * 512:l * 1024 + (hf + 1) * 512], reads=['stage0'], writes=['stage0'])
                        op('dve', lambda e: e.tensor_copy(out=wdst.rearrange("p b n -> p (b n)")[:, hf * 512:(hf + 1) * 512], in_=wst),
                           reads=['stage0'], writes=[kk])
                units = [(J, bl) for J in range(len(STS)) for bl in range(8)]
                yield

                def geom(u):
                    J, bl = units[u]
                    tiles = STS[J]
                    return J, bl, len(tiles) * 128, tiles[0] * 128, u % NB, u % 2, u % 5

                def stA(u, part):
                    J, bl, ntok, q0, s, s2, s5 = geom(u)
                    rows = slice(bl * 128, (bl + 1) * 128)
                    kl = 'lxe%d' % s
                    if part == 0 and J == 0:
                        op('dve', lambda e: e.memset(lxe[s][:, 0:3], 0.0), writes=[kl])
                        dma('sp', lxe[s][:, 3:3 + ntok], lxT[rows, q0:q0 + ntok], reads=[('lx', J)], writes=[kl])
                    elif part == 0:
                        dma('sp', lxe[s][:, 0:3 + ntok], lxT[rows, q0 - 3:q0 + ntok], reads=[('lx', J), ('lx', J - 1)], writes=[kl])
                    cwi = V_CW + (l * 8 + bl) * 4
                    cbi = V_CB + l * 8 + bl
                    kx = 'xc%d' % s5
                    if part == 0:
                      op('dve', lambda e: e.tensor_scalar(out=xc[s5][:, 0:ntok], in0=lxe[s][:, 3:3 + ntok], scalar1=vec[:, cwi + 3:cwi + 4],
                                                        scalar2=vec[:, cbi:cbi + 1], op0=ALU.mult, op1=ALU.add), reads=[kl, 'vec'], writes=[kx])
                    for k in ((2,) if part == 0 else (1, 0)):
                        op('dve', lambda e: e.scalar_tensor_tensor(out=xc[s5][:, 0:ntok], in0=lxe[s][:, k:k + ntok], scalar=vec[:, cwi + k:cwi + k + 1],
                                                                   in1=xc[s5][:, 0:ntok], op0=ALU.mult, op1=ALU.add), reads=[kl, 'vec', kx], writes=[kx])
                    if part == 1 and J == 0:
                        op('dve', lambda e: e.tensor_tensor(out=xc[s5][:, 0:128], in0=xc[s5][:, 0:128], in1=vm0, op=ALU.mult),
                           reads=[kx, 'vm0'], writes=[kx])

                def stA2(u):
                    J, bl, ntok, q0, s, s2, s5 = geom(u)
                    op('act', lambda e: e.activation(out=xcb[s2][:, 0:ntok], in_=xc[s5][:, 0:ntok], func=AF.Copy), reads=['xc%d' % s5], writes=['xcb%d' % s2])

                def stB(u):
                    J, bl, ntok, q0, s, s2, s5 = geom(u)
                    rps, ips = gbank[s2]
                    kr, ki = gkey[s2]
                    rows = slice(bl * 128, (bl + 1) * 128)
                    dma('sp', sl[s][:, 0:ntok], slg[rows, q0:q0 + ntok], reads=[('lx', J)], writes=['sl%d' % s])
                    op('pe', lambda e: e.matmul(rps[:, 0:ntok], lhsT=wrb[:, bl, :], rhs=xcb[s2][:, 0:ntok], start=True, stop=True),
                       reads=['wrb', 'xcb%d' % s2], writes=[kr])
                    op('pe', lambda e: e.matmul(ips[:, 0:ntok], lhsT=wib[:, bl, :], rhs=xcb[s2][:, 0:ntok], start=True, stop=True),
                       reads=['wib', 'xcb%d' % s2], writes=[ki])

                def stB2(u, part):
                    J, bl, ntok, q0, s, s2, s5 = geom(u)
                    rps, ips = gbank[s2]
                    kr, ki = gkey[s2]
                    bri = V_BR + l * 8 + bl
                    bii = V_BI + l * 8 + bl
                    li = l * 8 + bl
                    if part == 1:
                        for (o_, sc_) in ((aa[s2], hsp8), (a2[s2], hsp16)):
                            op('act', lambda e: e.activation(out=o_[:, 0:ntok], in_=rr[s2][:, 0:ntok], func=AF.Exp,
                                                             scale=sc_[:, li:li + 1], bias=sc_[:, li:li + 1]),
                               reads=['rr%d' % s2, 'hsp'], writes=['aa%d' % s2 if o_ is aa[s2] else 'rr%d' % s2])
                        op('act', lambda e: e.activation(out=a2[s2][:, 0:ntok], in_=a2[s2][:, 0:ntok], func=AF.Ln, scale=-1.0, bias=1.0),
                           reads=['rr%d' % s2], writes=['rr%d' % s2])
                        op('act', lambda e: e.activation(out=a2[s2][:, 0:ntok], in_=a2[s2][:, 0:ntok], func=AF.Exp, scale=0.5),
                           reads=['rr%d' % s2], writes=['rr%d' % s2])
                        return
                    op('act', lambda e: e.activation(out=rr[s2][:, 0:ntok], in_=rps[:, 0:ntok], func=AF.Tanh, scale=0.5, bias=hbr[:, li:li + 1]),
                       reads=[kr, 'hsp'], writes=['rr%d' % s2])
                    op('act', lambda e: e.activation(out=ig[s2][:, 0:ntok], in_=ips[:, 0:ntok], func=AF.Tanh, scale=0.5, bias=hbi[:, li:li + 1]),
                       reads=[ki, 'hsp'], writes=['ig%d' % s2])

                def stC(u):
                    J, bl, ntok, q0, s, s2, s5 = geom(u)
                    op('pool', lambda e: e.tensor_scalar(out=ig[s2][:, 0:ntok], in0=ig[s2][:, 0:ntok], scalar1=1.0, scalar2=0.5, op0=ALU.add, op1=ALU.mult),
                       reads=['ig%d' % s2], writes=['ig%d' % s2])
                    op('pool', lambda e: e.tensor_tensor(out=ig[s2][:, 0:ntok], in0=ig[s2][:, 0:ntok], in1=xc[s5][:, 0:ntok], op=ALU.mult),
                       reads=['ig%d' % s2, 'xc%d' % s5], writes=['ig%d' % s2])
                    op('pool', lambda e: e.tensor_tensor(out=ig[s2][:, 0:ntok], in0=ig[s2][:, 0:ntok], in1=a2[s2][:, 0:ntok], op=ALU.mult),
                       reads=['ig%d' % s2, 'rr%d' % s2], writes=['ig%d' % s2])
                    init = 0.0 if J == 0 else hprev[:, bl:bl + 1]
                    op('dve', lambda e: e.tensor_tensor_scan(out=hh[s2][:, 0:ntok], data0=aa[s2][:, 0:ntok], data1=ig[s2][:, 0:ntok],
                                                             initial=init, op0=ALU.mult, op1=ALU.add),
                       reads=['aa%d' % s2, 'ig%d' % s2, 'hprev'], writes=['hh%d' % s2])
                    op('dve', lambda e: e.tensor_copy(out=hprev[:, bl:bl + 1], in_=hh[s2][:, ntok - 1:ntok]), reads=['hh%d' % s2], writes=['hprev'])
                    op('dve', lambda e: e.tensor_tensor(out=yl[s][:, 0:ntok], in0=hh[s2][:, 0:ntok], in1=sl[s][:, 0:ntok], op=ALU.mult),
                       reads=['hh%d' % s2, 'sl%d' % s], writes=['yl%d' % s])

                def stD(u):
                    J, bl, ntok, q0, s, s2, s5 = geom(u)
                    dma('sp', yT[1024 + bl * 128:1024 + (bl + 1) * 128, q0:q0 + ntok], yl[s][:, 0:ntok], reads=['yl%d' % s], writes=['scr'])

                n = len(units)
                for t in range(n + 5):
                    pieces = []
                    if 0 <= t - 5 < n:
                        pieces.append((stD, (t - 5,)))
                    if 0 <= t - 4 < n:
                        pieces.append((stC, (t - 4,)))
                    if 0 <= t - 3 < n:
                        pieces.append((stB2, (t - 3, 0)))
                        pieces.append((stB2, (t - 3, 1)))
                    if 0 <= t - 2 < n:
                        pieces.append((stB, (t - 2,)))
                    if 0 <= t - 1 < n:
                        pieces.append((stA2, (t - 1,)))
                    if t < n:
                        pieces.append((stA, (t, 0)))
                        pieces.append((stA, (t, 1)))
                    for i, (f_, a_) in enumerate(pieces):
                        f_(*a_)
                        p4n[0] = t if i + 1 < len(pieces) else t + 1
                        yield

            if 'P1' in phases:
                A.reset()
                W = A.alloc(8 * DIN, BF16).rearrange("p (c n) -> p c n", c=8)
                SC = 808
                stage = [A.alloc(SC, F32) for _ in range(2)]
                p1stage = stage
                hb = [A.alloc(D, F32) for _ in range(2)]
                junk = A.alloc(D, BF16)
                ssq = [A.alloc(1, F32) for _ in range(3)]
                hn = [A.alloc(D, BF16) for _ in range(2)]
                hnT = [A.alloc(8 * 512, BF16).rearrange("p (c n) -> p c n", c=8) for _ in range(2)]
                evf = [A.alloc(512, F32) for _ in range(4)]
                evb = [A.alloc(512, BF16) for _ in range(4)]
                vt = [A.alloc(8 * 65, BF16).rearrange("p (h c) -> p h c", h=8) for _ in range(2)]
                gvt = [A.alloc(512, BF16) for _ in range(2)]
                ffs = A.alloc(512, F32, 8)
                spf = ffs
                g4 = p4_units()
                p4n = [0]
                gcount = [0]

                def p4adv(k, lim):
                    for _ in range(k):
                        if p4n[0] < lim:
                            next(g4)
                pstores = []

                def sdma(dst, src_, reads=(), writes=()):
                    pstores.append((dst, src_, reads, writes))

                def sflush(keep):
                    while len(pstores) > keep:
                        d_, s_, r_, w_ = pstores.pop(0)
                        dma('sp', d_, s_, reads=r_, writes=w_)
                cc = A.alloc(512, F32, 8)
                cprev = A.alloc(1, F32, 8)
                cq = A.alloc(512, BF16, 8)
                kp = A.alloc(3 * 512, BF16, 8).rearrange("p (r n) -> p r n", r=3)
                r1 = A.alloc(512, F32, 8)
                gab = A.alloc(512, BF16, 16)
                si = 0
                for c in range(8):
                    for s0 in range(0, DIN, SC):
                        st = stage[si % 2]
                        k = 'stage%d' % (si % 2)
                        si += 1
                        dma('sp', st, w_in[l, c * 128:(c + 1) * 128, s0:s0 + SC], writes=[k])
                        op('dve', lambda e, st=st, c=c, s0=s0: e.tensor_scalar(
                            out=W[:, c, s0:s0 + SC], in0=st, scalar1=vec[:, V_PREG + l * 8 + c:V_PREG + l * 8 + c + 1],
                            scalar2=None, op0=ALU.mult), reads=[k, 'vec'], writes=['W'])
                for hh in range(2):
                    op('dve', lambda e, hh=hh: e.memset(vt[hh][:, :, 64:65], 1.0), writes=['vt%d' % hh])
                psi = 0
                evi = 0
                tcount = 0
                pinfo = {}

                def prep_a(Jp, ti):
                    nonlocal tcount
                    t = STS[Jp][ti]
                    b = hb[tcount % 2]
                    kb = 'hb%d' % (tcount % 2)
                    sq = ssq[tcount % 3]
                    ksq = 'ssq%d' % (tcount % 3)
                    n_ = hn[tcount % 2]
                    kn = 'hn%d' % (tcount % 2)
                    pinfo[(Jp, ti)] = (n_, kn)
                    tcount += 1
                    dma('sp', b, hsrc[t * 128:(t + 1) * 128, :], writes=[kb])
                    op('act', lambda e: e.activation(out=junk, in_=b, func=AF.Square, accum_out=sq), reads=[kb], writes=['junk', ksq])
                    op('act', lambda e: e.activation(out=sq, in_=sq, func=AF.Ln, scale=1.0 / D, bias=EPS), reads=[ksq], writes=[ksq])
                    op('act', lambda e: e.activation(out=sq, in_=sq, func=AF.Exp, scale=-0.5), reads=[ksq], writes=[ksq])
                    op('dve', lambda e: e.tensor_scalar(out=n_, in0=b, scalar1=sq, scalar2=None, op0=ALU.mult), reads=[kb, ksq], writes=[kn])

                def prep_b(Jp, ti):
                    n_, kn = pinfo.pop((Jp, ti))
                    X = hnT[Jp % 2]
                    kX = 'hnT%d' % (Jp % 2)
                    pT = psB[0]
                    kpT = 'psB0'
                    for c in range(8):
                        op('pe', lambda e: e.transpose(out=pT[:, c * 128:(c + 1) * 128], in_=n_[:, c * 128:(c + 1) * 128], identity=ident),
                           reads=[kn, 'ident'], writes=[kpT])
                    if ti % 2 == 0:
                        op('act', lambda e: e.activation(out=X[:, :, ti * 128:(ti + 1) * 128], in_=pT.rearrange("p (c n) -> p c n", c=8), func=AF.Copy),
                           reads=[kpT], writes=[kX])
                    else:
                        op('dve', lambda e: e.tensor_copy(out=X[:, :, ti * 128:(ti + 1) * 128], in_=pT.rearrange("p (c n) -> p c n", c=8)),
                           reads=[kpT], writes=[kX])

                next(g4)
                for J, tiles in enumerate(STS):
                    ntl = len(tiles)
                    ntok = ntl * 128
                    q0 = tiles[0] * 128
                    X = hnT[J % 2]
                    kX = 'hnT%d' % (J % 2)
                    if J == 0:
                        prep_a(0, 0)
                        prep_b(0, 0)
                    gl = [0]
                    def fm_group(c0, M, evac):
                        nonlocal psi
                        ps = psF[psi % 3]
                        kps = 'psF%d' % (psi % 3)
                        psi += 1
                        gcount[0] += 1
                        gl[0] += 1
                        if J + 1 < len(STS) and gl[0] in (4, 12, 20, 28):
                            prep_a(J + 1, (gl[0] - 4) // 8)
                        if J + 1 < len(STS) and gl[0] in (8, 16, 24, 32):
                            prep_b(J + 1, (gl[0] - 8) // 8)
                        sflush(3)
                        p4adv(2 if gcount[0] % 2 == 0 else 1, 8 * J)
                        for kc in range(8):
                            op('pe', lambda e, ps=ps, kc=kc: e.matmul(ps[0:M, 0:ntok], lhsT=W[:, kc, c0:c0 + M],
                                                                     rhs=X[:, kc, 0:ntok], start=(kc == 0), stop=(kc == 7)),
                               reads=['W', kX], writes=[kps])
                        evac(ps[0:M, 0:ntok], kps)

                    def ev_copy(dst_list, scale=None, dtype=BF16, eng='dve', wkey='scr'):
                        def f(ps, kps):
                            nonlocal evi
                            M = ps.shape[0]
                            buf = (evb if dtype == BF16 else evf)[evi % 4]
                            kb_ = ('evb%d' if dtype == BF16 else 'evf%d') % (evi % 4)
                            evi += 1
                            o = buf[0:M, 0:ntok]
                            if eng == 'silu':
                                op('act', lambda e: e.activation(out=o, in_=ps, func=AF.Silu), reads=[kps], writes=[kb_])
                            elif scale is not None:
                                op('dve', lambda e: e.tensor_scalar(out=o, in0=ps, scalar1=scale, scalar2=None, op0=ALU.mult),
                                   reads=[kps], writes=[kb_])
                            else:
                                op('dve', lambda e: e.tensor_copy(out=o, in_=ps), reads=[kps], writes=[kb_])
                            for (r0, nr, dst) in dst_list:
                                sdma(dst, buf[r0:r0 + nr, 0:ntok], reads=[kb_], writes=[wkey])
                        return f

                    tk = slice(q0, q0 + ntok)
                    def ev_ff(ps, kps):
                        op('dve', lambda e: e.tensor_copy(out=ffs[:, 0:ntok], in_=ps), reads=[kps], writes=['ffs'])
                    fm_group(C_FF, 8, ev_ff)
                    op('act', lambda e: e.activation(out=spf[:, 0:ntok], in_=ffs[:, 0:ntok], func=AF.Exp,
                                                     bias=nbf[:, l:l + 1], scale=-1.0), reads=['ffs', 'nbf'], writes=['ffs'])
                    op('act', lambda e: e.activation(out=spf[:, 0:ntok], in_=spf[:, 0:ntok], func=AF.Ln, bias=1.0),
                       reads=['ffs'], writes=['ffs'])
                    if J == 0:
                        op('dve', lambda e: e.tensor_tensor(out=spf[:, 0:128], in0=spf[:, 0:128], in1=vm0[0:8, :], op=ALU.mult),
                           reads=['ffs', 'vm0'], writes=['ffs'])
                    init = 0.0 if J == 0 else cprev
                    op('dve', lambda e, init=init: e.tensor_tensor_scan(out=cc[:, 0:ntok], data0=ones_f[0:8, 0:ntok], data1=spf[:, 0:ntok],
                                                                        initial=init, op0=ALU.mult, op1=ALU.subtract),
                       reads=['ffs', 'ones', 'cprev'], writes=['cc'])
                    op('dve', lambda e: e.tensor_copy(out=cprev, in_=cc[:, ntok - 1:ntok]), reads=['cc'], writes=['cprev'])
                    op('dve', lambda e: e.tensor_copy(out=cq[:, 0:ntok], in_=cc[:, 0:ntok]), reads=['cc'], writes=['cq'])
                    op('dve', lambda e: e.tensor_scalar(out=kp[:, 0, 0:ntok], in0=cc[:, 0:ntok], scalar1=-1.0, scalar2=None, op0=ALU.mult),
                       reads=['cc'], writes=['kp'])
                    op('dve', lambda e: e.scalar_tensor_tensor(out=r1[:, 0:ntok], in0=cc[:, 0:ntok], scalar=-1.0, in1=kp[:, 0, 0:ntok],
                                                               op0=ALU.mult, op1=ALU.subtract), reads=['cc', 'kp'], writes=['r1'])
                    op('dve', lambda e: e.tensor_copy(out=kp[:, 1, 0:ntok], in_=r1[:, 0:ntok]), reads=['r1'], writes=['kp'])
                    op('dve', lambda e: e.tensor_tensor(out=r1[:, 0:ntok], in0=r1[:, 0:ntok], in1=kp[:, 1, 0:ntok], op=ALU.subtract),
                       reads=['r1', 'kp'], writes=['r1'])
                    op('dve', lambda e: e.tensor_copy(out=kp[:, 2, 0:ntok], in_=r1[:, 0:ntok]), reads=['r1'], writes=['kp'])
                    sdma(qa[:, 64, tk], cq[:, 0:ntok], reads=['cq'], writes=['scr'])
                    sdma(ka[:, 65:68, tk], kp[:, :, 0:ntok], reads=['kp'], writes=['scr'])

                    def ev_ga(ps, kps):
                        op('dve', lambda e: e.tensor_copy(out=gab[:, 0:ntok], in_=ps), reads=[kps], writes=['gab'])
                        sdma(gaT[:, tk], gab[:, 0:ntok], reads=['gab'], writes=['scr'])
                    fm_group(C_GA, 16, ev_ga)
                    for g in range(4):
                        fm_group(C_FQ + g * 128, 128, ev_copy([(0, 64, qa[2 * g, 0:64, tk]), (64, 64, qa[2 * g + 1, 0:64, tk])], scale=0.125))
                    for g in range(4):
                        fm_group(C_FK + g * 128, 128, ev_copy([(0, 64, ka[2 * g, 0:64, tk]), (64, 64, ka[2 * g + 1, 0:64, tk])]))
                    for g in range(4):
                        fm_group(C_FG + g * 128, 128, ev_copy([(0, 128, sgf[g * 128:(g + 1) * 128, tk])], eng='silu'))
                    for g in range(2):
                        fm_group(C_GQ + g * 128, 128, ev_copy([(0, 128, gqT[g * 128:(g + 1) * 128, tk])], dtype=F32))
                    for g in range(2):
                        fm_group(C_GK + g * 128, 128, ev_copy([(0, 128, gkT[g * 128:(g + 1) * 128, tk])], dtype=F32))
                    for g in range(4):
                        fm_group(C_GG + g * 128, 128, ev_copy([(0, 128, sgg[g * 128:(g + 1) * 128, tk])], eng='silu'))
                    for g in range(8):
                        fm_group(C_LX + g * 128, 128, ev_copy([(0, 128, lxT[g * 128:(g + 1) * 128, tk])], dtype=F32, wkey=('lx', J)))
                    for g in range(8):
                        fm_group(C_LG + g * 128, 128, ev_copy([(0, 128, slg[g * 128:(g + 1) * 128, tk])], eng='silu', wkey=('lx', J)))
                    for ti, t in enumerate(tiles):
                        for which in (0, 1):
                            ps = psF[psi % 3]
                            kps = 'psF%d' % (psi % 3)
                            psi += 1
                            c0 = C_FV if which == 0 else C_GV
                            sflush(3)
                            p4adv(2, 8 * J)
                            for kc in range(8):
                                op('pe', lambda e, ps=ps, kc=kc, c0=c0, ti=ti: e.matmul(
                                    ps[:, :], lhsT=X[:, kc, ti * 128:(ti + 1) * 128], rhs=W[:, kc, c0:c0 + 512],
                                    start=(kc == 0), stop=(kc == 7)), reads=['W', kX], writes=[kps])
                            if which == 0:
                                v_ = vt[t % 2]
                                kv_ = 'vt%d' % (t % 2)
                                op('dve', lambda e, ps=ps, v_=v_: e.tensor_copy(out=v_[:, :, 0:64],
                                                                               in_=ps.rearrange("p (h c) -> p h c", h=8)),
                                   reads=[kps], writes=[kv_])
                                if t == 0:
                                    sdma(va[:, 112:128, :].rearrange("h t c -> t h c"), v_[112:128, :, :], reads=[kv_], writes=['scr'])
                                    sdma(va[:, 0:112, :].rearrange("h t c -> t h c"),
                                        zeros_bf[0:112, :].rearrange("p (h c) -> p h c", h=8), reads=['ones'], writes=['scr'])
                                else:
                                    sdma(va[:, t * 128:(t + 1) * 128, :].rearrange("h t c -> t h c"), v_, reads=[kv_], writes=['scr'])
                            else:
                                g_ = gvt[t % 2]
                                kg_ = 'gvt%d' % (t % 2)
                                op('act', lambda e, ps=ps, g_=g_: e.activation(out=g_, in_=ps, func=AF.Copy), reads=[kps], writes=[kg_])
                                sdma(gv[t * 128:(t + 1) * 128, :], g_, reads=[kg_], writes=['scr'])
                    while p4n[0] < 8 * J:
                        next(g4)
                sflush(0)
                for _ in g4:
                    pass
                T.barrier()

            if 'P2' in phases:
                A.reset()
                Qa = [A.alloc(L, BF16, 68) for _ in range(2)]
                Ka = [A.alloc(L, BF16, 68) for _ in range(2)]
                Va = [A.alloc(NT * 65, BF16).rearrange("p (n c) -> p n c", n=NT) for _ in range(2)]
                SG = [A.alloc(L, BF16, 64) for _ in range(2)]
                pt = [A.alloc(512, BF16) for _ in range(3)]
                dn = A.alloc(512, F32, 65)
                rdb = A.alloc(512, BF16, 65)
                t1 = [A.alloc(512, F32, 64) for _ in range(2)]
                yb = [A.alloc(512, BF16, 64) for _ in range(2)]
                pt.append(A.alloc(512, BF16))
                pt.append(A.alloc(512, BF16))
                bpsF = psB[1].bitcast(F32)
                gk_ = [0]
                blk_ = [0]
                for h in range(8):
                    s = h % 2
                    kQ, kK, kV, kS = 'Qa%d' % s, 'Ka%d' % s, 'Va%d' % s, 'SG%d' % s
                    dma('sp', Qa[s], qa[h], writes=[kQ])
                    dma('sp', Ka[s], ka[h], writes=[kK])
                    for n0 in range(0, NT, 8):
                        n1 = min(NT, n0 + 8)
                        dma('sp', Va[s][:, n0:n1, :], va[h, n0 * 128:n1 * 128, :].rearrange("(n p) c -> p n c", p=128), writes=[kV])
                    dma('sp', SG[s], sgf[h * 64:(h + 1) * 64, :], writes=[kS])
                    tasks = []
                    for J, tiles in enumerate(STS):
                        ob = blk_[0] % 2
                        blk_[0] += 1
                        for I in range(tiles[-1] + 1):
                            tasks.append((J, I, ob, gk_[0]))
                            gk_[0] += 1
                    pend = []

                    def emit_S(task):
                        J, I, ob, g = task
                        tiles = STS[J]
                        ntok = len(tiles) * 128
                        q0 = tiles[0] * 128
                        off = max(0, I - tiles[0]) * 128
                        sps = psF[g % 4]
                        ksps = 'psF%d' % (g % 4)
                        p_ = pt[g % 5]
                        kp_ = 'pt%d' % (g % 5)
                        op('pe', lambda e: e.matmul(sps[:, off:ntok], lhsT=Ka[s][:, I * 128:(I + 1) * 128],
                                                    rhs=Qa[s][:, q0 + off:q0 + ntok], start=True, stop=True),
                           reads=[kQ, kK], writes=[ksps])
                        op('act', lambda e: e.activation(out=p_[:, off:ntok], in_=sps[:, off:ntok], func=AF.Exp),
                           reads=[ksps], writes=[kp_])
                        if I >= tiles[0]:
                            op('dve', lambda e: e.tensor_tensor(out=p_[:, off:off + 128], in0=p_[:, off:off + 128],
                                                                 in1=tri4[:, 0:128], op=ALU.mult),
                               reads=[kp_, 'tri4'], writes=[kp_])

                    def emit_PV(task):
                        J, I, ob, g = task
                        tiles = STS[J]
                        ntok = len(tiles) * 128
                        q0 = tiles[0] * 128
                        last = tiles[-1]
                        off = max(0, I - tiles[0]) * 128
                        ops_ = psF[4 + ob]
                        kops = 'psF%d' % (4 + ob)
                        p_ = pt[g % 5]
                        kp_ = 'pt%d' % (g % 5)
                        op('pe', lambda e: e.matmul(ops_[0:65, off:ntok], lhsT=Va[s][:, I, :], rhs=p_[:, off:ntok],
                                                    start=(I == 0), stop=(I == last)), reads=[kV, kp_], writes=[kops])
                        if I != last:
                            return
                        while pend:
                            pend.pop(0)[1]()
                        op('dve', lambda e: e.tensor_scalar(out=dn[64:65, 0:ntok], in0=ops_[64:65, 0:ntok], scalar1=1e-30,
                                                            scalar2=None, op0=ALU.max), reads=[kops], writes=['dn'])
                        op('dve', lambda e: e.reciprocal(out=dn[64:65, 0:ntok], in_=dn[64:65, 0:ntok]), reads=['dn'], writes=['dn'])
                        op('dve', lambda e: e.tensor_copy(out=rdb[64:65, 0:ntok], in_=dn[64:65, 0:ntok]), reads=['dn'], writes=['rdb'])

                        def partB():
                            bps = bpsF
                            op('pe', lambda e: e.matmul(bps[0:64, 0:ntok], lhsT=ones_bf[64:65, 0:64], rhs=rdb[64:65, 0:ntok],
                                                        start=True, stop=True), reads=['rdb', 'ones'], writes=['bpsF'])
                            t_ = t1[ob]
                            kt_ = 't1%d' % ob
                            y_ = yb[ob]
                            ky_ = 'yb%d' % ob
                            op('dve', lambda e: e.tensor_tensor(out=t_[:, 0:ntok], in0=ops_[0:64, 0:ntok], in1=SG[s][:, q0:q0 + ntok],
                                                                op=ALU.mult), reads=[kops, kS], writes=[kt_])
                            op('dve', lambda e: e.tensor_tensor(out=y_[:, 0:ntok], in0=t_[:, 0:ntok], in1=bps[0:64, 0:ntok],
                                                                op=ALU.mult), reads=[kt_, 'bpsF'], writes=[ky_])
                            dma('act', yT[h * 64:(h + 1) * 64, q0:q0 + ntok], y_[:, 0:ntok], reads=[ky_], writes=['scr'])
                        pend.append([8, partB])

                    DPT = 3
                    for k in range(len(tasks) + DPT):
                        if k < len(tasks):
                            emit_S(tasks[k])
                        if k - DPT >= 0:
                            for pb in pend:
                                pb[0] -= 1
                            while pend and pend[0][0] <= 0:
                                pend.pop(0)[1]()
                            emit_PV(tasks[k - DPT])
                    while pend:
                        pend.pop(0)[1]()
                T.barrier()

            def p3_units():
                r3 = lambda n, dt_: [A.alloc(n, dt_) for _ in range(3)]
                r2 = lambda n, dt_: [A.alloc(n, dt_) for _ in range(2)]
                v3 = lambda lst, c: [x.rearrange("p (c n) -> p c n", c=c) for x in lst]
                gq = v3(r3(1024, F32), 2)
                gk = v3(r3(1024, F32), 2)
                ga_ = [A.alloc(512, BF16, 16) for _ in range(3)]
                gvs = v3(r3(2048, BF16), 4)
                sg_ = v3(r3(2048, BF16), 4)
                ee = v3([A.alloc(1024, F32)] * 2, 2)
                bs = v3([A.alloc(1024, F32)] * 2, 2)
                ebh = v3([A.alloc(1024, F32)] * 2, 2)
                ebq = v3([A.alloc(1024, F32)] * 2, 2)
                ebk = v3([A.alloc(1024, F32)] * 2, 2)
                qz = [v3([A.alloc(1024, BF16) for _ in range(2)], 2) for _ in range(3)]
                kt = v3(r3(1024, BF16), 2)
                khT = v3(r3(1024, BF16), 2)
                nbl = v3(r3(8, F32), 2)
                dec = v3(r3(8, F32), 2)
                yg = v3(r2(2048, BF16), 4)
                at = r2(512, BF16)
                kh = r2(256, BF16)
                sqb = r2(512, BF16)
                rs = r2(512, F32)
                t1g = r2(512, F32)
                S = A.alloc(2 * 128, F32).rearrange("p (c n) -> p c n", c=2)
                Sbf = A.alloc(2 * 128, BF16).rearrange("p (c n) -> p c n", c=2)
                op('dve', lambda e: e.memset(S, 0.0), writes=['S'])
                op('dve', lambda e: e.memset(Sbf, 0.0), writes=['Sbf'])
                for j in range(3):
                    for p in range(2):
                        op('dve', lambda e: e.memset(qz[j][p], 0.0), writes=['qz%d' % j])
                aps, opsb, ups = psF[2], [psF[3], psF[4]], psF[5]
                xm = psB[1].bitcast(F32)
                tps = psB[0]
                flat = [(J, tt) for J, tiles in enumerate(STS) for tt in range(len(tiles))]
                v0 = {}
                for v, (J, tt) in enumerate(flat):
                    v0.setdefault(J, v)

                def geo(J):
                    tiles = STS[J]
                    return len(tiles), len(tiles) * 128, slice(tiles[0] * 128, (tiles[-1] + 1) * 128), J % 3, J % 2

                def PA(J):
                    ntl, ntok, tk, j3, j2 = geo(J)
                    dma('sp', gq[j3][:, :, 0:ntok], gqT[:, tk].rearrange("(c p) t -> p c t", p=128), writes=['gq%d' % j3])
                    dma('sp', gk[j3][:, :, 0:ntok], gkT[:, tk].rearrange("(c p) t -> p c t", p=128), writes=['gk%d' % j3])
                    dma('sp', ga_[j3][:, 0:ntok], gaT[:, tk], writes=['ga%d' % j3])
                    dma('sp', gvs[j3][:, 0:ntl, :], gv[tk, :].rearrange("(t p) n -> p t n", p=128), writes=['gvs%d' % j3])
                    dma('sp', sg_[j3][:, :, 0:ntok], sgg[:, tk].rearrange("(c p) t -> p c t", p=128), writes=['sg%d' % j3])
                    for c in range(2):
                        op('pe', lambda e: e.matmul(xm[:, 0:ntok], lhsT=wa2[:, l * 256 + c * 128:l * 256 + (c + 1) * 128],
                                                    rhs=ga_[j3][:, 0:ntok], start=True, stop=True), reads=['wa2', 'ga%d' % j3], writes=['xm'])
                        op('act', lambda e: e.activation(out=ee[j2][:, c, 0:ntok], in_=xm[:, 0:ntok], func=AF.Exp,
                                                         bias=nba[:, l * 2 + c:l * 2 + c + 1], scale=-1.0), reads=['xm', 'nba'], writes=['ee'])
                        op('act', lambda e: e.activation(out=ee[j2][:, c, 0:ntok], in_=ee[j2][:, c, 0:ntok], func=AF.Ln, bias=1.0),
                           reads=['ee'], writes=['ee'])

                def PB(J):
                    ntl, ntok, tk, j3, j2 = geo(J)
                    for c in range(2):
                        for tt in range(ntl):
                            r = slice(tt * 128, (tt + 1) * 128)
                            op('dve', lambda e: e.tensor_tensor_scan(out=bs[j2][:, c, r], data0=ones_f[:, 0:128], data1=ee[j2][:, c, r],
                                                                     initial=0.0, op0=ALU.mult, op1=ALU.add),
                               reads=['ee', 'ones'], writes=['bs'])
                            op('dve', lambda e: e.tensor_scalar(out=nbl[j3][:, c, tt:tt + 1], in0=bs[j2][:, c, tt * 128 + 127:tt * 128 + 128],
                                                                scalar1=-1.0 / 16, scalar2=None, op0=ALU.mult),
                               reads=['bs'], writes=['nbl%d' % j3])

                def PC(J):
                    ntl, ntok, tk, j3, j2 = geo(J)
                    for c in range(2):
                        for tt in range(ntl):
                            r = slice(tt * 128, (tt + 1) * 128)
                            op('act', lambda e: e.activation(out=ebh[j2][:, c, r], in_=bs[j2][:, c, r], func=AF.Exp,
                                                             bias=nbl[j3][:, c, tt:tt + 1], scale=1.0 / 16),
                               reads=['bs', 'nbl%d' % j3], writes=['ebh'])
                        op('act', lambda e: e.activation(out=dec[j3][:, c, 0:ntl], in_=nbl[j3][:, c, 0:ntl], func=AF.Exp),
                           reads=['nbl%d' % j3], writes=['dec%d' % j3])
                        op('act', lambda e: e.activation(out=ebq[j2][:, c, 0:ntok], in_=bs[j2][:, c, 0:ntok], func=AF.Exp, scale=-1.0 / 16),
                           reads=['bs'], writes=['ebq'])
                        op('act', lambda e: e.activation(out=ebk[j2][:, c, 0:ntok], in_=bs[j2][:, c, 0:ntok], func=AF.Exp, scale=1.0 / 16),
                           reads=['bs'], writes=['ebk'])

                def PD(J):
                    ntl, ntok, tk, j3, j2 = geo(J)
                    for c in range(2):
                        op('dve', lambda e: e.tensor_tensor(out=khT[j3][:, c, 0:ntok], in0=gk[j3][:, c, 0:ntok], in1=ebh[j2][:, c, 0:ntok], op=ALU.mult),
                           reads=['gk%d' % j3, 'ebh'], writes=['khT%d' % j3])
                        for p in range(2):
                            pr = slice(p * 64, (p + 1) * 64)
                            op('dve', lambda e: e.scalar_tensor_tensor(out=qz[j3][p][pr, c, 0:ntok], in0=gq[j3][pr, c, 0:ntok], scalar=0.125,
                                                                       in1=ebq[j2][pr, c, 0:ntok], op0=ALU.mult, op1=ALU.mult),
                               reads=['gq%d' % j3, 'ebq'], writes=['qz%d' % j3])
                        op('dve', lambda e: e.tensor_tensor(out=kt[j3][:, c, 0:ntok], in0=gk[j3][:, c, 0:ntok], in1=ebk[j2][:, c, 0:ntok], op=ALU.mult),
                           reads=['gk%d' % j3, 'ebk'], writes=['kt%d' % j3])

                def TA(v):
                    J, tt = flat[v]
                    j3 = J % 3
                    u = v % 2
                    r = slice(tt * 128, (tt + 1) * 128)
                    for hd in range(4):
                        c, p = hd // 2, hd % 2
                        op('pe', lambda e: e.matmul(aps[:, hd * 128:(hd + 1) * 128], lhsT=kt[j3][:, c, r], rhs=qz[j3][p][:, c, r],
                                                    start=True, stop=True), reads=['kt%d' % j3, 'qz%d' % j3], writes=['aps'])
                    op('dve', lambda e: e.tensor_tensor(out=at[u], in0=aps[:, :], in1=tri4, op=ALU.mult), reads=['aps', 'tri4'], writes=['at%d' % u])
                    for c in range(2):
                        op('pe', lambda e: e.transpose(out=tps[:, c * 128:(c + 1) * 128], in_=khT[j3][:, c, r], identity=ident),
                           reads=['khT%d' % j3, 'ident'], writes=['tps'])
                    op('act', lambda e: e.activation(out=kh[u], in_=tps[:, 0:256], func=AF.Copy), reads=['tps'], writes=['kh%d' % u])

                def TB(v):
                    J, tt = flat[v]
                    j3 = J % 3
                    u = v % 2
                    r = slice(tt * 128, (tt + 1) * 128)
                    ops_ = opsb[u]
                    ko = 'ops%d' % u
                    for hd in range(4):
                        c, p = hd // 2, hd % 2
                        op('pe', lambda e: e.matmul(ops_[:, hd * 128:(hd + 1) * 128], lhsT=gvs[j3][:, tt, hd * 128:(hd + 1) * 128],
                                                    rhs=at[u][:, hd * 128:(hd + 1) * 128], start=True, stop=False),
                           reads=['gvs%d' % j3, 'at%d' % u], writes=[ko])
                        op('pe', lambda e: e.matmul(ops_[:, hd * 128:(hd + 1) * 128], lhsT=Sbf[:, c, :], rhs=qz[j3][p][:, c, r],
                                                    start=False, stop=True), reads=['Sbf', 'qz%d' % j3], writes=[ko])
                    for c in range(2):
                        op('pe', lambda e: e.matmul(ups[:, c * 256:(c + 1) * 256], lhsT=kh[u][:, c * 128:(c + 1) * 128],
                                                    rhs=gvs[j3][:, tt, c * 256:(c + 1) * 256], start=True, stop=True),
                           reads=['kh%d' % u, 'gvs%d' % j3], writes=['ups'])
                    for hd in range(4):
                        c, p = hd // 2, hd % 2
                        pr = slice(p * 64, (p + 1) * 64)
                        op('dve', lambda e: e.scalar_tensor_tensor(out=S[pr, c, :], in0=S[pr, c, :], scalar=dec[j3][pr, c, tt:tt + 1],
                                                                   in1=ups[pr, c * 256 + p * 128:c * 256 + (p + 1) * 128],
                                                                   op0=ALU.mult, op1=ALU.add), reads=['S', 'dec%d' % j3, 'ups'], writes=['S'])
                    op('dve', lambda e: e.tensor_copy(out=Sbf, in_=S), reads=['S'], writes=['Sbf'])
                    op('act', lambda e: e.activation(out=sqb[u], in_=ops_[:, :], func=AF.Square), reads=[ko], writes=['sqb%d' % u])

                def TC(v):
                    J, tt = flat[v]
                    ntl, ntok, tk, j3, j2 = geo(J)
                    u = v % 2
                    r = slice(tt * 128, (tt + 1) * 128)
                    ops_ = opsb[u]
                    ko = 'ops%d' % u
                    op('pe', lambda e: e.matmul(xm[:, :], lhsT=onesN, rhs=sqb[u], start=True, stop=True), reads=['sqb%d' % u, 'ones'], writes=['xm'])
                    op('act', lambda e: e.activation(out=rs[u], in_=xm[:, :], func=AF.Ln, bias=EPS), reads=['xm'], writes=['rs%d' % u])
                    op('act', lambda e: e.activation(out=rs[u], in_=rs[u], func=AF.Exp, scale=-0.5), reads=['rs%d' % u], writes=['rs%d' % u])
                    op('dve', lambda e: e.tensor_tensor(out=t1g[u], in0=ops_[:, :], in1=rs[u], op=ALU.mult), reads=[ko, 'rs%d' % u], writes=['t1g%d' % u])
                    for hd in range(4):
                        op('dve', lambda e: e.scalar_tensor_tensor(out=yg[j2][:, hd, r], in0=t1g[u][:, hd * 128:(hd + 1) * 128],
                                                                   scalar=vec[:, V_GNG + l * 4 + hd:V_GNG + l * 4 + hd + 1],
                                                                   in1=sg_[j3][:, hd, r], op0=ALU.mult, op1=ALU.mult),
                           reads=['t1g%d' % u, 'vec', 'sg%d' % j3], writes=['yg%d' % j2])

                def TD(v):
                    J, tt = flat[v]
                    ntl, ntok, tk, j3, j2 = geo(J)
                    if tt == ntl - 1:
                        dma('sp', yT[512:1024, tk].rearrange("(c p) t -> p c t", p=128), yg[j2][:, :, 0:ntok], reads=['yg%d' % j2], writes=[('ygs', J)])
                        gla_done[0] = STS[J][-1] + 1

                nJ = len(STS)
                for t in range(-7, NT + 3):
                    if 0 <= t - 1 < NT:
                        TB(t - 1)
                        yield
                    if 0 <= t - 3 < NT:
                        TD(t - 3)
                    if 0 <= t < NT:
                        TA(t)
                        yield
                    if 0 <= t - 2 < NT:
                        TC(t - 2)
                        yield
                    for J in range(nJ):
                        k = t - (v0[J] - 7)
                        if k == 0:
                            PA(J)
                        elif k == 1:
                            PB(J)
                        elif k == 2:
                            PC(J)
                        elif k == 3:
                            PD(J)
                    yield

            def p5_units():
                Wo = A.alloc(16 * D, BF16).rearrange("p (c n) -> p c n", c=16)
                wst = A.alloc(D, F32)
                pg = A.alloc(D, F32)
                yt = [A.alloc(16 * 128, BF16).rearrange("p (c n) -> p c n", c=16) for _ in range(2)]
                hb = [A.alloc(D, F32) for _ in range(2)]
                zs = [A.alloc(D, F32) for _ in range(2)]
                junk = A.alloc(D, BF16)
                ssq = [A.alloc(1, F32) for _ in range(2)]
                for c in range(16):
                    dma('sp', wst, w_out[l, c * 128:(c + 1) * 128, :], writes=['wst5'])
                    op('dve', lambda e: e.tensor_copy(out=Wo[:, c, :], in_=wst), reads=['wst5'], writes=['Wo'])
                    if c % 4 == 3:
                        yield
                dma('sp', pg, pg_in[l], writes=['pg'])
                yield
                tJ = {}
                for J, tiles in enumerate(STS):
                    for t in tiles:
                        tJ[t] = J
                for t in range(NT):
                    while gla_done[0] <= t:
                        yield
                    u = t % 2
                    dma('sp', yt[u], yT[:, t * 128:(t + 1) * 128].rearrange("(c p) t -> p c t", p=128), reads=[('ygs', tJ[t])], writes=['yt%d' % u])
                    dma('sp', hb[u], hsrc[t * 128:(t + 1) * 128, :], writes=['hb%d' % u])
                    for nh in range(2):
                        zp = psF[nh]
                        for c0 in range(0, 16, 4):
                            for c in range(c0, c0 + 4):
                                op('pe', lambda e: e.matmul(zp[:, :], lhsT=yt[u][:, c, :], rhs=Wo[:, c, nh * 512:(nh + 1) * 512],
                                                            start=(c == 0), stop=(c == 15)), reads=['yt%d' % u, 'Wo'], writes=['psF%d' % nh])
                            yield
                        op('act', lambda e: e.activation(out=zs[u][:, nh * 512:(nh + 1) * 512], in_=zp[:, :], func=AF.Copy),
                           reads=['psF%d' % nh], writes=['zs%d' % u])
                    yield
                    op('act', lambda e: e.activation(out=junk, in_=zs[u], func=AF.Square, accum_out=ssq[u]),
                       reads=['zs%d' % u], writes=['junk5', 'ssq%d' % u])
                    op('act', lambda e: e.activation(out=ssq[u], in_=ssq[u], func=AF.Ln, scale=1.0 / D, bias=EPS),
                       reads=['ssq%d' % u], writes=['ssq%d' % u])
                    op('act', lambda e: e.activation(out=ssq[u], in_=ssq[u], func=AF.Exp, scale=-0.5),
                       reads=['ssq%d' % u], writes=['ssq%d' % u])
                    yield
                    op('dve', lambda e: e.scalar_tensor_tensor(out=zs[u], in0=zs[u], scalar=ssq[u], in1=pg, op0=ALU.mult, op1=ALU.mult),
                       reads=['zs%d' % u, 'ssq%d' % u, 'pg'], writes=['zs%d' % u])
                    op('dve', lambda e: e.tensor_tensor(out=zs[u], in0=zs[u], in1=hb[u], op=ALU.add),
                       reads=['zs%d' % u, 'hb%d' % u], writes=['zs%d' % u])
                    yield
                    if l == NL - 1 or l == layers[-1]:
                        if t >= 1:
                            dma('act', out[(t - 1) * 128:t * 128, :], zs[u], reads=['zs%d' % u], writes=['out'])
                    else:
                        dma('act', h1[t * 128:(t + 1) * 128, :], zs[u], reads=['zs%d' % u], writes=['scr'])
                    yield

            if 'P3' in phases:
                A.reset()
                gla_done = [0]
                g3, g5 = p3_units(), p5_units()
                a5 = True
                for _ in g3:
                    for _k in range(3):
                        if a5:
                            try:
                                next(g5)
                            except StopIteration:
                                a5 = False
                if a5:
                    for _ in g5:
                        pass
                T.barrier()

        T.barrier()
        with nc.Block() as block:
            @block.sync
            def _(e):
                T.replay('sp', e)

            @block.tensor
            def _(e):
                T.replay('pe', e)

            @block.scalar
            def _(e):
                T.replay('act', e)

            @block.vector
            def _(e):
                T.replay('dve', e)

            @block.gpsimd
            def _(e):
                T.replay('pool', e)
    return nc


def make_consts():
    ident = np.eye(128, dtype=np.float32)
    tri = (np.arange(128)[None, :] >= np.arange(128)[:, None]).astype(np.float32)
    vm = np.broadcast_to((np.arange(128) >= 112).astype(np.float32)[None, :], (128, 128))
    return np.ascontiguousarray(np.concatenate([ident, tri, vm], axis=1))


def prep_shared(pre_g, w_in, b_f, w_a2, b_a, gla_norm_g, conv_w, conv_b, w_r, b_r, w_i, b_i, lru_lambda, w_out, post_g):
    f = lambda a: np.ascontiguousarray(np.asarray(a, dtype=np.float32))
    pp = lambda a, n: f(a).reshape(NL, n, 128).transpose(2, 0, 1).reshape(128, NL * n)
    vec = np.zeros((128, NV), np.float32)
    vec[:, V_PREG:V_PREG + 16] = pp(pre_g, 8)
    vec[:, V_NBA:V_NBA + 4] = pp(b_a, 2)
    vec[:, V_GNG:V_GNG + 8] = pp(gla_norm_g, 4)
    cw = f(conv_w).reshape(NL, 4, 8, 128).transpose(3, 0, 2, 1).reshape(128, NL * 8 * 4)
    vec[:, V_CW:V_CW + 64] = cw
    vec[:, V_CB:V_CB + 16] = pp(conv_b, 8)
    vec[:, V_BR:V_BR + 16] = pp(b_r, 8)
    vec[:, V_BI:V_BI + 16] = pp(b_i, 8)
    vec[:, V_LAM:V_LAM + 16] = pp(lru_lambda, 8)
    shared = {
        "w_in": f(w_in), "w_out": f(w_out), "vec128": vec,
        "bf_in": f(np.asarray(b_f).T),
        "wa2_in": f(np.asarray(w_a2).transpose(1, 0, 2).reshape(16, NL * 256)),
        "wr_in": f(np.asarray(w_r).transpose(2, 0, 1, 3).reshape(128, NL * 8 * 128)),
        "wi_in": f(np.asarray(w_i).transpose(2, 0, 1, 3).reshape(128, NL * 8 * 128)),
        "pg_in": f(np.broadcast_to(np.asarray(post_g)[:, None, :], (NL, 128, D))),
        "cst_in": make_consts(),
    }
    return shared


def kernel(x, meta, pre_g, w_in, b_f, w_a2, b_a, gla_norm_g, conv_w, conv_b,
           w_r, b_r, w_i, b_i, lru_lambda, w_out, post_g):
    x = np.asarray(x, dtype=np.float32)
    B, S, _ = x.shape
    NT = S // 128 + 1
    shared = prep_shared(pre_g, w_in, b_f, w_a2, b_a, gla_norm_g, conv_w, conv_b, w_r, b_r, w_i, b_i, lru_lambda, w_out, post_g)
    head = np.concatenate([np.zeros((112, D), np.float32), np.asarray(meta, dtype=np.float32)], axis=0)
    real = [0, 1, 4, 5][:B]
    zero_map = {k: np.zeros_like(v) for k, v in shared.items()}
    zero_map["h0"] = np.zeros((NT * 128, D), np.float32)
    in_maps = []
    for c in range(8):
        if c in real:
            m = dict(shared)
            m["h0"] = np.ascontiguousarray(np.concatenate([head, x[real.index(c)]], axis=0))
        else:
            m = zero_map
        in_maps.append(m)
    nc = build_nc(NT)
    res = run_bass_kernel_spmd(nc, in_maps, core_ids=list(range(8)))
    return np.stack([res.results[real[b]]["out"] for b in range(B)], axis=0).astype(np.float32)
```

```python
from contextlib import ExitStack
import os
import numpy as np
import concourse.bass as bass
import concourse.mybir as mybir
from concourse.bass_utils import run_bass_kernel_spmd

F32 = mybir.dt.float32
BF16 = mybir.dt.bfloat16
AF = mybir.ActivationFunctionType
ALU = mybir.AluOpType

D = 1024
DIN = 5656
DMIX = 2048
NL = 2
EPS = 1e-6
C_FQ, C_FK, C_FV, C_FF, C_FG = 0, 512, 1024, 1536, 1544
C_GQ, C_GK, C_GV, C_GA, C_GG = 2056, 2312, 2568, 3080, 3096
C_LX, C_LG = 3608, 4632
NDS = 24
LV = int(os.environ.get('DBG_P3', '99'))

V_PREG = 0
V_NBA = V_PREG + 16
V_GNG = V_NBA + 4
V_CW = V_GNG + 8
V_CB = V_CW + 64
V_BR = V_CB + 16
V_BI = V_BR + 16
V_LAM = V_BI + 16
NV = V_LAM + 16


class _Rec:
    def __getattr__(self, name):
        def f(*a, **k):
            return (name, a, k)
        return f


_REC = _Rec()


class TR:
    def __init__(self, nc, es):
        self.nc = nc
        self.engs = ['pe', 'act', 'dve', 'pool', 'sp']
        self.q = {e: [] for e in self.engs}
        self.sem = {}
        self.cnt = {}
        for j, e in enumerate(self.engs):
            self.sem[e] = nc.monotonic_semaphore(j).sem()
            self.cnt[e] = 0
        for i in range(NDS):
            n = 'd%d' % i
            self.sem[n] = nc.monotonic_semaphore(len(self.engs) + i).sem()
            self.cnt[n] = 0
        self.dnext = 0
        self.waited = {e: {} for e in self.engs}
        self.W = {}
        self.R = {}
        self.G = {}

    def _deps(self, reads, writes):
        d = {}

        def add(m):
            for k, v in m.items():
                if d.get(k, 0) < v:
                    d[k] = v
        for k in reads:
            add(self.W.get(k, {}))
        for k in writes:
            if self.R.get(k):
                g = dict(self.R[k])
                for kk, vv in self.W.get(k, {}).items():
                    if g.get(kk, 0) < vv:
                        g[kk] = vv
                self.G[k] = g
                self.W[k] = {}
                self.R[k] = {}
            add(self.G.get(k, {}))
        return d

    def _commit(self, reads, writes, ev):
        sem, val = ev
        for k in writes:
            self.W.setdefault(k, {})[sem] = val
        for k in reads:
            self.R.setdefault(k, {})[sem] = val

    def _filter(self, eng, d):
        waits = []
        for sem, val in d.items():
            if sem == eng and eng == 'pe':
                continue
            if self.waited[eng].get(sem, 0) >= val:
                continue
            self.waited[eng][sem] = val
            waits.append((sem, val))
        return waits

    def op(self, eng, fn, reads=(), writes=()):
        fn = fn(_REC)
        d = self._deps(reads, writes)
        waits = self._filter(eng, d)
        self.cnt[eng] += 1
        ev = (eng, self.cnt[eng])
        self.q[eng].append((waits, fn, ev, 1))
        self._commit(reads, writes, ev)

    def dma(self, eng, out, in_, reads=(), writes=()):
        d = self._deps(reads, writes)
        ds = 'd%d' % self.dnext
        self.dnext = (self.dnext + 1) % NDS
        if self.cnt[ds] > 0:
            d[ds] = max(d.get(ds, 0), self.cnt[ds])
        waits = self._filter(eng, d)
        self.cnt[ds] += 16
        ev = (ds, self.cnt[ds])
        self.q[eng].append((waits, ('dma_start', (), dict(out=out, in_=in_)), ev, 16))
        self._commit(reads, writes, ev)

    def barrier(self):
        snap = dict(self.cnt)
        for e in self.engs:
            d = {k: v for k, v in snap.items() if v > 0 and k != e}
            waits = self._filter(e, d)
            self.cnt[e] += 1
            self.q[e].append((waits, ('nop', (), {}), (e, self.cnt[e]), 1))
        snap = dict(self.cnt)
        for e in self.engs:
            d = {k: snap[k] for k in self.engs if k != e}
            waits = self._filter(e, d)
            self.cnt[e] += 1
            self.q[e].append((waits, ('nop', (), {}), (e, self.cnt[e]), 1))
        self.W = {}
        self.R = {}
        self.G = {}

    def replay(self, eng, e):
        for waits, fn, (s, v), inc in self.q[eng]:
            for ws, wv in waits:
                e.wait_ge(self.sem[ws], wv)
            getattr(e, fn[0])(*fn[1], **fn[2]).then_inc(self.sem[s], inc)


class Arena:
    def __init__(self, handle, nwords):
        self.h = handle
        self.n = nwords
        self.base = 0
        self.top = 0

    def persist(self):
        self.base = self.top

    def reset(self):
        self.top = self.base

    def alloc(self, free_elems, dtype, parts=128):
        words = (free_elems * (2 if dtype == BF16 else 4) + 3) // 4
        words = (words + 7) // 8 * 8
        a = self.top
        self.top += words
        assert self.top <= self.n, ("SBUF arena overflow", self.top, self.n)
        v = self.h[:, a:a + words]
        if dtype == BF16:
            v = v.bitcast(BF16)
        return v[0:parts, 0:free_elems]


def build_nc(NT, phases=('P1', 'P2', 'P3', 'P4', 'P5'), layers=(0, 1), debug_out=False):
    L = NT * 128
    nc = bass.Bass("TRN2", target_bir_lowering=False, monotonic_sem_count=NDS + 8)
    dt = lambda n, s, d, k="Internal": nc.dram_tensor(n, s, d, kind=k).ap()
    h0 = dt("h0", [L, D], F32, "ExternalInput")
    w_in = dt("w_in", [NL, D, DIN], F32, "ExternalInput")
    w_out = dt("w_out", [NL, DMIX, D], F32, "ExternalInput")
    vec128 = dt("vec128", [128, NV], F32, "ExternalInput")
    bf_in = dt("bf_in", [8, NL], F32, "ExternalInput")
    wa2_in = dt("wa2_in", [16, NL * 256], F32, "ExternalInput")
    wr_in = dt("wr_in", [128, NL * 8 * 128], F32, "ExternalInput")
    wi_in = dt("wi_in", [128, NL * 8 * 128], F32, "ExternalInput")
    pg_in = dt("pg_in", [NL, 128, D], F32, "ExternalInput")
    cst_in = dt("cst_in", [128, 3 * 128], F32, "ExternalInput")
    out = dt("out", [L - 128, D], F32, "ExternalOutput")
    kind_s = "ExternalOutput" if debug_out else "Internal"
    qa = dt("qa", [8, 68, L], BF16, kind_s)
    ka = dt("ka", [8, 68, L], BF16, kind_s)
    va = dt("va", [8, L, 65], BF16, kind_s)
    sgf = dt("sgf", [512, L], BF16, kind_s)
    gqT = dt("gqT", [256, L], F32, kind_s)
    gkT = dt("gkT", [256, L], F32, kind_s)
    gaT = dt("gaT", [16, L], BF16, kind_s)
    gv = dt("gv", [L, 512], BF16, kind_s)
    sgg = dt("sgg", [512, L], BF16, kind_s)
    lxT = dt("lxT", [1024, L], F32, kind_s)
    slg = dt("slg", [1024, L], BF16, kind_s)
    yT = dt("yT", [DMIX, L], BF16, kind_s)
    h1 = dt("h1", [L, D], F32, kind_s)

    AW = 51 * 1024 + 512
    arena_h = nc.alloc_sbuf_tensor("arena", [128, AW], F32)
    A = Arena(arena_h, AW)
    psF = [nc.alloc_psum_tensor("psf%d" % i, [128, 512], F32)[:, :] for i in range(6)]
    psB = [nc.alloc_psum_tensor("psb%d" % i, [128, 1024], BF16)[:, :] for i in range(2)]

    assert (NT - 1) % 4 == 0
    STS = [[0]] + [list(range(1 + 4 * j, 5 + 4 * j)) for j in range((NT - 1) // 4)]

    with ExitStack() as es:
        T = TR(nc, es)
        op, dma = T.op, T.dma

        vec = A.alloc(NV, F32)
        cst = A.alloc(384, F32)
        ident = A.alloc(128, BF16)
        tri4 = A.alloc(512, BF16)
        vm0 = A.alloc(128, F32)
        ones_bf = A.alloc(512, BF16)
        ones_f = A.alloc(512, F32)
        onesN = A.alloc(128, BF16)
        nbf = A.alloc(NL, F32, 8)
        wa2 = A.alloc(NL * 256, BF16, 16)
        sp8 = A.alloc(16, F32)
        sp16 = A.alloc(16, F32)
        nba = A.alloc(4, F32)
        zeros_bf = A.alloc(8 * 65, BF16)
        tmpc = A.alloc(NL * 256, F32)
        hsp8 = A.alloc(16, F32)
        hsp16 = A.alloc(16, F32)
        hbr = A.alloc(16, F32)
        hbi = A.alloc(16, F32)
        A.persist()

        dma('sp', vec, vec128, writes=['vec'])
        dma('sp', cst, cst_in, writes=['cst'])
        dma('sp', tmpc[0:8, 0:NL], bf_in, writes=['tmpbf'])
        op('dve', lambda e: e.tensor_scalar(out=nbf, in0=tmpc[0:8, 0:NL], scalar1=-1.0, scalar2=None, op0=ALU.mult),
           reads=['tmpbf'], writes=['nbf'])
        op('dve', lambda e: e.tensor_copy(out=ident, in_=cst[:, 0:128]), reads=['cst'], writes=['ident'])
        for r in range(4):
            op('dve', lambda e, r=r: e.tensor_copy(out=tri4[:, r * 128:(r + 1) * 128], in_=cst[:, 128:256]),
               reads=['cst'], writes=['tri4'])
        op('dve', lambda e: e.tensor_copy(out=vm0, in_=cst[:, 256:384]), reads=['cst'], writes=['vm0'])
        op('dve', lambda e: e.memset(ones_bf, 1.0), writes=['ones'])
        op('dve', lambda e: e.memset(ones_f, 1.0), writes=['ones'])
        op('dve', lambda e: e.memset(onesN, 1.0 / 128), writes=['ones'])
        op('dve', lambda e: e.memset(zeros_bf, 0.0), writes=['ones'])
        op('dve', lambda e: e.tensor_scalar(out=nba, in0=vec[:, V_NBA:V_NBA + 4], scalar1=-1.0, scalar2=None, op0=ALU.mult),
           reads=['vec'], writes=['nba'])
        op('act', lambda e: e.activation(out=sp8, in_=vec[:, V_LAM:V_LAM + 16], func=AF.Exp, scale=-1.0),
           reads=['vec'], writes=['sp8'])
        op('act', lambda e: e.activation(out=sp8, in_=sp8, func=AF.Ln, bias=1.0), reads=['sp8'], writes=['sp8'])
        op('dve', lambda e: e.tensor_scalar(out=sp16, in0=sp8, scalar1=-16.0, scalar2=None, op0=ALU.mult),
           reads=['sp8'], writes=['sp16'])
        op('dve', lambda e: e.tensor_scalar(out=sp8, in0=sp8, scalar1=-8.0, scalar2=None, op0=ALU.mult),
           reads=['sp8', 'sp16'], writes=['sp8'])
        op('dve', lambda e: e.tensor_scalar(out=hsp8, in0=sp8, scalar1=0.5, scalar2=None, op0=ALU.mult), reads=['sp8'], writes=['hsp'])
        op('dve', lambda e: e.tensor_scalar(out=hsp16, in0=sp16, scalar1=0.5, scalar2=None, op0=ALU.mult), reads=['sp16'], writes=['hsp'])
        op('dve', lambda e: e.tensor_scalar(out=hbr, in0=vec[:, V_BR:V_BR + 16], scalar1=0.5, scalar2=None, op0=ALU.mult), reads=['vec'], writes=['hsp'])
        op('dve', lambda e: e.tensor_scalar(out=hbi, in0=vec[:, V_BI:V_BI + 16], scalar1=0.5, scalar2=None, op0=ALU.mult), reads=['vec'], writes=['hsp'])
        dma('sp', tmpc[0:16, :], wa2_in, reads=['nbf'], writes=['tmpwa'])
        op('dve', lambda e: e.tensor_copy(out=wa2, in_=tmpc[0:16, :]), reads=['tmpwa'], writes=['wa2'])
        for h in range(8):
            for r in (65, 66, 67):
                dma('sp', qa[h, r:r + 1, :].rearrange("o (a b) -> (o a) b", a=NT), ones_bf[0:NT, 0:128],
                    reads=['ones'], writes=['qa_ones'])
            dma('sp', ka[h, 64:65, :].rearrange("o (a b) -> (o a) b", a=NT), ones_bf[0:NT, 0:128],
                reads=['ones'], writes=['ka_ones'])
        T.barrier()

        for l in layers:
            hsrc = h0 if l == 0 else h1
            def p4_units():
                NB = 3
                gbank = [(psF[3], psF[4]), (psF[5], psB[1].bitcast(F32))]
                gkey = [('psF3', 'psF4'), ('psF5', 'psB1')]
                wrb = A.alloc(8 * 128, BF16).rearrange("p (b n) -> p b n", b=8)
                wib = A.alloc(8 * 128, BF16).rearrange("p (b n) -> p b n", b=8)
                wst = p1stage[0][:, 0:512]
                lxe = [A.alloc(3 + 512, F32) for _ in range(NB)]
                sl = [A.alloc(512, BF16) for _ in range(NB)]
                xc = [A.alloc(512, F32) for _ in range(NB)] + [p1stage[0][:, 0:512], p1stage[1][:, 0:512]]
                xcb = [A.alloc(512, BF16) for _ in range(2)]
                rr = [A.alloc(512, F32) for _ in range(2)]
                ig = [A.alloc(512, F32) for _ in range(2)]
                aa = [A.alloc(512, F32) for _ in range(2)]
                a2 = rr
                hh = [A.alloc(512, F32) for _ in range(2)]
                hprev = A.alloc(8, F32)
                yl = [A.alloc(512, BF16) for _ in range(NB)]
                for hf in range(2):
                    for (wsrc, wdst, kk) in ((wr_in, wrb, 'wrb'), (wi_in, wib, 'wib')):
                        dma('sp', wst, wsrc[:, l * 1024 + hf * 512:l * 1024 + (hf + 1) * 512], reads=['stage0'], writes=['stage0'])
                        op('dve', lambda e: e.tensor_copy(out=wdst.rearrange("p b n -> p (b n)")[:, hf * 512:(hf + 1) * 512], in_=wst),
                           reads=['stage0'], writes=[kk])
                units = [(J, bl) for J in range(len(STS)) for bl in range(8)]
                yield

                def geom(u):
                    J, bl = units[u]
                    tiles = STS[J]
                    return J, bl, len(tiles) * 128, tiles[0] * 128, u % NB, u % 2, u % 5

                def stA(u, part):
                    J, bl, ntok, q0, s, s2, s5 = geom(u)
                    rows = slice(bl * 128, (bl + 1) * 128)
                    kl = 'lxe%d' % s
                    if part == 0 and J == 0:
                        op('dve', lambda e: e.memset(lxe[s][:, 0:3], 0.0), writes=[kl])
                        dma('sp', lxe[s][:, 3:3 + ntok], lxT[rows, q0:q0 + ntok], reads=[('lx', J)], writes=[kl])
                    elif part == 0:
                        dma('sp', lxe[s][:, 0:3 + ntok], lxT[rows, q0 - 3:q0 + ntok], reads=[('lx', J), ('lx', J - 1)], writes=[kl])
                    cwi = V_CW + (l * 8 + bl) * 4
                    cbi = V_CB + l * 8 + bl
                    kx = 'xc%d' % s5
                    if part == 0:
                      op('dve', lambda e: e.tensor_scalar(out=xc[s5][:, 0:ntok], in0=lxe[s][:, 3:3 + ntok], scalar1=vec[:, cwi + 3:cwi + 4],
                                                        scalar2=vec[:, cbi:cbi + 1], op0=ALU.mult, op1=ALU.add), reads=[kl, 'vec'], writes=[kx])
                    for k in ((2,) if part == 0 else (1, 0)):
                        op('dve', lambda e: e.scalar_tensor_tensor(out=xc[s5][:, 0:ntok], in0=lxe[s][:, k:k + ntok], scalar=vec[:, cwi + k:cwi + k + 1],
                                                                   in1=xc[s5][:, 0:ntok], op0=ALU.mult, op1=ALU.add), reads=[kl, 'vec', kx], writes=[kx])
                    if part == 1 and J == 0:
                        op('dve', lambda e: e.tensor_tensor(out=xc[s5][:, 0:128], in0=xc[s5][:, 0:128], in1=vm0, op=ALU.mult),
                           reads=[kx, 'vm0'], writes=[kx])

                def stA2(u):
                    J, bl, ntok, q0, s, s2, s5 = geom(u)
                    op('act', lambda e: e.activation(out=xcb[s2][:, 0:ntok], in_=xc[s5][:, 0:ntok], func=AF.Copy), reads=['xc%d' % s5], writes=['xcb%d' % s2])

                def stB(u):
                    J, bl, ntok, q0, s, s2, s5 = geom(u)
                    rps, ips = gbank[s2]
                    kr, ki = gkey[s2]
                    rows = slice(bl * 128, (bl + 1) * 128)
                    dma('sp', sl[s][:, 0:ntok], slg[rows, q0:q0 + ntok], reads=[('lx', J)], writes=['sl%d' % s])
                    op('pe', lambda e: e.matmul(rps[:, 0:ntok], lhsT=wrb[:, bl, :], rhs=xcb[s2][:, 0:ntok], start=True, stop=True),
                       reads=['wrb', 'xcb%d' % s2], writes=[kr])
                    op('pe', lambda e: e.matmul(ips[:, 0:ntok], lhsT=wib[:, bl, :], rhs=xcb[s2][:, 0:ntok], start=True, stop=True),
                       reads=['wib', 'xcb%d' % s2], writes=[ki])

                def stB2(u, part):
                    J, bl, ntok, q0, s, s2, s5 = geom(u)
                    rps, ips = gbank[s2]
                    kr, ki = gkey[s2]
                    bri = V_BR + l * 8 + bl
                    bii = V_BI + l * 8 + bl
                    li = l * 8 + bl
                    if part == 1:
                        for (o_, sc_) in ((aa[s2], hsp8), (a2[s2], hsp16)):
                            op('act', lambda e: e.activation(out=o_[:, 0:ntok], in_=rr[s2][:, 0:ntok], func=AF.Exp,
                                                             scale=sc_[:, li:li + 1], bias=sc_[:, li:li + 1]),
                               reads=['rr%d' % s2, 'hsp'], writes=['aa%d' % s2 if o_ is aa[s2] else 'rr%d' % s2])
                        op('act', lambda e: e.activation(out=a2[s2][:, 0:ntok], in_=a2[s2][:, 0:ntok], func=AF.Ln, scale=-1.0, bias=1.0),
                           reads=['rr%d' % s2], writes=['rr%d' % s2])
                        op('act', lambda e: e.activation(out=a2[s2][:, 0:ntok], in_=a2[s2][:, 0:ntok], func=AF.Exp, scale=0.5),
                           reads=['rr%d' % s2], writes=['rr%d' % s2])
                        return
                    op('act', lambda e: e.activation(out=rr[s2][:, 0:ntok], in_=rps[:, 0:ntok], func=AF.Tanh, scale=0.5, bias=hbr[:, li:li + 1]),
                       reads=[kr, 'hsp'], writes=['rr%d' % s2])
                    op('act', lambda e: e.activation(out=ig[s2][:, 0:ntok], in_=ips[:, 0:ntok], func=AF.Tanh, scale=0.5, bias=hbi[:, li:li + 1]),
                       reads=[ki, 'hsp'], writes=['ig%d' % s2])

                def stC(u):
                    J, bl, ntok, q0, s, s2, s5 = geom(u)
                    op('pool', lambda e: e.tensor_scalar(out=ig[s2][:, 0:ntok], in0=ig[s2][:, 0:ntok], scalar1=1.0, scalar2=0.5, op0=ALU.add, op1=ALU.mult),
                       reads=['ig%d' % s2], writes=['ig%d' % s2])
                    op('pool', lambda e: e.tensor_tensor(out=ig[s2][:, 0:ntok], in0=ig[s2][:, 0:ntok], in1=xc[s5][:, 0:ntok], op=ALU.mult),
                       reads=['ig%d' % s2, 'xc%d' % s5], writes=['ig%d' % s2])
                    op('pool', lambda e: e.tensor_tensor(out=ig[s2][:, 0:ntok], in0=ig[s2][:, 0:ntok], in1=a2[s2][:, 0:ntok], op=ALU.mult),
                       reads=['ig%d' % s2, 'rr%d' % s2], writes=['ig%d' % s2])
                    init = 0.0 if J == 0 else hprev[:, bl:bl + 1]
                    op('dve', lambda e: e.tensor_tensor_scan(out=hh[s2][:, 0:ntok], data0=aa[s2][:, 0:ntok], data1=ig[s2][:, 0:ntok],
                                                             initial=init, op0=ALU.mult, op1=ALU.add),
                       reads=['aa%d' % s2, 'ig%d' % s2, 'hprev'], writes=['hh%d' % s2])
                    op('dve', lambda e: e.tensor_copy(out=hprev[:, bl:bl + 1], in_=hh[s2][:, ntok - 1:ntok]), reads=['hh%d' % s2], writes=['hprev'])
                    op('dve', lambda e: e.tensor_tensor(out=yl[s][:, 0:ntok], in0=hh[s2][:, 0:ntok], in1=sl[s][:, 0:ntok], op=ALU.mult),
                       reads=['hh%d' % s2, 'sl%d' % s], writes=['yl%d' % s])

                def stD(u):
                    J, bl, ntok, q0, s, s2, s5 = geom(u)
                    dma('sp', yT[1024 + bl * 128:1024 + (bl + 1) * 128, q0:q0 + ntok], yl[s][:, 0:ntok], reads=['yl%d' % s], writes=['scr'])

                n = len(units)
                for t in range(n + 5):
                    pieces = []
                    if 0 <= t - 5 < n:
                        pieces.append((stD, (t - 5,)))
                    if 0 <= t - 4 < n:
                        pieces.append((stC, (t - 4,)))
                    if 0 <= t - 3 < n:
                        pieces.append((stB2, (t - 3, 0)))
                        pieces.append((stB2, (t - 3, 1)))
                    if 0 <= t - 2 < n:
                        pieces.append((stB, (t - 2,)))
                    if 0 <= t - 1 < n:
                        pieces.append((stA2, (t - 1,)))
                    if t < n:
                        pieces.append((stA, (t, 0)))
                        pieces.append((stA, (t, 1)))
                    for i, (f_, a_) in enumerate(pieces):
                        f_(*a_)
                        p4n[0] = t if i + 1 < len(pieces) else t + 1
                        yield

            if 'P1' in phases:
                A.reset()
                W = A.alloc(8 * DIN, BF16).rearrange("p (c n) -> p c n", c=8)
                SC = 808
                stage = [A.alloc(SC, F32) for _ in range(2)]
                p1stage = stage
                hb = [A.alloc(D, F32) for _ in range(2)]
                junk = A.alloc(D, BF16)
                ssq = [A.alloc(1, F32) for _ in range(3)]
                hn = [A.alloc(D, BF16) for _ in range(2)]
                hnT = [A.alloc(8 * 512, BF16).rearrange("p (c n) -> p c n", c=8) for _ in range(2)]
                evf = [A.alloc(512, F32) for _ in range(4)]
                evb = [A.alloc(512, BF16) for _ in range(4)]
                vt = [A.alloc(8 * 65, BF16).rearrange("p (h c) -> p h c", h=8) for _ in range(2)]
                gvt = [A.alloc(512, BF16) for _ in range(2)]
                ffs = A.alloc(512, F32, 8)
                spf = ffs
                g4 = p4_units()
                p4n = [0]
                gcount = [0]

                def p4adv(k, lim):
                    for _ in range(k):
                        if p4n[0] < lim:
                            next(g4)
                pstores = []

                def sdma(dst, src_, reads=(), writes=()):
                    pstores.append((dst, src_, reads, writes))

                def sflush(keep):
                    while len(pstores) > keep:
                        d_, s_, r_, w_ = pstores.pop(0)
                        dma('sp', d_, s_, reads=r_, writes=w_)
                cc = A.alloc(512, F32, 8)
                cprev = A.alloc(1, F32, 8)
                cq = A.alloc(512, BF16, 8)
                kp = A.alloc(3 * 512, BF16, 8).rearrange("p (r n) -> p r n", r=3)
                r1 = A.alloc(512, F32, 8)
                gab = A.alloc(512, BF16, 16)
                si = 0
                for c in range(8):
                    for s0 in range(0, DIN, SC):
                        st = stage[si % 2]
                        k = 'stage%d' % (si % 2)
                        si += 1
                        dma('sp', st, w_in[l, c * 128:(c + 1) * 128, s0:s0 + SC], writes=[k])
                        op('dve', lambda e, st=st, c=c, s0=s0: e.tensor_scalar(
                            out=W[:, c, s0:s0 + SC], in0=st, scalar1=vec[:, V_PREG + l * 8 + c:V_PREG + l * 8 + c + 1],
                            scalar2=None, op0=ALU.mult), reads=[k, 'vec'], writes=['W'])
                for hh in range(2):
                    op('dve', lambda e, hh=hh: e.memset(vt[hh][:, :, 64:65], 1.0), writes=['vt%d' % hh])
                psi = 0
                evi = 0
                tcount = 0
                pinfo = {}

                def prep_a(Jp, ti):
                    nonlocal tcount
                    t = STS[Jp][ti]
                    b = hb[tcount % 2]
                    kb = 'hb%d' % (tcount % 2)
                    sq = ssq[tcount % 3]
                    ksq = 'ssq%d' % (tcount % 3)
                    n_ = hn[tcount % 2]
                    kn = 'hn%d' % (tcount % 2)
                    pinfo[(Jp, ti)] = (n_, kn)
                    tcount += 1
                    dma('sp', b, hsrc[t * 128:(t + 1) * 128, :], writes=[kb])
                    op('act', lambda e: e.activation(out=junk, in_=b, func=AF.Square, accum_out=sq), reads=[kb], writes=['junk', ksq])
                    op('act', lambda e: e.activation(out=sq, in_=sq, func=AF.Ln, scale=1.0 / D, bias=EPS), reads=[ksq], writes=[ksq])
                    op('act', lambda e: e.activation(out=sq, in_=sq, func=AF.Exp, scale=-0.5), reads=[ksq], writes=[ksq])
                    op('dve', lambda e: e.tensor_scalar(out=n_, in0=b, scalar1=sq, scalar2=None, op0=ALU.mult), reads=[kb, ksq], writes=[kn])

                def prep_b(Jp, ti):
                    n_, kn = pinfo.pop((Jp, ti))
                    X = hnT[Jp % 2]
                    kX = 'hnT%d' % (Jp % 2)
                    pT = psB[0]
                    kpT = 'psB0'
                    for c in range(8):
                        op('pe', lambda e: e.transpose(out=pT[:, c * 128:(c + 1) * 128], in_=n_[:, c * 128:(c + 1) * 128], identity=ident),
                           reads=[kn, 'ident'], writes=[kpT])
                    if ti % 2 == 0:
                        op('act', lambda e: e.activation(out=X[:, :, ti * 128:(ti + 1) * 128], in_=pT.rearrange("p (c n) -> p c n", c=8), func=AF.Copy),
                           reads=[kpT], writes=[kX])
                    else:
                        op('dve', lambda e: e.tensor_copy(out=X[:, :, ti * 128:(ti + 1) * 128], in_=pT.rearrange("p (c n) -> p c n", c=8)),
                           reads=[kpT], writes=[kX])

                next(g4)
                for J, tiles in enumerate(STS):
                    ntl = len(tiles)
                    ntok = ntl * 128
                    q0 = tiles[0] * 128
                    X = hnT[J % 2]
                    kX = 'hnT%d' % (J % 2)
                    if J == 0:
                        prep_a(0, 0)
                        prep_b(0, 0)
                    gl = [0]
                    def fm_group(c0, M, evac):
                        nonlocal psi
                        ps = psF[psi % 3]
                        kps = 'psF%d' % (psi % 3)
                        psi += 1
                        gcount[0] += 1
                        gl[0] += 1
                        if J + 1 < len(STS) and gl[0] in (4, 12, 20, 28):
                            prep_a(J + 1, (gl[0] - 4) // 8)
                        if J + 1 < len(STS) and gl[0] in (8, 16, 24, 32):
                            prep_b(J + 1, (gl[0] - 8) // 8)
                        sflush(3)
                        p4adv(2 if gcount[0] % 2 == 0 else 1, 8 * J)
                        for kc in range(8):
                            op('pe', lambda e, ps=ps, kc=kc: e.matmul(ps[0:M, 0:ntok], lhsT=W[:, kc, c0:c0 + M],
                                                                     rhs=X[:, kc, 0:ntok], start=(kc == 0), stop=(kc == 7)),
                               reads=['W', kX], writes=[kps])
                        evac(ps[0:M, 0:ntok], kps)

                    def ev_copy(dst_list, scale=None, dtype=BF16, eng='dve', wkey='scr'):
                        def f(ps, kps):
                            nonlocal evi
                            M = ps.shape[0]
                            buf = (evb if dtype == BF16 else evf)[evi % 4]
                            kb_ = ('evb%d' if dtype == BF16 else 'evf%d') % (evi % 4)
                            evi += 1
                            o = buf[0:M, 0:ntok]
                            if eng == 'silu':
                                op('act', lambda e: e.activation(out=o, in_=ps, func=AF.Silu), reads=[kps], writes=[kb_])
                            elif scale is not None:
                                op('dve', lambda e: e.tensor_scalar(out=o, in0=ps, scalar1=scale, scalar2=None, op0=ALU.mult),
                                   reads=[kps], writes=[kb_])
                            else:
                                op('dve', lambda e: e.tensor_copy(out=o, in_=ps), reads=[kps], writes=[kb_])
                            for (r0, nr, dst) in dst_list:
                                sdma(dst, buf[r0:r0 + nr, 0:ntok], reads=[kb_], writes=[wkey])
                        return f

                    tk = slice(q0, q0 + ntok)
                    def ev_ff(ps, kps):
                        op('dve', lambda e: e.tensor_copy(out=ffs[:, 0:ntok], in_=ps), reads=[kps], writes=['ffs'])
                    fm_group(C_FF, 8, ev_ff)
                    op('act', lambda e: e.activation(out=spf[:, 0:ntok], in_=ffs[:, 0:ntok], func=AF.Exp,
                                                     bias=nbf[:, l:l + 1], scale=-1.0), reads=['ffs', 'nbf'], writes=['ffs'])
                    op('act', lambda e: e.activation(out=spf[:, 0:ntok], in_=spf[:, 0:ntok], func=AF.Ln, bias=1.0),
                       reads=['ffs'], writes=['ffs'])
                    if J == 0:
                        op('dve', lambda e: e.tensor_tensor(out=spf[:, 0:128], in0=spf[:, 0:128], in1=vm0[0:8, :], op=ALU.mult),
                           reads=['ffs', 'vm0'], writes=['ffs'])
                    init = 0.0 if J == 0 else cprev
                    op('dve', lambda e, init=init: e.tensor_tensor_scan(out=cc[:, 0:ntok], data0=ones_f[0:8, 0:ntok], data1=spf[:, 0:ntok],
                                                                        initial=init, op0=ALU.mult, op1=ALU.subtract),
                       reads=['ffs', 'ones', 'cprev'], writes=['cc'])
                    op('dve', lambda e: e.tensor_copy(out=cprev, in_=cc[:, ntok - 1:ntok]), reads=['cc'], writes=['cprev'])
                    op('dve', lambda e: e.tensor_copy(out=cq[:, 0:ntok], in_=cc[:, 0:ntok]), reads=['cc'], writes=['cq'])
                    op('dve', lambda e: e.tensor_scalar(out=kp[:, 0, 0:ntok], in0=cc[:, 0:ntok], scalar1=-1.0, scalar2=None, op0=ALU.mult),
                       reads=['cc'], writes=['kp'])
                    op('dve', lambda e: e.scalar_tensor_tensor(out=r1[:, 0:ntok], in0=cc[:, 0:ntok], scalar=-1.0, in1=kp[:, 0, 0:ntok],
                                                               op0=ALU.mult, op1=ALU.subtract), reads=['cc', 'kp'], writes=['r1'])
                    op('dve', lambda e: e.tensor_copy(out=kp[:, 1, 0:ntok], in_=r1[:, 0:ntok]), reads=['r1'], writes=['kp'])
                    op('dve', lambda e: e.tensor_tensor(out=r1[:, 0:ntok], in0=r1[:, 0:ntok], in1=kp[:, 1, 0:ntok], op=ALU.subtract),
                       reads=['r1', 'kp'], writes=['r1'])
                    op('dve', lambda e: e.tensor_copy(out=kp[:, 2, 0:ntok], in_=r1[:, 0:ntok]), reads=['r1'], writes=['kp'])
                    sdma(qa[:, 64, tk], cq[:, 0:ntok], reads=['cq'], writes=['scr'])
                    sdma(ka[:, 65:68, tk], kp[:, :, 0:ntok], reads=['kp'], writes=['scr'])

                    def ev_ga(ps, kps):
                        op('dve', lambda e: e.tensor_copy(out=gab[:, 0:ntok], in_=ps), reads=[kps], writes=['gab'])
                        sdma(gaT[:, tk], gab[:, 0:ntok], reads=['gab'], writes=['scr'])
                    fm_group(C_GA, 16, ev_ga)
                    for g in range(4):
                        fm_group(C_FQ + g * 128, 128, ev_copy([(0, 64, qa[2 * g, 0:64, tk]), (64, 64, qa[2 * g + 1, 0:64, tk])], scale=0.125))
                    for g in range(4):
                        fm_group(C_FK + g * 128, 128, ev_copy([(0, 64, ka[2 * g, 0:64, tk]), (64, 64, ka[2 * g + 1, 0:64, tk])]))
                    for g in range(4):
                        fm_group(C_FG + g * 128, 128, ev_copy([(0, 128, sgf[g * 128:(g + 1) * 128, tk])], eng='silu'))
                    for g in range(2):
                        fm_group(C_GQ + g * 128, 128, ev_copy([(0, 128, gqT[g * 128:(g + 1) * 128, tk])], dtype=F32))
                    for g in range(2):
                        fm_group(C_GK + g * 128, 128, ev_copy([(0, 128, gkT[g * 128:(g + 1) * 128, tk])], dtype=F32))
                    for g in range(4):
                        fm_group(C_GG + g * 128, 128, ev_copy([(0, 128, sgg[g * 128:(g + 1) * 128, tk])], eng='silu'))
                    for g in range(8):
                        fm_group(C_LX + g * 128, 128, ev_copy([(0, 128, lxT[g * 128:(g + 1) * 128, tk])], dtype=F32, wkey=('lx', J)))
                    for g in range(8):
                        fm_group(C_LG + g * 128, 128, ev_copy([(0, 128, slg[g * 128:(g + 1) * 128, tk])], eng='silu', wkey=('lx', J)))
                    for ti, t in enumerate(tiles):
                        for which in (0, 1):
                            ps = psF[psi % 3]
                            kps = 'psF%d' % (psi % 3)
                            psi += 1
                            c0 = C_FV if which == 0 else C_GV
                            sflush(3)
                            p4adv(2, 8 * J)
                            for kc in range(8):
                                op('pe', lambda e, ps=ps, kc=kc, c0=c0, ti=ti: e.matmul(
                                    ps[:, :], lhsT=X[:, kc, ti * 128:(ti + 1) * 128], rhs=W[:, kc, c0:c0 + 512],
                                    start=(kc == 0), stop=(kc == 7)), reads=['W', kX], writes=[kps])
                            if which == 0:
                                v_ = vt[t % 2]
                                kv_ = 'vt%d' % (t % 2)
                                op('dve', lambda e, ps=ps, v_=v_: e.tensor_copy(out=v_[:, :, 0:64],
                                                                               in_=ps.rearrange("p (h c) -> p h c", h=8)),
                                   reads=[kps], writes=[kv_])
                                if t == 0:
                                    sdma(va[:, 112:128, :].rearrange("h t c -> t h c"), v_[112:128, :, :], reads=[kv_], writes=['scr'])
                                    sdma(va[:, 0:112, :].rearrange("h t c -> t h c"),
                                        zeros_bf[0:112, :].rearrange("p (h c) -> p h c", h=8), reads=['ones'], writes=['scr'])
                                else:
                                    sdma(va[:, t * 128:(t + 1) * 128, :].rearrange("h t c -> t h c"), v_, reads=[kv_], writes=['scr'])
                            else:
                                g_ = gvt[t % 2]
                                kg_ = 'gvt%d' % (t % 2)
                                op('act', lambda e, ps=ps, g_=g_: e.activation(out=g_, in_=ps, func=AF.Copy), reads=[kps], writes=[kg_])
                                sdma(gv[t * 128:(t + 1) * 128, :], g_, reads=[kg_], writes=['scr'])
                    while p4n[0] < 8 * J:
                        next(g4)
                sflush(0)
                for _ in g4:
                    pass
                T.barrier()

            if 'P2' in phases:
                A.reset()
                Qa = [A.alloc(L, BF16, 68) for _ in range(2)]
                Ka = [A.alloc(L, BF16, 68) for _ in range(2)]
                Va = [A.alloc(NT * 65, BF16).rearrange("p (n c) -> p n c", n=NT) for _ in range(2)]
                SG = [A.alloc(L, BF16, 64) for _ in range(2)]
                pt = [A.alloc(512, BF16) for _ in range(3)]
                dn = A.alloc(512, F32, 65)
                rdb = A.alloc(512, BF16, 65)
                t1 = [A.alloc(512, F32, 64) for _ in range(2)]
                yb = [A.alloc(512, BF16, 64) for _ in range(2)]
                pt.append(A.alloc(512, BF16))
                pt.append(A.alloc(512, BF16))
                bpsF = psB[1].bitcast(F32)
                gk_ = [0]
                blk_ = [0]
                for h in range(8):
                    s = h % 2
                    kQ, kK, kV, kS = 'Qa%d' % s, 'Ka%d' % s, 'Va%d' % s, 'SG%d' % s
                    dma('sp', Qa[s], qa[h], writes=[kQ])
                    dma('sp', Ka[s], ka[h], writes=[kK])
                    for n0 in range(0, NT, 8):
                        n1 = min(NT, n0 + 8)
                        dma('sp', Va[s][:, n0:n1, :], va[h, n0 * 128:n1 * 128, :].rearrange("(n p) c -> p n c", p=128), writes=[kV])
                    dma('sp', SG[s], sgf[h * 64:(h + 1) * 64, :], writes=[kS])
                    tasks = []
                    for J, tiles in enumerate(STS):
                        ob = blk_[0] % 2
                        blk_[0] += 1
                        for I in range(tiles[-1] + 1):
                            tasks.append((J, I, ob, gk_[0]))
                            gk_[0] += 1
                    pend = []

                    def emit_S(task):
                        J, I, ob, g = task
                        tiles = STS[J]
                        ntok = len(tiles) * 128
                        q0 = tiles[0] * 128
                        off = max(0, I - tiles[0]) * 128
                        sps = psF[g % 4]
                        ksps = 'psF%d' % (g % 4)
                        p_ = pt[g % 5]
                        kp_ = 'pt%d' % (g % 5)
                        op('pe', lambda e: e.matmul(sps[:, off:ntok], lhsT=Ka[s][:, I * 128:(I + 1) * 128],
                                                    rhs=Qa[s][:, q0 + off:q0 + ntok], start=True, stop=True),
                           reads=[kQ, kK], writes=[ksps])
                        op('act', lambda e: e.activation(out=p_[:, off:ntok], in_=sps[:, off:ntok], func=AF.Exp),
                           reads=[ksps], writes=[kp_])
                        if I >= tiles[0]:
                            op('dve', lambda e: e.tensor_tensor(out=p_[:, off:off + 128], in0=p_[:, off:off + 128],
                                                                 in1=tri4[:, 0:128], op=ALU.mult),
                               reads=[kp_, 'tri4'], writes=[kp_])

                    def emit_PV(task):
                        J, I, ob, g = task
                        tiles = STS[J]
                        ntok = len(tiles) * 128
                        q0 = tiles[0] * 128
                        last = tiles[-1]
                        off = max(0, I - tiles[0]) * 128
                        ops_ = psF[4 + ob]
                        kops = 'psF%d' % (4 + ob)
                        p_ = pt[g % 5]
                        kp_ = 'pt%d' % (g % 5)
                        op('pe', lambda e: e.matmul(ops_[0:65, off:ntok], lhsT=Va[s][:, I, :], rhs=p_[:, off:ntok],
                                                    start=(I == 0), stop=(I == last)), reads=[kV, kp_], writes=[kops])
                        if I != last:
                            return
                        while pend:
                            pend.pop(0)[1]()
                        op('dve', lambda e: e.tensor_scalar(out=dn[64:65, 0:ntok], in0=ops_[64:65, 0:ntok], scalar1=1e-30,
                                                            scalar2=None, op0=ALU.max), reads=[kops], writes=['dn'])
                        op('dve', lambda e: e.reciprocal(out=dn[64:65, 0:ntok], in_=dn[64:65, 0:ntok]), reads=['dn'], writes=['dn'])
                        op('dve', lambda e: e.tensor_copy(out=rdb[64:65, 0:ntok], in_=dn[64:65, 0:ntok]), reads=['dn'], writes=['rdb'])

                        def partB():
                            bps = bpsF
                            op('pe', lambda e: e.matmul(bps[0:64, 0:ntok], lhsT=ones_bf[64:65, 0:64], rhs=rdb[64:65, 0:ntok],
                                                        start=True, stop=True), reads=['rdb', 'ones'], writes=['bpsF'])
                            t_ = t1[ob]
                            kt_ = 't1%d' % ob
                            y_ = yb[ob]
                            ky_ = 'yb%d' % ob
                            op('dve', lambda e: e.tensor_tensor(out=t_[:, 0:ntok], in0=ops_[0:64, 0:ntok], in1=SG[s][:, q0:q0 + ntok],
                                                                op=ALU.mult), reads=[kops, kS], writes=[kt_])
                            op('dve', lambda e: e.tensor_tensor(out=y_[:, 0:ntok], in0=t_[:, 0:ntok], in1=bps[0:64, 0:ntok],
                                                                op=ALU.mult), reads=[kt_, 'bpsF'], writes=[ky_])
                            dma('act', yT[h * 64:(h + 1) * 64, q0:q0 + ntok], y_[:, 0:ntok], reads=[ky_], writes=['scr'])
                        pend.append([8, partB])

                    DPT = 3
                    for k in range(len(tasks) + DPT):
                        if k < len(tasks):
                            emit_S(tasks[k])
                        if k - DPT >= 0:
                            for pb in pend:
                                pb[0] -= 1
                            while pend and pend[0][0] <= 0:
                                pend.pop(0)[1]()
                            emit_PV(tasks[k - DPT])
                    while pend:
                        pend.pop(0)[1]()
                T.barrier()

            def p3_units():
                r3 = lambda n, dt_: [A.alloc(n, dt_) for _ in range(3)]
                r2 = lambda n, dt_: [A.alloc(n, dt_) for _ in range(2)]
                v3 = lambda lst, c: [x.rearrange("p (c n) -> p c n", c=c) for x in lst]
                gq = v3(r3(1024, F32), 2)
                gk = v3(r3(1024, F32), 2)
                ga_ = [A.alloc(512, BF16, 16) for _ in range(3)]
                gvs = v3(r3(2048, BF16), 4)
                sg_ = v3(r3(2048, BF16), 4)
                ee = v3([A.alloc(1024, F32)] * 2, 2)
                bs = v3([A.alloc(1024, F32)] * 2, 2)
                ebh = v3([A.alloc(1024, F32)] * 2, 2)
                ebq = v3([A.alloc(1024, F32)] * 2, 2)
                ebk = v3([A.alloc(1024, F32)] * 2, 2)
                qz = [v3([A.alloc(1024, BF16) for _ in range(2)], 2) for _ in range(3)]
                kt = v3(r3(1024, BF16), 2)
                khT = v3(r3(1024, BF16), 2)
                nbl = v3(r3(8, F32), 2)
                dec = v3(r3(8, F32), 2)
                yg = v3(r2(2048, BF16), 4)
                at = r2(512, BF16)
                kh = r2(256, BF16)
                sqb = r2(512, BF16)
                rs = r2(512, F32)
                t1g = r2(512, F32)
                S = A.alloc(2 * 128, F32).rearrange("p (c n) -> p c n", c=2)
                Sbf = A.alloc(2 * 128, BF16).rearrange("p (c n) -> p c n", c=2)
                op('dve', lambda e: e.memset(S, 0.0), writes=['S'])
                op('dve', lambda e: e.memset(Sbf, 0.0), writes=['Sbf'])
                for j in range(3):
                    for p in range(2):
                        op('dve', lambda e: e.memset(qz[j][p], 0.0), writes=['qz%d' % j])
                aps, opsb, ups = psF[2], [psF[3], psF[4]], psF[5]
                xm = psB[1].bitcast(F32)
                tps = psB[0]
                flat = [(J, tt) for J, tiles in enumerate(STS) for tt in range(len(tiles))]
                v0 = {}
                for v, (J, tt) in enumerate(flat):
                    v0.setdefault(J, v)

                def geo(J):
                    tiles = STS[J]
                    return len(tiles), len(tiles) * 128, slice(tiles[0] * 128, (tiles[-1] + 1) * 128), J % 3, J % 2

                def PA(J):
                    ntl, ntok, tk, j3, j2 = geo(J)
                    dma('sp', gq[j3][:, :, 0:ntok], gqT[:, tk].rearrange("(c p) t -> p c t", p=128), writes=['gq%d' % j3])
                    dma('sp', gk[j3][:, :, 0:ntok], gkT[:, tk].rearrange("(c p) t -> p c t", p=128), writes=['gk%d' % j3])
                    dma('sp', ga_[j3][:, 0:ntok], gaT[:, tk], writes=['ga%d' % j3])
                    dma('sp', gvs[j3][:, 0:ntl, :], gv[tk, :].rearrange("(t p) n -> p t n", p=128), writes=['gvs%d' % j3])
                    dma('sp', sg_[j3][:, :, 0:ntok], sgg[:, tk].rearrange("(c p) t -> p c t", p=128), writes=['sg%d' % j3])
                    for c in range(2):
                        op('pe', lambda e: e.matmul(xm[:, 0:ntok], lhsT=wa2[:, l * 256 + c * 128:l * 256 + (c + 1) * 128],
                                                    rhs=ga_[j3][:, 0:ntok], start=True, stop=True), reads=['wa2', 'ga%d' % j3], writes=['xm'])
                        op('act', lambda e: e.activation(out=ee[j2][:, c, 0:ntok], in_=xm[:, 0:ntok], func=AF.Exp,
                                                         bias=nba[:, l * 2 + c:l * 2 + c + 1], scale=-1.0), reads=['xm', 'nba'], writes=['ee'])
                        op('act', lambda e: e.activation(out=ee[j2][:, c, 0:ntok], in_=ee[j2][:, c, 0:ntok], func=AF.Ln, bias=1.0),
                           reads=['ee'], writes=['ee'])

                def PB(J):
                    ntl, ntok, tk, j3, j2 = geo(J)
                    for c in range(2):
                        for tt in range(ntl):
                            r = slice(tt * 128, (tt + 1) * 128)
                            op('dve', lambda e: e.tensor_tensor_scan(out=bs[j2][:, c, r], data0=ones_f[:, 0:128], data1=ee[j2][:, c, r],
                                                                     initial=0.0, op0=ALU.mult, op1=ALU.add),
                               reads=['ee', 'ones'], writes=['bs'])
                            op('dve', lambda e: e.tensor_scalar(out=nbl[j3][:, c, tt:tt + 1], in0=bs[j2][:, c, tt * 128 + 127:tt * 128 + 128],
                                                                scalar1=-1.0 / 16, scalar2=None, op0=ALU.mult),
                               reads=['bs'], writes=['nbl%d' % j3])

                def PC(J):
                    ntl, ntok, tk, j3, j2 = geo(J)
                    for c in range(2):
                        for tt in range(ntl):
                            r = slice(tt * 128, (tt + 1) * 128)
                            op('act', lambda e: e.activation(out=ebh[j2][:, c, r], in_=bs[j2][:, c, r], func=AF.Exp,
                                                             bias=nbl[j3][:, c, tt:tt + 1], scale=1.0 / 16),
                               reads=['bs', 'nbl%d' % j3], writes=['ebh'])
                        op('act', lambda e: e.activation(out=dec[j3][:, c, 0:ntl], in_=nbl[j3][:, c, 0:ntl], func=AF.Exp),
                           reads=['nbl%d' % j3], writes=['dec%d' % j3])
                        op('act', lambda e: e.activation(out=ebq[j2][:, c, 0:ntok], in_=bs[j2][:, c, 0:ntok], func=AF.Exp, scale=-1.0 / 16),
                           reads=['bs'], writes=['ebq'])
                        op('act', lambda e: e.activation(out=ebk[j2][:, c, 0:ntok], in_=bs[j2][:, c, 0:ntok], func=AF.Exp, scale=1.0 / 16),
                           reads=['bs'], writes=['ebk'])

                def PD(J):
                    ntl, ntok, tk, j3, j2 = geo(J)
                    for c in range(2):
                        op('dve', lambda e: e.tensor_tensor(out=khT[j3][:, c, 0:ntok], in0=gk[j3][:, c, 0:ntok], in1=ebh[j2][:, c, 0:ntok], op=ALU.mult),
                           reads=['gk%d' % j3, 'ebh'], writes=['khT%d' % j3])
                        for p in range(2):
                            pr = slice(p * 64, (p + 1) * 64)
                            op('dve', lambda e: e.scalar_tensor_tensor(out=qz[j3][p][pr, c, 0:ntok], in0=gq[j3][pr, c, 0:ntok], scalar=0.125,
                                                                       in1=ebq[j2][pr, c, 0:ntok], op0=ALU.mult, op1=ALU.mult),
                               reads=['gq%d' % j3, 'ebq'], writes=['qz%d' % j3])
                        op('dve', lambda e: e.tensor_tensor(out=kt[j3][:, c, 0:ntok], in0=gk[j3][:, c, 0:ntok], in1=ebk[j2][:, c, 0:ntok], op=ALU.mult),
                           reads=['gk%d' % j3, 'ebk'], writes=['kt%d' % j3])

                def TA(v):
                    J, tt = flat[v]
                    j3 = J % 3
                    u = v % 2
                    r = slice(tt * 128, (tt + 1) * 128)
                    for hd in range(4):
                        c, p = hd // 2, hd % 2
                        op('pe', lambda e: e.matmul(aps[:, hd * 128:(hd + 1) * 128], lhsT=kt[j3][:, c, r], rhs=qz[j3][p][:, c, r],
                                                    start=True, stop=True), reads=['kt%d' % j3, 'qz%d' % j3], writes=['aps'])
                    op('dve', lambda e: e.tensor_tensor(out=at[u], in0=aps[:, :], in1=tri4, op=ALU.mult), reads=['aps', 'tri4'], writes=['at%d' % u])
                    for c in range(2):
                        op('pe', lambda e: e.transpose(out=tps[:, c * 128:(c + 1) * 128], in_=khT[j3][:, c, r], identity=ident),
                           reads=['khT%d' % j3, 'ident'], writes=['tps'])
                    op('act', lambda e: e.activation(out=kh[u], in_=tps[:, 0:256], func=AF.Copy), reads=['tps'], writes=['kh%d' % u])

                def TB(v):
                    J, tt = flat[v]
                    j3 = J % 3
                    u = v % 2
                    r = slice(tt * 128, (tt + 1) * 128)
                    ops_ = opsb[u]
                    ko = 'ops%d' % u
                    for c in range(2):
                        op('pe', lambda e: e.matmul(ups[:, c * 256:(c + 1) * 256], lhsT=kh[u][:, c * 128:(c + 1) * 128],
                                                    rhs=gvs[j3][:, tt, c * 256:(c + 1) * 256], start=True, stop=True),
                           reads=['kh%d' % u, 'gvs%d' % j3], writes=['ups'])
                    for hd in range(4):
                        c, p = hd // 2, hd % 2
                        op('pe', lambda e: e.matmul(ops_[:, hd * 128:(hd + 1) * 128], lhsT=gvs[j3][:, tt, hd * 128:(hd + 1) * 128],
                                                    rhs=at[u][:, hd * 128:(hd + 1) * 128], start=True, stop=False),
                           reads=['gvs%d' % j3, 'at%d' % u], writes=[ko])
                        op('pe', lambda e: e.matmul(ops_[:, hd * 128:(hd + 1) * 128], lhsT=Sbf[:, c, :], rhs=qz[j3][p][:, c, r],
                                                    start=False, stop=True), reads=['Sbf', 'qz%d' % j3], writes=[ko])
                    for hd in range(4):
                        c, p = hd // 2, hd % 2
                        pr = slice(p * 64, (p + 1) * 64)
                        op('dve', lambda e: e.scalar_tensor_tensor(out=S[pr, c, :], in0=S[pr, c, :], scalar=dec[j3][pr, c, tt:tt + 1],
                                                                   in1=ups[pr, c * 256 + p * 128:c * 256 + (p + 1) * 128],
                                                                   op0=ALU.mult, op1=ALU.add), reads=['S', 'dec%d' % j3, 'ups'], writes=['S'])
                    op('dve', lambda e: e.tensor_copy(out=Sbf, in_=S), reads=['S'], writes=['Sbf'])
                    op('act', lambda e: e.activation(out=sqb[u], in_=ops_[:, :], func=AF.Square), reads=[ko], writes=['sqb%d' % u])

                def TC(v):
                    J, tt = flat[v]
                    ntl, ntok, tk, j3, j2 = geo(J)
                    u = v % 2
                    r = slice(tt * 128, (tt + 1) * 128)
                    ops_ = opsb[u]
                    ko = 'ops%d' % u
                    op('pe', lambda e: e.matmul(xm[:, :], lhsT=onesN, rhs=sqb[u], start=True, stop=True), reads=['sqb%d' % u, 'ones'], writes=['xm'])
                    op('act', lambda e: e.activation(out=rs[u], in_=xm[:, :], func=AF.Ln, bias=EPS), reads=['xm'], writes=['rs%d' % u])
                    op('act', lambda e: e.activation(out=rs[u], in_=rs[u], func=AF.Exp, scale=-0.5), reads=['rs%d' % u], writes=['rs%d' % u])
                    op('dve', lambda e: e.tensor_tensor(out=t1g[u], in0=ops_[:, :], in1=rs[u], op=ALU.mult), reads=[ko, 'rs%d' % u], writes=['t1g%d' % u])
                    for hd in range(4):
                        op('dve', lambda e: e.scalar_tensor_tensor(out=yg[j2][:, hd, r], in0=t1g[u][:, hd * 128:(hd + 1) * 128],
                                                                   scalar=vec[:, V_GNG + l * 4 + hd:V_GNG + l * 4 + hd + 1],
                                                                   in1=sg_[j3][:, hd, r], op0=ALU.mult, op1=ALU.mult),
                           reads=['t1g%d' % u, 'vec', 'sg%d' % j3], writes=['yg%d' % j2])

                def TD(v):
                    J, tt = flat[v]
                    ntl, ntok, tk, j3, j2 = geo(J)
                    if tt == ntl - 1:
                        dma('sp', yT[512:1024, tk].rearrange("(c p) t -> p c t", p=128), yg[j2][:, :, 0:ntok], reads=['yg%d' % j2], writes=[('ygs', J)])
                        gla_done[0] = STS[J][-1] + 1

                nJ = len(STS)
                for t in range(-7, NT + 3):
                    if 0 <= t - 1 < NT:
                        TB(t - 1)
                        yield
                    if 0 <= t - 3 < NT:
                        TD(t - 3)
                    if 0 <= t < NT:
                        TA(t)
                        yield
                    if 0 <= t - 2 < NT:
                        TC(t - 2)
                        yield
                    for J in range(nJ):
                        k = t - (v0[J] - 7)
                        if k == 0:
                            PA(J)
                        elif k == 1:
                            PB(J)
                        elif k == 2:
                            PC(J)
                        elif k == 3:
                            PD(J)
                    yield

            def p5_units():
                Wo = A.alloc(16 * D, BF16).rearrange("p (c n) -> p c n", c=16)
                wst = A.alloc(D, F32)
                pg = A.alloc(D, F32)
                yt = [A.alloc(16 * 128, BF16).rearrange("p (c n) -> p c n", c=16) for _ in range(2)]
                hb = [A.alloc(D, F32) for _ in range(2)]
                zs = [A.alloc(D, F32) for _ in range(2)]
                junk = A.alloc(D, BF16)
                ssq = [A.alloc(1, F32) for _ in range(2)]
                for c in range(16):
                    dma('sp', wst, w_out[l, c * 128:(c + 1) * 128, :], writes=['wst5'])
                    op('dve', lambda e: e.tensor_copy(out=Wo[:, c, :], in_=wst), reads=['wst5'], writes=['Wo'])
                    if c % 4 == 3:
                        yield
                dma('sp', pg, pg_in[l], writes=['pg'])
                yield
                tJ = {}
                for J, tiles in enumerate(STS):
                    for t in tiles:
                        tJ[t] = J
                for t in range(NT):
                    while gla_done[0] <= t:
                        yield
                    u = t % 2
                    dma('sp', yt[u], yT[:, t * 128:(t + 1) * 128].rearrange("(c p) t -> p c t", p=128), reads=[('ygs', tJ[t])], writes=['yt%d' % u])
                    dma('sp', hb[u], hsrc[t * 128:(t + 1) * 128, :], writes=['hb%d' % u])
                    for nh in range(2):
                        zp = psF[nh]
                        for c0 in range(0, 16, 4):
                            for c in range(c0, c0 + 4):
                                op('pe', lambda e: e.matmul(zp[:, :], lhsT=yt[u][:, c, :], rhs=Wo[:, c, nh * 512:(nh + 1) * 512],
                                                            start=(c == 0), stop=(c == 15)), reads=['yt%d' % u, 'Wo'], writes=['psF%d' % nh])
                            yield
                        op('act', lambda e: e.activation(out=zs[u][:, nh * 512:(nh + 1) * 512], in_=zp[:, :], func=AF.Copy),
                           reads=['psF%d' % nh], writes=['zs%d' % u])
                    yield
                    op('act', lambda e: e.activation(out=junk, in_=zs[u], func=AF.Square, accum_out=ssq[u]),
                       reads=['zs%d' % u], writes=['junk5', 'ssq%d' % u])
                    op('act', lambda e: e.activation(out=ssq[u], in_=ssq[u], func=AF.Ln, scale=1.0 / D, bias=EPS),
                       reads=['ssq%d' % u], writes=['ssq%d' % u])
                    op('act', lambda e: e.activation(out=ssq[u], in_=ssq[u], func=AF.Exp, scale=-0.5),
                       reads=['ssq%d' % u], writes=['ssq%d' % u])
                    yield
                    op('dve', lambda e: e.scalar_tensor_tensor(out=zs[u], in0=zs[u], scalar=ssq[u], in1=pg, op0=ALU.mult, op1=ALU.mult),
                       reads=['zs%d' % u, 'ssq%d' % u, 'pg'], writes=['zs%d' % u])
                    op('dve', lambda e: e.tensor_tensor(out=zs[u], in0=zs[u], in1=hb[u], op=ALU.add),
                       reads=['zs%d' % u, 'hb%d' % u], writes=['zs%d' % u])
                    yield
                    if l == NL - 1 or l == layers[-1]:
                        if t >= 1:
                            dma('act', out[(t - 1) * 128:t * 128, :], zs[u], reads=['zs%d' % u], writes=['out'])
                    else:
                        dma('act', h1[t * 128:(t + 1) * 128, :], zs[u], reads=['zs%d' % u], writes=['scr'])
                    yield

            if 'P3' in phases:
                A.reset()
                gla_done = [0]
                g3, g5 = p3_units(), p5_units()
                a5 = True
                for _ in g3:
                    for _k in range(3):
                        if a5:
                            try:
                                next(g5)
                            except StopIteration:
                                a5 = False
                if a5:
                    for _ in g5:
                        pass
                T.barrier()

        T.barrier()
        with nc.Block() as block:
            @block.sync
            def _(e):
                T.replay('sp', e)

            @block.tensor
            def _(e):
                T.replay('pe', e)

            @block.scalar
            def _(e):
                T.replay('act', e)

            @block.vector
            def _(e):
                T.replay('dve', e)

            @block.gpsimd
            def _(e):
                T.replay('pool', e)
    return nc


def make_consts():
    ident = np.eye(128, dtype=np.float32)
    tri = (np.arange(128)[None, :] >= np.arange(128)[:, None]).astype(np.float32)
    vm = np.broadcast_to((np.arange(128) >= 112).astype(np.float32)[None, :], (128, 128))
    return np.ascontiguousarray(np.concatenate([ident, tri, vm], axis=1))


def prep_shared(pre_g, w_in, b_f, w_a2, b_a, gla_norm_g, conv_w, conv_b, w_r, b_r, w_i, b_i, lru_lambda, w_out, post_g):
    f = lambda a: np.ascontiguousarray(np.asarray(a, dtype=np.float32))
    pp = lambda a, n: f(a).reshape(NL, n, 128).transpose(2, 0, 1).reshape(128, NL * n)
    vec = np.zeros((128, NV), np.float32)
    vec[:, V_PREG:V_PREG + 16] = pp(pre_g, 8)
    vec[:, V_NBA:V_NBA + 4] = pp(b_a, 2)
    vec[:, V_GNG:V_GNG + 8] = pp(gla_norm_g, 4)
    cw = f(conv_w).reshape(NL, 4, 8, 128).transpose(3, 0, 2, 1).reshape(128, NL * 8 * 4)
    vec[:, V_CW:V_CW + 64] = cw
    vec[:, V_CB:V_CB + 16] = pp(conv_b, 8)
    vec[:, V_BR:V_BR + 16] = pp(b_r, 8)
    vec[:, V_BI:V_BI + 16] = pp(b_i, 8)
    vec[:, V_LAM:V_LAM + 16] = pp(lru_lambda, 8)
    shared = {
        "w_in": f(w_in), "w_out": f(w_out), "vec128": vec,
        "bf_in": f(np.asarray(b_f).T),
        "wa2_in": f(np.asarray(w_a2).transpose(1, 0, 2).reshape(16, NL * 256)),
        "wr_in": f(np.asarray(w_r).transpose(2, 0, 1, 3).reshape(128, NL * 8 * 128)),
        "wi_in": f(np.asarray(w_i).transpose(2, 0, 1, 3).reshape(128, NL * 8 * 128)),
        "pg_in": f(np.broadcast_to(np.asarray(post_g)[:, None, :], (NL, 128, D))),
        "cst_in": make_consts(),
    }
    return shared


def kernel(x, meta, pre_g, w_in, b_f, w_a2, b_a, gla_norm_g, conv_w, conv_b,
           w_r, b_r, w_i, b_i, lru_lambda, w_out, post_g):
    x = np.asarray(x, dtype=np.float32)
    B, S, _ = x.shape
    NT = S // 128 + 1
    shared = prep_shared(pre_g, w_in, b_f, w_a2, b_a, gla_norm_g, conv_w, conv_b, w_r, b_r, w_i, b_i, lru_lambda, w_out, post_g)
    head = np.concatenate([np.zeros((112, D), np.float32), np.asarray(meta, dtype=np.float32)], axis=0)
    real = [0, 1, 4, 5][:B]
    zero_map = {k: np.zeros_like(v) for k, v in shared.items()}
    zero_map["h0"] = np.zeros((NT * 128, D), np.float32)
    in_maps = []
    for c in range(8):
        if c in real:
            m = dict(shared)
            m["h0"] = np.ascontiguousarray(np.concatenate([head, x[real.index(c)]], axis=0))
        else:
            m = zero_map
        in_maps.append(m)
    nc = build_nc(NT)
    res = run_bass_kernel_spmd(nc, in_maps, core_ids=list(range(8)))
    return np.stack([res.results[real[b]]["out"] for b in range(B)], axis=0).astype(np.float32)
```
